# Optimizing a Trainium2 kernel written in Bass

```python
import jax, jax.numpy as jnp
from jax import lax
import numpy as np

D_MODEL = 1024
BATCH = 8
SEQ = 2048
DEPTH = 1
DEC_BATCH = 8
DEC_SEQ = 32
PAST_LEN = 4096

CHUNK = 64
D_CONV = D_MODEL
D_RNN = D_MODEL
CONV_K = 31
RCONV_K = 4
N_RNN_BLOCKS = 8
RNN_BLOCK = D_RNN // N_RNN_BLOCKS
RG_C = 8.0
D_FF = ((8 * D_MODEL // 3 + 127) // 128) * 128
EPS = 1e-6
D_IN = 2 * D_CONV + 2 * D_RNN + 2 * D_MODEL

kernel_name = "macaron_conv_rglru_streaming_step"


def _rmsnorm(x, g):
    xf = x.astype(jnp.float32)
    y = xf * lax.rsqrt(jnp.mean(xf * xf, axis=-1, keepdims=True) + EPS)
    return (y * g.astype(jnp.float32)).astype(x.dtype)


def _layernorm(x, g, b):
    xf = x.astype(jnp.float32)
    mu = jnp.mean(xf, axis=-1, keepdims=True)
    xc = xf - mu
    y = xc * lax.rsqrt(jnp.mean(xc * xc, axis=-1, keepdims=True) + EPS)
    return (y * g.astype(jnp.float32) + b.astype(jnp.float32)).astype(x.dtype)


def _swiglu_half_step(x, g_pre, g_post, w_in, w_out):
    h = _rmsnorm(x, g_pre)
    gate, up = jnp.split(h @ w_in, 2, axis=-1)
    return x + 0.5 * _rmsnorm((jax.nn.silu(gate) * up) @ w_out, g_post)


def _causal_dwconv(hist, u, w, b):
    K, C = w.shape
    full = jnp.concatenate([hist.astype(u.dtype), u], axis=1)
    y = lax.conv_general_dilated(full, w[:, None, :].astype(u.dtype), (1,), 'VALID',
                                 dimension_numbers=('NWC', 'WIO', 'NWC'),
                                 feature_group_count=C)
    return y + b.astype(u.dtype), full[:, -(K - 1):]


def _rglru(x, h0, w_a, b_a, w_x, b_x, lam):
    B, S, C = x.shape
    xb = x.reshape(B, S, N_RNN_BLOCKS, RNN_BLOCK)
    r_gate = jax.nn.sigmoid((jnp.einsum('bsnh,nhk->bsnk', xb, w_a).reshape(B, S, C) + b_a).astype(jnp.float32))
    i_gate = jax.nn.sigmoid((jnp.einsum('bsnh,nhk->bsnk', xb, w_x).reshape(B, S, C) + b_x).astype(jnp.float32))
    log_a = -RG_C * r_gate * jax.nn.softplus(-lam.astype(jnp.float32))
    a = jnp.exp(log_a)
    u = jnp.sqrt(-jnp.expm1(2.0 * log_a)) * (i_gate * x.astype(jnp.float32))
    u = u.at[:, 0].add(a[:, 0] * h0.astype(jnp.float32))

    def combine(left, right):
        a1, b1 = left
        a2, b2 = right
        return a1 * a2, a2 * b1 + b2

    _, h = lax.associative_scan(combine, (a, u), axis=1)
    return h, h[:, -1]


def setup_inputs(seed: int = 0) -> dict:
    key = jax.random.key(seed)
    ks = iter(jax.random.split(key, 64))
    f32 = jnp.float32

    def nrm(shape, scale):
        return jax.random.normal(next(ks), shape, f32) * scale

    def gain(shape):
        return 1.0 + 0.05 * jax.random.normal(next(ks), shape, f32)

    L = DEPTH
    u_a = jax.random.uniform(next(ks), (L, D_RNN), f32, minval=0.9, maxval=0.999)
    a_base = u_a ** (1.0 / RG_C)
    lam = jnp.log(a_base / (1.0 - a_base))
    return {
        "x_prompt": nrm((BATCH, SEQ, D_MODEL), 1.0),
        "x_sample": nrm((DEC_BATCH, DEC_SEQ, D_MODEL), 1.0),
        "state_conv": nrm((L, DEC_BATCH, CONV_K - 1, D_CONV), 0.5),
        "state_rconv": nrm((L, DEC_BATCH, RCONV_K - 1, D_RNN), 1.0),
        "state_h": nrm((L, DEC_BATCH, D_RNN), 0.5),
        "g_ffn1_pre": gain((L, D_MODEL)),
        "g_ffn1_post": gain((L, D_MODEL)),
        "w_ffn1_in": nrm((L, D_MODEL, 2 * D_FF), D_MODEL ** -0.5),
        "w_ffn1_out": nrm((L, D_FF, D_MODEL), D_FF ** -0.5),
        "g_mix_pre": gain((L, D_MODEL)),
        "g_mix_post": gain((L, D_MODEL)),
        "w_in": nrm((L, D_MODEL, D_IN), D_MODEL ** -0.5),
        "w_dw": nrm((L, CONV_K, D_CONV), CONV_K ** -0.5),
        "b_dw": nrm((L, D_CONV), 0.02),
        "ln_g": gain((L, D_CONV)),
        "ln_b": nrm((L, D_CONV), 0.02),
        "w_conv_out": nrm((L, D_CONV, D_MODEL), D_CONV ** -0.5),
        "w_rconv": nrm((L, RCONV_K, D_RNN), RCONV_K ** -0.5),
        "b_rconv": nrm((L, D_RNN), 0.02),
        "w_rg_a": nrm((L, N_RNN_BLOCKS, RNN_BLOCK, RNN_BLOCK), RNN_BLOCK ** -0.5),
        "b_rg_a": nrm((L, D_RNN), 0.02),
        "w_rg_x": nrm((L, N_RNN_BLOCKS, RNN_BLOCK, RNN_BLOCK), RNN_BLOCK ** -0.5),
        "b_rg_x": nrm((L, D_RNN), 0.02),
        "lam": lam,
        "w_rnn_out": nrm((L, D_RNN, D_MODEL), D_RNN ** -0.5),
        "w_out": nrm((L, D_MODEL, D_MODEL), D_MODEL ** -0.5),
        "g_ffn2_pre": gain((L, D_MODEL)),
        "g_ffn2_post": gain((L, D_MODEL)),
        "w_ffn2_in": nrm((L, D_MODEL, 2 * D_FF), D_MODEL ** -0.5),
        "w_ffn2_out": nrm((L, D_FF, D_MODEL), D_FF ** -0.5),
        "g_final": gain((L, D_MODEL)),
    }


def reference(x_prompt, x_sample, state_conv, state_rconv, state_h,
              g_ffn1_pre, g_ffn1_post, w_ffn1_in, w_ffn1_out,
              g_mix_pre, g_mix_post, w_in,
              w_dw, b_dw, ln_g, ln_b, w_conv_out,
              w_rconv, b_rconv, w_rg_a, b_rg_a, w_rg_x, b_rg_x, lam, w_rnn_out,
              w_out,
              g_ffn2_pre, g_ffn2_post, w_ffn2_in, w_ffn2_out, g_final):
    split_at = [D_CONV, 2 * D_CONV, 2 * D_CONV + D_RNN, 2 * D_CONV + 2 * D_RNN,
                2 * D_CONV + 2 * D_RNN + D_MODEL]

    def layer(x, conv_hist, rconv_hist, h0, l):
        x = _swiglu_half_step(x, g_ffn1_pre[l], g_ffn1_post[l], w_ffn1_in[l], w_ffn1_out[l])
        h = _rmsnorm(x, g_mix_pre[l])
        c_val, c_gate, r_x, r_gate, m_conv, m_rnn = jnp.split(h @ w_in[l], split_at, axis=-1)
        u = c_val * jax.nn.sigmoid(c_gate)
        cv, new_conv = _causal_dwconv(conv_hist, u, w_dw[l], b_dw[l])
        cv = jax.nn.silu(_layernorm(cv, ln_g[l], ln_b[l]))
        conv_out = cv @ w_conv_out[l]
        rx, new_rconv = _causal_dwconv(rconv_hist, r_x, w_rconv[l], b_rconv[l])
        hs, h_last = _rglru(rx, h0, w_rg_a[l], b_rg_a[l], w_rg_x[l], b_rg_x[l], lam[l])
        rnn_out = (hs.astype(x.dtype) * jax.nn.gelu(r_gate)) @ w_rnn_out[l]
        merged = jax.nn.sigmoid(m_conv) * conv_out + jax.nn.sigmoid(m_rnn) * rnn_out
        x = x + _rmsnorm(merged @ w_out[l], g_mix_post[l])
        x = _swiglu_half_step(x, g_ffn2_pre[l], g_ffn2_post[l], w_ffn2_in[l], w_ffn2_out[l])
        x = _rmsnorm(x, g_final[l])
        return x, new_conv, new_rconv, h_last.astype(x.dtype)

    bp = x_prompt.shape[0]
    yp = x_prompt
    ys = x_sample
    pc, pr, ph, sc, sr, sh = [], [], [], [], [], []
    for l in range(DEPTH):
        zc = jnp.zeros((bp, CONV_K - 1, D_CONV), x_prompt.dtype)
        zr = jnp.zeros((bp, RCONV_K - 1, D_RNN), x_prompt.dtype)
        zh = jnp.zeros((bp, D_RNN), x_prompt.dtype)
        yp, c1, r1, h1 = layer(yp, zc, zr, zh, l)
        ys, c2, r2, h2 = layer(ys, state_conv[l], state_rconv[l], state_h[l], l)
        pc.append(c1); pr.append(r1); ph.append(h1)
        sc.append(c2.astype(state_conv.dtype)); sr.append(r2.astype(state_rconv.dtype)); sh.append(h2.astype(state_h.dtype))
    new_conv_prompt = jnp.stack(pc)
    new_rconv_prompt = jnp.stack(pr)
    new_h_prompt = jnp.stack(ph)
    new_conv_sample = jnp.stack(sc)
    new_rconv_sample = jnp.stack(sr)
    new_h_sample = jnp.stack(sh)
    return (yp, ys, new_conv_prompt, new_rconv_prompt, new_h_prompt, new_conv_sample, new_rconv_sample, new_h_sample)
```

```python
import os
from contextlib import ExitStack

import numpy as np
import concourse.bass as bass
import concourse.mybir as mybir
from concourse.bass_utils import run_bass_kernel_spmd

F32 = mybir.dt.float32
BF16 = mybir.dt.bfloat16
AF = mybir.ActivationFunctionType
ALU = mybir.AluOpType

D = 1024
KC = 8
DFF = 2816
FC = 22
S = 2048
SS = 32
CK = 31
RK = 4
EPS = 1e-6
NCORES = 8

R_G1PRE, R_G1POST, R_GMPRE, R_GMPOST, R_BDW, R_LNG, R_LNB, R_BRC, R_BA, R_BX, R_LAM, R_G2PRE, R_G2POST, R_GFIN = range(14)
R_WDW = 14
R_WRC = R_WDW + CK
R_SCONV = R_WRC + RK
R_SRC = R_SCONV + 30
R_SH = R_SRC + 3
NR = R_SH + 1
O_CP, O_RP, O_HP, O_CS, O_RS, O_HS = 0, 30, 33, 34, 64, 67
NSO = 68

GROUPS = [(0, 704, False), (704, 704, False), (1408, 640, True)]
TGM = 704
U_S = 30 + TGM
UW = U_S + 30 + SS
RX_S = 3 + TGM
RXW = RX_S + 3 + SS
NSLOT = 6


class Eng:
    def __init__(self, name, h, sem):
        self.name, self.h, self.sem = name, h, sem
        self.count = 0
        self.seen = {}
        self.strict = name in ("act", "dve", "pool")


class DSem:
    def __init__(self, name, sem):
        self.name, self.sem = name, sem
        self.count = 0


class Buf:
    def __init__(self, name):
        self.name = name
        self.w = {}
        self.r = {}


def _deps(eng, reads, writes):
    deps = {}
    for b in reads:
        for o, c in b.w.items():
            if c > deps.get(o, 0):
                deps[o] = c
    for b in writes:
        for o, c in b.w.items():
            if (o is not eng or eng.strict) and c > deps.get(o, 0):
                deps[o] = c
        for o, c in b.r.items():
            if (o is not eng or eng.strict) and c > deps.get(o, 0):
                deps[o] = c
    return deps


def _wait(eng, deps):
    for o, c in deps.items():
        if c <= eng.seen.get(o, 0):
            continue
        eng.h.wait_ge(o.sem, c)
        eng.seen[o] = c


def op(eng, fn, reads=(), writes=()):
    _wait(eng, _deps(eng, reads, writes))
    inst = fn()
    eng.count += 1
    inst.then_inc(eng.sem, 1)
    for b in reads:
        b.r[eng] = eng.count
    for b in writes:
        b.w[eng] = eng.count


def dma(q, dsem, out_ap, in_ap, reads=(), writes=()):
    deps = {}
    for b in reads:
        for o, c in b.w.items():
            if c > deps.get(o, 0):
                deps[o] = c
    for b in writes:
        for o, c in b.w.items():
            if o is not q and o is not dsem and c > deps.get(o, 0):
                deps[o] = c
        for o, c in b.r.items():
            if o is not q and c > deps.get(o, 0):
                deps[o] = c
    _wait(q, deps)
    q.h.dma_start(out=out_ap, in_=in_ap).then_inc(dsem.sem, 16)
    dsem.count += 16
    for b in reads:
        b.r[dsem] = dsem.count
    for b in writes:
        b.w[dsem] = dsem.count


def build_program(dbg=False):
    nc = bass.Bass("TRN2", target_bir_lowering=False)

    def din(name, shape):
        return nc.dram_tensor(name, shape, F32, kind="ExternalInput").ap()

    def dout(name, shape):
        return nc.dram_tensor(name, shape, F32, kind="ExternalOutput").ap()

    xp_d = din("xp", [S, D])
    xs_d = din("xs", [SS, D])
    rows_d = din("rows", [NR, D])
    ident_d = din("ident", [128, 128])
    w1i_d = din("w1i", [D, 2 * DFF])
    w1o_d = din("w1o", [DFF, D])
    wi_d = din("wi", [D, 6 * D])
    wco_d = din("wco", [D, D])
    wro_d = din("wro", [D, D])
    wo_d = din("wo", [D, D])
    wga_d = din("wga", [8, 128, 128])
    wgx_d = din("wgx", [8, 128, 128])
    w2i_d = din("w2i", [D, 2 * DFF])
    w2o_d = din("w2o", [DFF, D])
    yp_d = dout("yp", [S, D])
    ys_d = dout("ys", [SS, D])
    so_d = dout("so", [NSO, D])
    dbg_d = dout("dbg", [16, 128, TGM]) if dbg else None

    es = ExitStack()
    with es:
        def sb(name, shape, dt):
            return es.enter_context(nc.sbuf_tensor(name, shape, dt))

        def sem(name):
            return es.enter_context(nc.semaphore(name))

        PE = Eng("pe", nc.tensor, sem("s_pe"))
        ACT = Eng("act", nc.scalar, sem("s_act"))
        DVE = Eng("dve", nc.vector, sem("s_dve"))
        POOL = Eng("pool", nc.gpsimd, sem("s_pool"))
        SP = Eng("sp", nc.sync, sem("s_sp"))

        x = sb("x", [128, KC, TGM], F32)
        xg = sb("xg", [128, KC, TGM], BF16)
        mg = sb("mg", [128, KC, TGM], BF16)
        ffo = sb("ffo", [128, KC, TGM], F32)
        act = sb("act", [128, FC, TGM], BF16)
        gh = act[:, 0:8, :]
        cvn = act[:, 8:16, :]
        mg_b, act_b, gh_b, cvn_b = Buf("mg"), Buf("act"), Buf("gh"), Buf("cvn")
        xg_bs = [Buf(f"xg{c}") for c in range(KC)]
        x_bs = [Buf(f"x{c}") for c in range(KC)]
        ffo_bs = [Buf(f"ffo{c}") for c in range(KC)]
        rstd = sb("rstd", [128, TGM], F32)
        mean = sb("mean", [128, TGM], F32)
        rstd_b, mean_b = Buf("rstd"), Buf("mean")
        P = sb("P", [128, KC, NR], F32)
        P_b = Buf("P")
        c1 = sb("c1", [128, KC], F32)
        c1_b = Buf("c1")
        c1h = sb("c1h", [128, KC], F32)
        eps_t = sb("eps_t", [128, 1], F32)
        q_t = sb("q_t", [128, 1], F32)
        bh = sb("bh", [128, 2 * KC], F32)
        SO = sb("SO", [128, KC, NSO], F32)
        SO_b = Buf("SO")
        uh = sb("uh", [128, KC, 30], BF16)
        rh = sb("rh", [128, KC, 3], F32)
        hprev = sb("hprev", [128, KC], F32)
        uh_b, rh_b, hprev_b = Buf("uh"), Buf("rh"), Buf("hprev")
        ident = sb("identf", [128, 128], F32)
        identb = sb("identb", [128, 128], BF16)
        ones = sb("ones", [128, 128], BF16)
        const_b = Buf("const")
        NTF, NTB = 7, 4
        tf = [(sb(f"tf{i}", [128, TGM], F32), Buf(f"tf{i}")) for i in range(NTF)]
        tb = [(sb(f"tb{i}", [128, TGM], BF16), Buf(f"tb{i}")) for i in range(NTB)]
        ut = [(sb(f"ut{i}", [128, UW], BF16), Buf(f"ut{i}")) for i in range(KC)]
        rxi = [(sb(f"rxi{i}", [128, RXW], F32), Buf(f"rxi{i}")) for i in range(1)]
        cbqs = [(sb(f"cbq{i}", [128, 2, TGM], BF16), Buf(f"cbq{i}")) for i in range(2)]
        Dg = [(sb(f"Dg{i}", [128, CK, 128], BF16), Buf(f"Dg{i}")) for i in range(2)]
        stage = [(sb(f"stg{i}", [128, D], F32), Buf(f"stg{i}"), DSem(f"stg{i}", sem(f"d_stg{i}"))) for i in range(2)]
        stage_h2 = (Buf("stg1h"), DSem("stg1h", sem("d_stg1h")))
        ring = [(sb(f"wr{i}", [128, 2048], BF16), Buf(f"wr{i}"), DSem(f"wr{i}", sem(f"d_wr{i}"))) for i in range(NSLOT)]
        banks = [(es.enter_context(nc.psum_tensor(f"bk{i}", [128, 512], F32)), Buf(f"bk{i}")) for i in range(8)]
        misc_ds = DSem("misc", sem("d_misc"))
        dbg_dss = [DSem(f"dbg{i}", sem(f"d_dbg{i}")) for i in range(16)] if dbg else []

        def dump(i, ap, buf, n=TGM):
            if dbg:
                dma(SP, dbg_dss[i], dbg_d[i, :, :n], ap, reads=[buf])

        cnt = {"tf": 0, "tb": 0, "ut": 0, "rxi": 0, "dg": 0, "stg": 0, "bank": 0, "ldt": 0}
        load_pend = []
        pool = list(range(8))

        def rr(key, lst):
            i = cnt[key] % len(lst)
            cnt[key] += 1
            return lst[i]

        def tmpf():
            return rr("tf", tf)

        def tmpb():
            return rr("tb", tb)

        def bank():
            i = pool[cnt["bank"] % len(pool)]
            cnt["bank"] += 1
            return banks[i]

        def take_banks(n):
            out = [banks[pool.pop()] for _ in range(n)]
            return out

        def give_banks(bs):
            for b in bs:
                pool.append([i for i in range(8) if banks[i] is b][0])

        def pe_mm(out_ap, pairs, reads, writes, start=True, stop=True):
            def fn():
                n = len(pairs)
                ins = None
                for i, (l, r) in enumerate(pairs):
                    ins = nc.tensor.matmul(out_ap, lhsT=l, rhs=r, start=(start and i == 0), stop=(stop and i == n - 1))
                return ins
            op(PE, fn, reads, writes)

        def pe_tr(out_ap, in_ap, idn, reads, writes):
            op(PE, lambda: nc.tensor.transpose(out=out_ap, in_=in_ap, identity=idn), reads, writes)

        def a_act(out_ap, in_ap, func, reads, writes, scale=None, bias=None):
            kw = {}
            if scale is not None:
                kw["scale"] = scale
            if bias is not None:
                kw["bias"] = bias
            op(ACT, lambda: nc.scalar.activation(out=out_ap, in_=in_ap, func=func, **kw), reads, writes)

        def v_tt(out_ap, in0, in1, alu, reads, writes):
            op(DVE, lambda: nc.vector.tensor_tensor(out=out_ap, in0=in0, in1=in1, op=alu), reads, writes)

        def v_stt(out_ap, in0, scalar, in1, op0, op1, reads, writes):
            op(DVE, lambda: nc.vector.scalar_tensor_tensor(out=out_ap, in0=in0, scalar=scalar, in1=in1, op0=op0, op1=op1), reads, writes)

        def v_ts(out_ap, in0, s1, s2, op0, op1, reads, writes):
            op(DVE, lambda: nc.vector.tensor_scalar(out=out_ap, in0=in0, scalar1=s1, scalar2=s2, op0=op0, op1=op1), reads, writes)

        def v_copy(out_ap, in_ap, reads, writes):
            op(DVE, lambda: nc.vector.tensor_copy(out=out_ap, in_=in_ap), reads, writes)

        def v_recip(out_ap, in_ap, reads, writes):
            op(DVE, lambda: nc.vector.reciprocal(out=out_ap, in_=in_ap), reads, writes)

        def prm(c, r):
            return P[:, c, r:r + 1]

        def wv(w, kc):
            return w.rearrange("(kc p) n -> p kc n", p=128)

        units = []

        def ffn_units(wi_, wo_):
            wiv = wv(wi_, KC)
            wov = wv(wo_, FC)
            for m in range(FC):
                units.append([(0, KC, wiv[:, :, m * 128:(m + 1) * 128]),
                              (1024, KC, wiv[:, :, DFF + m * 128:DFF + (m + 1) * 128])])
            for o in range(KC):
                for h in range(2):
                    units.append([(0, 11, wov[:, 11 * h:11 * h + 11, o * 128:(o + 1) * 128])])

        def mixer_units():
            wiv = wv(wi_d, KC)
            for c in range(KC):
                units.append([(0, KC, wiv[:, :, c * 128:(c + 1) * 128]),
                              (1024, KC, wiv[:, :, D + c * 128:D + (c + 1) * 128])])
            for c in range(KC):
                units.append([(0, KC, wiv[:, :, 2 * D + c * 128:2 * D + (c + 1) * 128]),
                              (1024, KC, wiv[:, :, 3 * D + c * 128:3 * D + (c + 1) * 128])])
            wcov, wrov, wov = wv(wco_d, KC), wv(wro_d, KC), wv(wo_d, KC)
            for o in range(KC):
                units.append([(0, KC, wiv[:, :, 4 * D + o * 128:4 * D + (o + 1) * 128]),
                              (1024, KC, wiv[:, :, 5 * D + o * 128:5 * D + (o + 1) * 128])])
                units.append([(0, KC, wcov[:, :, o * 128:(o + 1) * 128]),
                              (1024, KC, wrov[:, :, o * 128:(o + 1) * 128])])
            for oo in range(4):
                units.append([(0, KC, wov[:, :, (2 * oo) * 128:(2 * oo + 1) * 128]),
                              (1024, KC, wov[:, :, (2 * oo + 1) * 128:(2 * oo + 2) * 128])])

        for _ in GROUPS:
            ffn_units(w1i_d, w1o_d)
            mixer_units()
            ffn_units(w2i_d, w2o_d)

        wstate = {"issued": 0, "used": 0}

        def issue_unit():
            u = wstate["issued"]
            if u >= len(units):
                return
            t, b, ds = ring[u % NSLOT]
            for (off, a, src) in units[u]:
                dst = t[:, off:off + a * 128].rearrange("p (a b) -> p a b", a=a)
                dma(POOL, ds, dst, src, writes=[b])
            wstate["issued"] += 1

        def next_unit():
            u = wstate["used"]
            while wstate["issued"] < min(len(units), u + NSLOT - 1):
                issue_unit()
            wstate["used"] += 1
            return ring[u % NSLOT]

        gw = sb("gw", [128, 2048], BF16)
        gwb = Buf("gw")
        gw_ds = DSem("gw", sem("d_gw"))
        dma(POOL, gw_ds, gw[:, 0:1024].rearrange("p (a b) -> p a b", a=8), wga_d.rearrange("n h k -> h n k"), writes=[gwb])
        dma(POOL, gw_ds, gw[:, 1024:2048].rearrange("p (a b) -> p a b", a=8), wgx_d.rearrange("n h k -> h n k"), writes=[gwb])
        dma(SP, misc_ds, ident[:], ident_d, writes=[const_b])
        st_t, st_b, st_ds = stage[0]
        dma(SP, st_ds, st_t[:NR, :], rows_d, writes=[st_b])
        op(DVE, lambda: nc.vector.memset(ones[:], 1.0), writes=[const_b])
        v_copy(identb[:], ident[:], [const_b], [const_b])
        op(DVE, lambda: nc.vector.memset(uh[:], 0.0), writes=[uh_b])
        op(DVE, lambda: nc.vector.memset(rh[:], 0.0), writes=[rh_b])
        op(DVE, lambda: nc.vector.memset(hprev[:], 0.0), writes=[hprev_b])
        for half in range(2):
            bt, bb = bank()
            for q in range(4):
                kc = half * 4 + q
                pe_tr(bt[:, q * 128:q * 128 + NR], st_t[:NR, kc * 128:(kc + 1) * 128], ident[:NR, :NR], [st_b, const_b], [bb])
            a_act(P[:, half * 4:half * 4 + 4, :], bt[:, :].rearrange("p (q n) -> p q n", q=4)[:, :, :NR], AF.Copy, [bb], [P_b])
        t1, t1b = tmpf()
        a_act(t1[:, 0:KC], P[:, :, R_LAM], AF.Exp, [P_b], [t1b], scale=-1.0)
        a_act(t1[:, 8:8 + KC], t1[:, 0:KC], AF.Ln, [t1b], [t1b], bias=1.0)
        v_ts(c1[:, :], t1[:, 8:8 + KC], -8.0, None, ALU.mult, ALU.bypass, [t1b], [c1_b])
        v_ts(c1h[:, :], t1[:, 8:8 + KC], -4.0, None, ALU.mult, ALU.bypass, [t1b], [c1_b])
        op(DVE, lambda: nc.vector.memset(eps_t[:], EPS), writes=[c1_b])
        op(DVE, lambda: nc.vector.memset(q_t[:], 0.25), writes=[c1_b])
        v_ts(bh[:, 0:KC], P[:, :, R_BA], 0.5, None, ALU.mult, ALU.bypass, [P_b], [c1_b])
        v_ts(bh[:, KC:2 * KC], P[:, :, R_BX], 0.5, None, ALU.mult, ALU.bypass, [P_b], [c1_b])

        def stats_mm(sbank, nb, src_ap, src_b, first, last):
            st, sbuf_ = sbank
            pe_mm(st[:, :nb], [(ones[:, :], src_ap)], [const_b, src_b], [sbuf_], start=first, stop=last)

        def rstd_from(sbank, b0, nb, f):
            st, sbuf_ = sbank
            t, tb_ = tmpf()
            a_act(t[:, :nb], st[:, :nb], AF.Ln, [sbuf_, c1_b], [tb_], scale=1.0 / D, bias=eps_t[:, 0:1])
            a_act(rstd[:, b0:b0 + nb], t[:, :nb], AF.Exp, [tb_], [rstd_b], scale=-0.5, bias=float(np.log(f)))

        def prenorm(TG, blocks, grow, out_t, out_b):
            sbk = take_banks(len(blocks))
            for c in range(KC):
                q, qb = tmpb()
                a_act(q[:, :TG], x[:, c, :TG], AF.Square, [x_bs[c]], [qb])
                for bi, (b0, nb) in enumerate(blocks):
                    stats_mm(sbk[bi], nb, q[:, b0:b0 + nb], qb, c == 0, c == KC - 1)
            for bi, (b0, nb) in enumerate(blocks):
                rstd_from(sbk[bi], b0, nb, 1.0)
            give_banks(sbk)
            for c in range(KC):
                v_stt(out_t[:, c, :TG], x[:, c, :TG], prm(c, grow), rstd[:, :TG], ALU.mult, ALU.mult, [x_bs[c], P_b, rstd_b], [out_b[c] if isinstance(out_b, list) else out_b])

        def postnorm(TG, blocks, sbk, grow, f):
            for bi, (b0, nb) in enumerate(blocks):
                rstd_from(sbk[bi], b0, nb, f)
            for c in range(KC):
                v_tt(ffo[:, c, :TG], ffo[:, c, :TG], rstd[:, :TG], ALU.mult, [ffo_bs[c], rstd_b], [ffo_bs[c]])
                v_stt(x[:, c, :TG], ffo[:, c, :TG], prm(c, grow), x[:, c, :TG], ALU.mult, ALU.add, [ffo_bs[c], P_b, x_bs[c]], [x_bs[c]])

        def first_unit(wt, wb, blocks):
            res = [(bank(), bank()) for _ in blocks]
            for kc in range(KC):
                for (b0, nb), ((a_t, a_b), (b_t, b_b)) in zip(blocks, res):
                    pe_mm(a_t[:, :nb], [(wt[:, kc * 128:(kc + 1) * 128], xg[:, kc, b0:b0 + nb])], [wb, xg_bs[kc]], [a_b],
                          start=(kc == 0), stop=(kc == KC - 1))
                    pe_mm(b_t[:, :nb], [(wt[:, 1024 + kc * 128:1024 + (kc + 1) * 128], xg[:, kc, b0:b0 + nb])], [wb, xg_bs[kc]], [b_b],
                          start=(kc == 0), stop=(kc == KC - 1))
            return res

        def ffn(TG, blocks, grow_pre, grow_post, do_prenorm=True):
            if do_prenorm:
                prenorm(TG, blocks, grow_pre, xg, xg_bs)
            for m in range(FC):
                wt, wb, _ = next_unit()
                pre = first_unit(wt, wb, blocks) if m == 0 else None
                for bi, (b0, nb) in enumerate(blocks):
                    if pre is not None:
                        (gt, gb), (upt, upb) = pre[bi]
                    else:
                        gt, gb = bank()
                        upt, upb = bank()
                        pe_mm(gt[:, :nb], [(wt[:, kc * 128:(kc + 1) * 128], xg[:, kc, b0:b0 + nb]) for kc in range(KC)], [wb] + xg_bs, [gb])
                        pe_mm(upt[:, :nb], [(wt[:, 1024 + kc * 128:1024 + (kc + 1) * 128], xg[:, kc, b0:b0 + nb]) for kc in range(KC)], [wb] + xg_bs, [upb])
                    t, tb_ = tmpf()
                    a_act(t[:, :nb], gt[:, :nb], AF.Silu, [gb], [tb_])
                    v_tt(act[:, m, b0:b0 + nb], upt[:, :nb], t[:, :nb], ALU.mult, [upb, tb_], [act_b, gh_b, cvn_b])
            sbk = take_banks(len(blocks))
            pending = []
            for o in range(KC):
                w0 = next_unit()
                w1 = next_unit()
                for bi, (b0, nb) in enumerate(blocks):
                    bt, bb = bank()
                    pairs = []
                    for kc in range(FC):
                        wt = (w0 if kc < 11 else w1)[0]
                        kk = kc % 11
                        pairs.append((wt[:, kk * 128:(kk + 1) * 128], act[:, kc, b0:b0 + nb]))
                    pe_mm(bt[:, :nb], pairs, [w0[1], w1[1], act_b, gh_b, cvn_b], [bb])
                    for fn_ in pending:
                        fn_()
                    pending = []
                    a_act(ffo[:, o, b0:b0 + nb], bt[:, :nb], AF.Copy, [bb], [ffo_bs[o]])
                    q, qb = tmpb()
                    a_act(q[:, :nb], bt[:, :nb], AF.Square, [bb], [qb])
                    pending.append(lambda bi=bi, nb=nb, q=q, qb=qb, o=o: stats_mm(sbk[bi], nb, q[:, :nb], qb, o == 0, o == KC - 1))
            for fn_ in pending:
                fn_()
            postnorm(TG, blocks, sbk, grow_post, 0.5)
            give_banks(sbk)

        def mixer(TG, TGp, has_s, blocks, last):
            prenorm(TG, blocks, R_GMPRE, xg, xg_bs)

            def split(b0, nb):
                npr = min(nb, TGp - b0)
                return npr, nb - npr

            for c in range(KC):
                wt, wb, _ = next_unit()
                u, ub = ut[c]
                v_copy(u[:, 0:30], uh[:, c, :], [uh_b], [ub])
                if has_s:
                    v_copy(u[:, U_S:U_S + 30], P[:, c, R_SCONV:R_SCONV + 30], [P_b], [ub])
                pre = first_unit(wt, wb, blocks) if c == 0 else None
                for bi, (b0, nb) in enumerate(blocks):
                    npr, nsm = split(b0, nb)
                    if pre is not None:
                        (vt, vb), (gt, gb) = pre[bi]
                    else:
                        vt, vb = bank()
                        gt, gb = bank()
                        pe_mm(vt[:, :nb], [(wt[:, kc * 128:(kc + 1) * 128], xg[:, kc, b0:b0 + nb]) for kc in range(KC)], [wb] + xg_bs, [vb])
                        pe_mm(gt[:, :nb], [(wt[:, 1024 + kc * 128:1024 + (kc + 1) * 128], xg[:, kc, b0:b0 + nb]) for kc in range(KC)], [wb] + xg_bs, [gb])
                    t, tb_ = tmpf()
                    a_act(t[:, :nb], gt[:, :nb], AF.Sigmoid, [gb], [tb_])
                    v_tt(u[:, 30 + b0:30 + b0 + npr], vt[:, :npr], t[:, :npr], ALU.mult, [vb, tb_], [ub])
                    if nsm:
                        v_tt(u[:, U_S + 30:U_S + 30 + SS], vt[:, npr:nb], t[:, npr:nb], ALU.mult, [vb, tb_], [ub])
                    if last and bi == len(blocks) - 1:
                        v_tt(SO[:, c, O_CP:O_CP + 30], vt[:, npr - 30:npr], t[:, npr - 30:npr], ALU.mult, [vb, tb_], [SO_b])
                        v_tt(SO[:, c, O_CS:O_CS + 30], vt[:, npr + 2:npr + 32], t[:, npr + 2:npr + 32], ALU.mult, [vb, tb_], [SO_b])
                v_copy(uh[:, c, :], u[:, TGp:TGp + 30], [ub], [uh_b])

            def build_dg(c):
                dg, dgb = Dg[c % 2]
                v_tt(dg[:, :, :], identb[:, :].unsqueeze(1).broadcast_to([128, CK, 128]),
                     P[:, c, R_WDW:R_WDW + CK].unsqueeze(2).broadcast_to([128, CK, 128]), ALU.mult, [const_b, P_b], [dgb])

            build_dg(0)

            spieces = [(q0, min(256, TG - q0)) for q0 in range(0, TG, 256)]
            sPQ = take_banks(len(spieces))
            def ln_stats(c):
                cbq, cbq_b = cbqs[c % 2]
                for pi, (q0, w) in enumerate(spieces):
                    st, stb = sPQ[pi]
                    pe_mm(st[:, 0:2 * w].rearrange("p (a b) -> p a b", a=2), [(ones[:, :], cbq[:, :, q0:q0 + w])], [const_b, cbq_b], [stb],
                          start=(c == 0), stop=(c == KC - 1))

            st8 = [dict() for _ in range(KC)]

            def H1(c):
                d = st8[c]
                wt, wb, _ = next_unit()
                ri, rib = rr("rxi", rxi)
                d["ri"] = (ri, rib)
                v_copy(ri[:, 0:3], rh[:, c, :], [rh_b], [rib])
                if has_s:
                    v_copy(ri[:, RX_S:RX_S + 3], P[:, c, R_SRC:R_SRC + 3], [P_b], [rib])
                rg, rgb = tmpb()
                d["rg"] = (rg, rgb)
                for bi, (b0, nb) in enumerate(blocks):
                    npr, nsm = split(b0, nb)
                    xt_, xb_ = bank()
                    gt, gb = bank()
                    pe_mm(xt_[:, :nb], [(wt[:, kc * 128:(kc + 1) * 128], xg[:, kc, b0:b0 + nb]) for kc in range(KC)], [wb] + xg_bs, [xb_])
                    pe_mm(gt[:, :nb], [(wt[:, 1024 + kc * 128:1024 + (kc + 1) * 128], xg[:, kc, b0:b0 + nb]) for kc in range(KC)], [wb] + xg_bs, [gb])
                    v_copy(ri[:, 3 + b0:3 + b0 + npr], xt_[:, :npr], [xb_], [rib])
                    if nsm:
                        v_copy(ri[:, RX_S + 3:RX_S + 3 + SS], xt_[:, npr:nb], [xb_], [rib])
                    a_act(rg[:, b0:b0 + nb], gt[:, :nb], AF.Gelu_apprx_tanh, [gb], [rgb])

            def H2(c):
                d = st8[c]
                ri, rib = d["ri"]
                v_copy(rh[:, c, :], ri[:, TGp:TGp + 3], [rib], [rh_b])
                if last:
                    v_copy(SO[:, c, O_RP:O_RP + 3], ri[:, TGp:TGp + 3], [rib], [SO_b])
                    v_copy(SO[:, c, O_RS:O_RS + 3], ri[:, RX_S + SS:RX_S + SS + 3], [rib], [SO_b])
                rx, rxb = tmpf()
                d["rx"] = (rx, rxb)
                segs = [(0, 0, TGp)] + ([(RX_S, TGp, SS)] if has_s else [])
                for (src0, dst0, n) in segs:
                    v_ts(rx[:, dst0:dst0 + n], ri[:, src0:src0 + n], prm(c, R_WRC), prm(c, R_BRC), ALU.mult, ALU.add, [rib, P_b], [rxb])
                    for k in range(1, RK):
                        v_stt(rx[:, dst0:dst0 + n], ri[:, src0 + k:src0 + k + n], prm(c, R_WRC + k), rx[:, dst0:dst0 + n], ALU.mult, ALU.add, [rib, P_b, rxb], [rxb])
                rxq, rxqb = tmpb()
                d["rxq"] = (rxq, rxqb)
                a_act(rxq[:, :TG], rx[:, :TG], AF.Copy, [rxb], [rxqb])

            def H3(c):
                u, ub = ut[c]
                dg, dgb = Dg[c % 2]
                for bi, (b0, nb) in enumerate(blocks):
                    npr, nsm = split(b0, nb)
                    bt, bb = bank()
                    pe_mm(bt[:, :npr], [(dg[:, k, :], u[:, b0 + k:b0 + k + npr]) for k in range(CK)], [dgb, ub], [bb])
                    if nsm:
                        pe_mm(bt[:, npr:nb], [(dg[:, k, :], u[:, U_S + k:U_S + k + SS]) for k in range(CK)], [dgb, ub], [bb])
                    a_act(ffo[:, c, b0:b0 + nb], bt[:, :nb], AF.Identity, [bb, P_b], [ffo_bs[c]], bias=prm(c, R_BDW))
                if c + 2 < KC:
                    build_dg(c + 2)

            def H4(c):
                d = st8[c]
                rxq, rxqb = d["rxq"]
                rt, rtb = tmpf()
                it, itb = tmpf()
                d["rt"], d["it"] = (rt, rtb), (it, itb)
                for bi, (b0, nb) in enumerate(blocks):
                    rp, rpb = bank()
                    ip, ipb = bank()
                    pe_mm(rp[:, :nb], [(gw[:, c * 128:(c + 1) * 128], rxq[:, b0:b0 + nb])], [gwb, rxqb], [rpb])
                    pe_mm(ip[:, :nb], [(gw[:, 1024 + c * 128:1024 + (c + 1) * 128], rxq[:, b0:b0 + nb])], [gwb, rxqb], [ipb])
                    a_act(rt[:, b0:b0 + nb], rp[:, :nb], AF.Tanh, [rpb, c1_b], [rtb], scale=0.5, bias=bh[:, c:c + 1])
                    a_act(it[:, b0:b0 + nb], ip[:, :nb], AF.Tanh, [ipb, c1_b], [itb], scale=0.5, bias=bh[:, KC + c:KC + c + 1])
                if c >= 1:
                    ln_stats(c - 1)
                cbq, cbq_b = cbqs[c % 2]
                a_act(cbq[:, 0, :TG], ffo[:, c, :TG], AF.Copy, [ffo_bs[c]], [cbq_b])
                a_act(cbq[:, 1, :TG], ffo[:, c, :TG], AF.Square, [ffo_bs[c]], [cbq_b])

            def T1(c):
                d = st8[c]
                rt, rtb = d["rt"]
                at, atb = tmpf()
                d["at"] = (at, atb)
                a_act(at[:, :TG], rt[:, :TG], AF.Exp, [rtb, c1_b], [atb], scale=c1h[:, c:c + 1], bias=c1h[:, c:c + 1])
                a_act(rt[:, :TG], at[:, :TG], AF.Square, [atb], [rtb])
                a_act(rt[:, :TG], rt[:, :TG], AF.Sqrt, [rtb, c1_b], [rtb], scale=-0.25, bias=q_t[:, 0:1])

            def T2(c):
                d = st8[c]
                rt, rtb = d["rt"]
                it, itb = d["it"]
                at, atb = d["at"]
                rx, rxb = d["rx"]
                rg, rgb = d["rg"]
                ri, rib = d["ri"]
                v_stt(it[:, :TG], it[:, :TG], 1.0, rx[:, :TG], ALU.add, ALU.mult, [itb, rxb], [itb])
                v_tt(it[:, :TG], it[:, :TG], rt[:, :TG], ALU.mult, [itb, rtb], [itb])
                hs, hsb = tmpf()
                op(DVE, lambda: nc.vector.tensor_tensor_scan(out=hs[:, :TGp], data0=at[:, :TGp], data1=it[:, :TGp],
                                                             initial=hprev[:, c:c + 1], op0=ALU.mult, op1=ALU.add),
                   [atb, itb, hprev_b], [hsb])
                if has_s:
                    op(DVE, lambda: nc.vector.tensor_tensor_scan(out=hs[:, TGp:TG], data0=at[:, TGp:TG], data1=it[:, TGp:TG],
                                                                 initial=prm(c, R_SH), op0=ALU.mult, op1=ALU.add),
                       [atb, itb, P_b], [hsb])
                if dbg and c == int(os.environ.get('DBGC', '0')) and not has_s and TGp == 704 and cnt.get("dumped") is None:
                    cnt["dumped"] = 1
                    dump(0, rx[:, :], rxb); dump(1, rt[:, :], rtb); dump(2, it[:, :], itb); dump(3, at[:, :], atb); dump(4, hs[:, :], hsb)
                    dump(5, c1[:, :], c1_b, KC); dump(7, ffo[:, c, :], ffo_bs[c]); dump(8, x[:, c, :], x_bs[c])
                v_copy(hprev[:, c:c + 1], hs[:, TGp - 1:TGp], [hsb], [hprev_b])
                if last:
                    v_copy(SO[:, c, O_HP:O_HP + 1], hs[:, TGp - 1:TGp], [hsb], [SO_b])
                    v_copy(SO[:, c, O_HS:O_HS + 1], hs[:, TG - 1:TG], [hsb], [SO_b])
                v_tt(gh[:, c, :TG], hs[:, :TG], rg[:, :TG], ALU.mult, [hsb, rgb], [gh_b])
                st8[c].clear()

            build_dg(1)
            H1(0); H2(0); H3(0); H4(0)
            for c in range(KC):
                if c + 1 < KC:
                    H1(c + 1)
                T1(c)
                if c + 1 < KC:
                    H2(c + 1)
                T2(c)
                if c + 1 < KC:
                    H3(c + 1)
                    H4(c + 1)
            ln_stats(KC - 1)

            for pi, (q0, w) in enumerate(spieces):
                st, stb = sPQ[pi]
                a_act(mean[:, q0:q0 + w], st[:, 0:w], AF.Copy, [stb], [mean_b], scale=1.0 / D)
                t, tb_ = tmpf()
                v_tt(t[:, :w], mean[:, q0:q0 + w], mean[:, q0:q0 + w], ALU.mult, [mean_b], [tb_])
                v_stt(t[:, :w], st[:, w:2 * w], 1.0 / D, t[:, :w], ALU.mult, ALU.subtract, [stb, tb_], [tb_])
                a_act(t[:, :w], t[:, :w], AF.Ln, [tb_, c1_b], [tb_], bias=eps_t[:, 0:1])
                a_act(rstd[:, q0:q0 + w], t[:, :w], AF.Exp, [tb_], [rstd_b], scale=-0.5)
            give_banks(sPQ)
            if dbg and cnt.get("dumped3") is None:
                cnt["dumped3"] = 1
                dump(12, mean[:, :], mean_b); dump(13, rstd[:, :], rstd_b)
            for c in range(KC):
                t, tb_ = tmpf()
                v_tt(t[:, :TG], ffo[:, c, :TG], mean[:, :TG], ALU.subtract, [ffo_bs[c], mean_b], [tb_])
                v_tt(t[:, :TG], t[:, :TG], rstd[:, :TG], ALU.mult, [tb_, rstd_b], [tb_])
                a_act(cvn[:, c, :TG], t[:, :TG], AF.Silu, [tb_, P_b], [cvn_b], scale=prm(c, R_LNG), bias=prm(c, R_LNB))

            for o in range(KC):
                w1 = next_unit()
                w2 = next_unit()
                srcs = [(w1, 0, xg, xg_bs), (w1, 1024, xg, xg_bs), (w2, 0, cvn, [cvn_b]), (w2, 1024, gh, [gh_b])]
                pss = [[bank() for _ in range(4)] for _ in blocks]
                order = ([(bi, j) for bi in range(len(blocks)) for j in (0, 1, 3)] + [(bi, 2) for bi in range(len(blocks))]) if o == 0 \
                    else [(bi, j) for bi in range(len(blocks)) for j in range(4)]
                for bi, j in order:
                    b0, nb = blocks[bi]
                    pt, pb = pss[bi][j]
                    wu, off, rhs_t, rhs_b = srcs[j]
                    pe_mm(pt[:, :nb], [(wu[0][:, off + kc * 128:off + (kc + 1) * 128], rhs_t[:, kc, b0:b0 + nb]) for kc in range(KC)],
                          [wu[1]] + rhs_b, [pb])
                for bi, (b0, nb) in enumerate(blocks):
                    ps = pss[bi]
                    ta, tab = tmpf()
                    tr_, trb = tmpf()
                    a_act(ta[:, :nb], ps[0][0][:, :nb], AF.Sigmoid, [ps[0][1]], [tab])
                    a_act(tr_[:, :nb], ps[1][0][:, :nb], AF.Sigmoid, [ps[1][1]], [trb])
                    v_tt(ta[:, :nb], ps[2][0][:, :nb], ta[:, :nb], ALU.mult, [ps[2][1], tab], [tab])
                    v_tt(tr_[:, :nb], ps[3][0][:, :nb], tr_[:, :nb], ALU.mult, [ps[3][1], trb], [trb])
                    v_tt(mg[:, o, b0:b0 + nb], ta[:, :nb], tr_[:, :nb], ALU.add, [tab, trb], [mg_b])

            sbk = take_banks(len(blocks))
            pending = []
            for oo in range(4):
                wu = next_unit()
                for o in (2 * oo, 2 * oo + 1):
                    off = (o % 2) * 1024
                    for bi, (b0, nb) in enumerate(blocks):
                        bt, bb = bank()
                        pe_mm(bt[:, :nb], [(wu[0][:, off + kc * 128:off + (kc + 1) * 128], mg[:, kc, b0:b0 + nb]) for kc in range(KC)],
                              [wu[1], mg_b], [bb])
                        for fn_ in pending:
                            fn_()
                        pending = []
                        a_act(ffo[:, o, b0:b0 + nb], bt[:, :nb], AF.Copy, [bb], [ffo_bs[o]])
                        q, qb = tmpb()
                        a_act(q[:, :nb], bt[:, :nb], AF.Square, [bb], [qb])
                        pending.append(lambda bi=bi, nb=nb, q=q, qb=qb, o=o: stats_mm(sbk[bi], nb, q[:, :nb], qb, o == 0, o == KC - 1))
            for fn_ in pending:
                fn_()
            postnorm(TG, blocks, sbk, R_GMPOST, 1.0)
            give_banks(sbk)

        def group_cfg(gi):
            p0, TGp, has_s = GROUPS[gi]
            TG = TGp + (SS if has_s else 0)
            h = TGp // 2
            return p0, TGp, has_s, TG, [(0, h), (h, TG - h)]

        def load_tile(src, r0, c0, nt):
            st_t, st_b, st_ds = stage[1]
            halves = [(st_b, st_ds), stage_h2]
            for half in range(2):
                hb, hds = halves[half]
                dma(SP, hds, st_t[:nt, half * 512:(half + 1) * 512], src[r0:r0 + nt, half * 512:(half + 1) * 512], writes=[hb])
            for half in range(2):
                hb, hds = halves[half]
                bt, bb = bank()
                for q in range(4):
                    kc = half * 4 + q
                    pe_tr(bt[:, q * 128:q * 128 + nt], st_t[:nt, kc * 128:(kc + 1) * 128], ident[:nt, :nt], [hb, const_b], [bb])
                a_act(x[:, half * 4:half * 4 + 4, c0:c0 + nt], bt[:, :].rearrange("p (q n) -> p q n", q=4)[:, :, :nt], AF.Copy, [bb], x_bs[half * 4:half * 4 + 4])
            cq_t, cq_b = cbqs[cnt["ldt"] % 2]
            cnt["ldt"] += 1
            q3 = cq_t[:, :, :].rearrange("p a b -> p (a b)")[:, 0:KC * 128].rearrange("p (k n) -> p k n", k=KC)[:, :, :nt]
            a_act(q3, x[:, :, c0:c0 + nt], AF.Square, x_bs, [cq_b])
            for fn_ in load_pend:
                fn_()
            load_pend.clear()

            def rest(c0=c0, nt=nt, q3=q3, cq_b=cq_b):
                sbank = bank()
                for kc in range(KC):
                    stats_mm(sbank, nt, q3[:, kc, :], cq_b, kc == 0, kc == KC - 1)
                rstd_from(sbank, c0, nt, 1.0)
                for c in range(KC):
                    v_stt(xg[:, c, c0:c0 + nt], x[:, c, c0:c0 + nt], prm(c, R_G1PRE), rstd[:, c0:c0 + nt], ALU.mult, ALU.mult,
                          [x_bs[c], P_b, rstd_b], [xg_bs[c]])
            load_pend.append(rest)

        def flush_load():
            for fn_ in load_pend:
                fn_()
            load_pend.clear()

        def store_tile(dst, r0, c0, nt):
            st_t, st_b, st_ds = stage[0]
            for half in range(2):
                bt, bb = bank()
                for q in range(4):
                    kc = half * 4 + q
                    pe_tr(bt[:nt, q * 128:(q + 1) * 128], ffo[:, kc, c0:c0 + nt], ident[:, :], [ffo_bs[kc], const_b], [bb])
                v_copy(st_t[:nt, half * 512:(half + 1) * 512], bt[:nt, :], [bb], [st_b])
            dma(ACT, st_ds, dst[r0:r0 + nt, :], st_t[:nt, :], reads=[st_b])

        def tiles_of(gi, dp, ds_):
            p0, TGp, has_s, TG, blocks = group_cfg(gi)
            tl = [(dp, p0 + t0, t0, min(128, TGp - t0)) for t0 in range(0, TGp, 128)]
            if has_s:
                tl.append((ds_, 0, TGp, SS))
            return tl

        for tl in tiles_of(0, xp_d, xs_d):
            load_tile(*tl)
        flush_load()
        for gi in range(len(GROUPS)):
            p0, TGp, has_s, TG, blocks = group_cfg(gi)
            last = gi == len(GROUPS) - 1
            ffn(TG, blocks, R_G1PRE, R_G1POST, do_prenorm=False)
            mixer(TG, TGp, has_s, blocks, last)
            ffn(TG, blocks, R_G2PRE, R_G2POST)
            prenorm(TG, blocks, R_GFIN, ffo, ffo_bs)
            outs = tiles_of(gi, yp_d, ys_d)
            ins = tiles_of(gi + 1, xp_d, xs_d) if not last else []
            for i in range(max(len(outs), len(ins))):
                if i < len(outs):
                    store_tile(*outs[i])
                if i < len(ins):
                    load_tile(*ins[i])
            flush_load()

        st_t, st_b, st_ds = stage[0]
        for half in range(2):
            bt, bb = bank()
            for q in range(4):
                kc = half * 4 + q
                pe_tr(bt[:NSO, q * 128:(q + 1) * 128], SO[:, kc, :], ident[:, :], [SO_b, const_b], [bb])
            a_act(st_t[:NSO, half * 512:(half + 1) * 512], bt[:NSO, :], AF.Copy, [bb], [st_b])
        dma(SP, st_ds, so_d, st_t[:NSO, :], reads=[st_b])
        for (_, _, ds) in stage:
            nc.sync.wait_ge(ds.sem, ds.count)
        for ds in dbg_dss:
            if ds.count:
                nc.sync.wait_ge(ds.sem, ds.count)
    return nc


_CACHE = {}


def kernel(x_prompt, x_sample, state_conv, state_rconv, state_h,
           g_ffn1_pre, g_ffn1_post, w_ffn1_in, w_ffn1_out,
           g_mix_pre, g_mix_post, w_in,
           w_dw, b_dw, ln_g, ln_b, w_conv_out,
           w_rconv, b_rconv, w_rg_a, b_rg_a, w_rg_x, b_rg_x, lam, w_rnn_out,
           w_out,
           g_ffn2_pre, g_ffn2_post, w_ffn2_in, w_ffn2_out, g_final):
    f = lambda a: np.ascontiguousarray(np.asarray(a, dtype=np.float32))
    if "nc" not in _CACHE:
        _CACHE["nc"] = build_program()
    nc = _CACHE["nc"]
    vec_rows = [g_ffn1_pre, g_ffn1_post, g_mix_pre, g_mix_post, b_dw, ln_g, ln_b, b_rconv, b_rg_a, b_rg_x, lam,
                g_ffn2_pre, g_ffn2_post, g_final]
    common = np.concatenate([f(v)[0][None, :] for v in vec_rows] + [f(w_dw)[0], f(w_rconv)[0]], axis=0)
    shared = {
        "ident": np.eye(128, dtype=np.float32),
        "w1i": f(w_ffn1_in)[0], "w1o": f(w_ffn1_out)[0], "wi": f(w_in)[0],
        "wco": f(w_conv_out)[0], "wro": f(w_rnn_out)[0], "wo": f(w_out)[0],
        "wga": f(w_rg_a)[0], "wgx": f(w_rg_x)[0],
        "w2i": f(w_ffn2_in)[0], "w2o": f(w_ffn2_out)[0],
    }
    xp, xs = f(x_prompt), f(x_sample)
    sc, sr, sh = f(state_conv)[0], f(state_rconv)[0], f(state_h)[0]
    in_maps = []
    for i in range(NCORES):
        rows = np.ascontiguousarray(np.concatenate([common, sc[i], sr[i], sh[i][None, :]], axis=0))
        m = dict(shared)
        m.update({"xp": xp[i], "xs": xs[i], "rows": rows})
        in_maps.append(m)
    res = run_bass_kernel_spmd(nc, in_maps, core_ids=list(range(NCORES)))
    r = res.results
    yp = np.stack([r[i]["yp"] for i in range(NCORES)])
    ys = np.stack([r[i]["ys"] for i in range(NCORES)])
    so = np.stack([r[i]["so"] for i in range(NCORES)])
    return (yp.astype(np.float32), ys.astype(np.float32),
            so[None, :, O_CP:O_CP + 30, :], so[None, :, O_RP:O_RP + 3, :], so[None, :, O_HP, :],
            so[None, :, O_CS:O_CS + 30, :], so[None, :, O_RS:O_RS + 3, :], so[None, :, O_HS, :])
```

```python
import os
from contextlib import ExitStack

import numpy as np
import concourse.bass as bass
import concourse.mybir as mybir
from concourse.bass_utils import run_bass_kernel_spmd

F32 = mybir.dt.float32
BF16 = mybir.dt.bfloat16
AF = mybir.ActivationFunctionType
ALU = mybir.AluOpType

D = 1024
KC = 8
DFF = 2816
FC = 22
S = 2048
SS = 32
CK = 31
RK = 4
EPS = 1e-6
NCORES = 8

R_G1PRE, R_G1POST, R_GMPRE, R_GMPOST, R_BDW, R_LNG, R_LNB, R_BRC, R_BA, R_BX, R_LAM, R_G2PRE, R_G2POST, R_GFIN = range(14)
R_WDW = 14
R_WRC = R_WDW + CK
R_SCONV = R_WRC + RK
R_SRC = R_SCONV + 30
R_SH = R_SRC + 3
NR = R_SH + 1
O_CP, O_RP, O_HP, O_CS, O_RS, O_HS = 0, 30, 33, 34, 64, 67
NSO = 68

GROUPS = [(0, 704, False), (704, 704, False), (1408, 640, True)]
TGM = 704
U_S = 30 + TGM
UW = U_S + 30 + SS
RX_S = 3 + TGM
RXW = RX_S + 3 + SS
NSLOT = 6


class Eng:
    def __init__(self, name, h, sem):
        self.name, self.h, self.sem = name, h, sem
        self.count = 0
        self.seen = {}
        self.strict = name in ("act", "dve", "pool")


class DSem:
    def __init__(self, name, sem):
        self.name, self.sem = name, sem
        self.count = 0


class Buf:
    def __init__(self, name):
        self.name = name
        self.w = {}
        self.r = {}


def _deps(eng, reads, writes):
    deps = {}
    for b in reads:
        for o, c in b.w.items():
            if c > deps.get(o, 0):
                deps[o] = c
    for b in writes:
        for o, c in b.w.items():
            if (o is not eng or eng.strict) and c > deps.get(o, 0):
                deps[o] = c
        for o, c in b.r.items():
            if (o is not eng or eng.strict) and c > deps.get(o, 0):
                deps[o] = c
    return deps


def _wait(eng, deps):
    for o, c in deps.items():
        if c <= eng.seen.get(o, 0):
            continue
        eng.h.wait_ge(o.sem, c)
        eng.seen[o] = c


def op(eng, fn, reads=(), writes=()):
    _wait(eng, _deps(eng, reads, writes))
    inst = fn()
    eng.count += 1
    inst.then_inc(eng.sem, 1)
    for b in reads:
        b.r[eng] = eng.count
    for b in writes:
        b.w[eng] = eng.count


def dma(q, dsem, out_ap, in_ap, reads=(), writes=()):
    deps = {}
    for b in reads:
        for o, c in b.w.items():
            if c > deps.get(o, 0):
                deps[o] = c
    for b in writes:
        for o, c in b.w.items():
            if o is not q and o is not dsem and c > deps.get(o, 0):
                deps[o] = c
        for o, c in b.r.items():
            if o is not q and c > deps.get(o, 0):
                deps[o] = c
    _wait(q, deps)
    q.h.dma_start(out=out_ap, in_=in_ap).then_inc(dsem.sem, 16)
    dsem.count += 16
    for b in reads:
        b.r[dsem] = dsem.count
    for b in writes:
        b.w[dsem] = dsem.count


def build_program(dbg=False):
    nc = bass.Bass("TRN2", target_bir_lowering=False)

    def din(name, shape):
        return nc.dram_tensor(name, shape, F32, kind="ExternalInput").ap()

    def dout(name, shape):
        return nc.dram_tensor(name, shape, F32, kind="ExternalOutput").ap()

    xp_d = din("xp", [S, D])
    xs_d = din("xs", [SS, D])
    rows_d = din("rows", [NR, D])
    ident_d = din("ident", [128, 128])
    w1i_d = din("w1i", [D, 2 * DFF])
    w1o_d = din("w1o", [DFF, D])
    wi_d = din("wi", [D, 6 * D])
    wco_d = din("wco", [D, D])
    wro_d = din("wro", [D, D])
    wo_d = din("wo", [D, D])
    wga_d = din("wga", [8, 128, 128])
    wgx_d = din("wgx", [8, 128, 128])
    w2i_d = din("w2i", [D, 2 * DFF])
    w2o_d = din("w2o", [DFF, D])
    yp_d = dout("yp", [S, D])
    ys_d = dout("ys", [SS, D])
    so_d = dout("so", [NSO, D])
    dbg_d = dout("dbg", [16, 128, TGM]) if dbg else None

    es = ExitStack()
    with es:
        def sb(name, shape, dt):
            return es.enter_context(nc.sbuf_tensor(name, shape, dt))

        def sem(name):
            return es.enter_context(nc.semaphore(name))

        PE = Eng("pe", nc.tensor, sem("s_pe"))
        ACT = Eng("act", nc.scalar, sem("s_act"))
        DVE = Eng("dve", nc.vector, sem("s_dve"))
        POOL = Eng("pool", nc.gpsimd, sem("s_pool"))
        SP = Eng("sp", nc.sync, sem("s_sp"))

        x = sb("x", [128, KC, TGM], F32)
        xg = sb("xg", [128, KC, TGM], BF16)
        mg = sb("mg", [128, KC, TGM], BF16)
        ffo = sb("ffo", [128, KC, TGM], F32)
        act = sb("act", [128, FC, TGM], BF16)
        gh = act[:, 0:8, :]
        cvn = act[:, 8:16, :]
        mg_b, act_b, gh_b, cvn_b = Buf("mg"), Buf("act"), Buf("gh"), Buf("cvn")
        xg_bs = [Buf(f"xg{c}") for c in range(KC)]
        x_bs = [Buf(f"x{c}") for c in range(KC)]
        ffo_bs = [Buf(f"ffo{c}") for c in range(KC)]
        rstd = sb("rstd", [128, TGM], F32)
        mean = sb("mean", [128, TGM], F32)
        rstd_b, mean_b = Buf("rstd"), Buf("mean")
        P = sb("P", [128, KC, NR], F32)
        P_b = Buf("P")
        c1 = sb("c1", [128, KC], F32)
        c1_b = Buf("c1")
        c1h = sb("c1h", [128, KC], F32)
        eps_t = sb("eps_t", [128, 1], F32)
        q_t = sb("q_t", [128, 1], F32)
        bh = sb("bh", [128, 2 * KC], F32)
        SO = sb("SO", [128, KC, NSO], F32)
        SO_b = Buf("SO")
        uh = sb("uh", [128, KC, 30], BF16)
        rh = sb("rh", [128, KC, 3], F32)
        hprev = sb("hprev", [128, KC], F32)
        uh_b, rh_b, hprev_b = Buf("uh"), Buf("rh"), Buf("hprev")
        ident = sb("identf", [128, 128], F32)
        identb = sb("identb", [128, 128], BF16)
        ones = sb("ones", [128, 128], BF16)
        const_b = Buf("const")
        NTF, NTB = 7, 4
        tf = [(sb(f"tf{i}", [128, TGM], F32), Buf(f"tf{i}")) for i in range(NTF)]
        tb = [(sb(f"tb{i}", [128, TGM], BF16), Buf(f"tb{i}")) for i in range(NTB)]
        ut = [(sb(f"ut{i}", [128, UW], BF16), Buf(f"ut{i}")) for i in range(KC)]
        rxi = [(sb(f"rxi{i}", [128, RXW], F32), Buf(f"rxi{i}")) for i in range(1)]
        cbqs = [(sb(f"cbq{i}", [128, 2, TGM], BF16), Buf(f"cbq{i}")) for i in range(2)]
        Dg = [(sb(f"Dg{i}", [128, CK, 128], BF16), Buf(f"Dg{i}")) for i in range(2)]
        stage = [(sb(f"stg{i}", [128, D], F32), Buf(f"stg{i}"), DSem(f"stg{i}", sem(f"d_stg{i}"))) for i in range(2)]
        ring = [(sb(f"wr{i}", [128, 2048], BF16), Buf(f"wr{i}"), DSem(f"wr{i}", sem(f"d_wr{i}"))) for i in range(NSLOT)]
        banks = [(es.enter_context(nc.psum_tensor(f"bk{i}", [128, 512], F32)), Buf(f"bk{i}")) for i in range(8)]
        misc_ds = DSem("misc", sem("d_misc"))
        dbg_dss = [DSem(f"dbg{i}", sem(f"d_dbg{i}")) for i in range(16)] if dbg else []

        def dump(i, ap, buf, n=TGM):
            if dbg:
                dma(SP, dbg_dss[i], dbg_d[i, :, :n], ap, reads=[buf])

        cnt = {"tf": 0, "tb": 0, "ut": 0, "rxi": 0, "dg": 0, "stg": 0, "bank": 0, "ldt": 0}
        load_pend = []
        pool = list(range(8))

        def rr(key, lst):
            i = cnt[key] % len(lst)
            cnt[key] += 1
            return lst[i]

        def tmpf():
            return rr("tf", tf)

        def tmpb():
            return rr("tb", tb)

        def bank():
            i = pool[cnt["bank"] % len(pool)]
            cnt["bank"] += 1
            return banks[i]

        def take_banks(n):
            out = [banks[pool.pop()] for _ in range(n)]
            return out

        def give_banks(bs):
            for b in bs:
                pool.append([i for i in range(8) if banks[i] is b][0])

        def pe_mm(out_ap, pairs, reads, writes, start=True, stop=True):
            def fn():
                n = len(pairs)
                ins = None
                for i, (l, r) in enumerate(pairs):
                    ins = nc.tensor.matmul(out_ap, lhsT=l, rhs=r, start=(start and i == 0), stop=(stop and i == n - 1))
                return ins
            op(PE, fn, reads, writes)

        def pe_tr(out_ap, in_ap, idn, reads, writes):
            op(PE, lambda: nc.tensor.transpose(out=out_ap, in_=in_ap, identity=idn), reads, writes)

        def a_act(out_ap, in_ap, func, reads, writes, scale=None, bias=None):
            kw = {}
            if scale is not None:
                kw["scale"] = scale
            if bias is not None:
                kw["bias"] = bias
            op(ACT, lambda: nc.scalar.activation(out=out_ap, in_=in_ap, func=func, **kw), reads, writes)

        def v_tt(out_ap, in0, in1, alu, reads, writes):
            op(DVE, lambda: nc.vector.tensor_tensor(out=out_ap, in0=in0, in1=in1, op=alu), reads, writes)

        def v_stt(out_ap, in0, scalar, in1, op0, op1, reads, writes):
            op(DVE, lambda: nc.vector.scalar_tensor_tensor(out=out_ap, in0=in0, scalar=scalar, in1=in1, op0=op0, op1=op1), reads, writes)

        def v_ts(out_ap, in0, s1, s2, op0, op1, reads, writes):
            op(DVE, lambda: nc.vector.tensor_scalar(out=out_ap, in0=in0, scalar1=s1, scalar2=s2, op0=op0, op1=op1), reads, writes)

        def v_copy(out_ap, in_ap, reads, writes):
            op(DVE, lambda: nc.vector.tensor_copy(out=out_ap, in_=in_ap), reads, writes)

        def v_recip(out_ap, in_ap, reads, writes):
            op(DVE, lambda: nc.vector.reciprocal(out=out_ap, in_=in_ap), reads, writes)

        def prm(c, r):
            return P[:, c, r:r + 1]

        def wv(w, kc):
            return w.rearrange("(kc p) n -> p kc n", p=128)

        units = []

        def ffn_units(wi_, wo_):
            wiv = wv(wi_, KC)
            wov = wv(wo_, FC)
            for m in range(FC):
                units.append([(0, KC, wiv[:, :, m * 128:(m + 1) * 128]),
                              (1024, KC, wiv[:, :, DFF + m * 128:DFF + (m + 1) * 128])])
            for o in range(KC):
                for h in range(2):
                    units.append([(0, 11, wov[:, 11 * h:11 * h + 11, o * 128:(o + 1) * 128])])

        def mixer_units():
            wiv = wv(wi_d, KC)
            for c in range(KC):
                units.append([(0, KC, wiv[:, :, c * 128:(c + 1) * 128]),
                              (1024, KC, wiv[:, :, D + c * 128:D + (c + 1) * 128])])
            for c in range(KC):
                units.append([(0, KC, wiv[:, :, 2 * D + c * 128:2 * D + (c + 1) * 128]),
                              (1024, KC, wiv[:, :, 3 * D + c * 128:3 * D + (c + 1) * 128])])
            wcov, wrov, wov = wv(wco_d, KC), wv(wro_d, KC), wv(wo_d, KC)
            for o in range(KC):
                units.append([(0, KC, wiv[:, :, 4 * D + o * 128:4 * D + (o + 1) * 128]),
                              (1024, KC, wiv[:, :, 5 * D + o * 128:5 * D + (o + 1) * 128])])
                units.append([(0, KC, wcov[:, :, o * 128:(o + 1) * 128]),
                              (1024, KC, wrov[:, :, o * 128:(o + 1) * 128])])
            for oo in range(4):
                units.append([(0, KC, wov[:, :, (2 * oo) * 128:(2 * oo + 1) * 128]),
                              (1024, KC, wov[:, :, (2 * oo + 1) * 128:(2 * oo + 2) * 128])])

        for _ in GROUPS:
            ffn_units(w1i_d, w1o_d)
            mixer_units()
            ffn_units(w2i_d, w2o_d)

        wstate = {"issued": 0, "used": 0}

        def issue_unit():
            u = wstate["issued"]
            if u >= len(units):
                return
            t, b, ds = ring[u % NSLOT]
            for (off, a, src) in units[u]:
                dst = t[:, off:off + a * 128].rearrange("p (a b) -> p a b", a=a)
                dma(POOL, ds, dst, src, writes=[b])
            wstate["issued"] += 1

        def next_unit():
            u = wstate["used"]
            while wstate["issued"] < min(len(units), u + NSLOT - 1):
                issue_unit()
            wstate["used"] += 1
            return ring[u % NSLOT]

        gw = sb("gw", [128, 2048], BF16)
        gwb = Buf("gw")
        gw_ds = DSem("gw", sem("d_gw"))
        dma(POOL, gw_ds, gw[:, 0:1024].rearrange("p (a b) -> p a b", a=8), wga_d.rearrange("n h k -> h n k"), writes=[gwb])
        dma(POOL, gw_ds, gw[:, 1024:2048].rearrange("p (a b) -> p a b", a=8), wgx_d.rearrange("n h k -> h n k"), writes=[gwb])
        dma(SP, misc_ds, ident[:], ident_d, writes=[const_b])
        st_t, st_b, st_ds = stage[0]
        dma(SP, st_ds, st_t[:NR, :], rows_d, writes=[st_b])
        op(DVE, lambda: nc.vector.memset(ones[:], 1.0), writes=[const_b])
        v_copy(identb[:], ident[:], [const_b], [const_b])
        op(DVE, lambda: nc.vector.memset(uh[:], 0.0), writes=[uh_b])
        op(DVE, lambda: nc.vector.memset(rh[:], 0.0), writes=[rh_b])
        op(DVE, lambda: nc.vector.memset(hprev[:], 0.0), writes=[hprev_b])
        for half in range(2):
            bt, bb = bank()
            for q in range(4):
                kc = half * 4 + q
                pe_tr(bt[:, q * 128:q * 128 + NR], st_t[:NR, kc * 128:(kc + 1) * 128], ident[:NR, :NR], [st_b, const_b], [bb])
            a_act(P[:, half * 4:half * 4 + 4, :], bt[:, :].rearrange("p (q n) -> p q n", q=4)[:, :, :NR], AF.Copy, [bb], [P_b])
        t1, t1b = tmpf()
        a_act(t1[:, 0:KC], P[:, :, R_LAM], AF.Exp, [P_b], [t1b], scale=-1.0)
        a_act(t1[:, 8:8 + KC], t1[:, 0:KC], AF.Ln, [t1b], [t1b], bias=1.0)
        v_ts(c1[:, :], t1[:, 8:8 + KC], -8.0, None, ALU.mult, ALU.bypass, [t1b], [c1_b])
        v_ts(c1h[:, :], t1[:, 8:8 + KC], -4.0, None, ALU.mult, ALU.bypass, [t1b], [c1_b])
        op(DVE, lambda: nc.vector.memset(eps_t[:], EPS), writes=[c1_b])
        op(DVE, lambda: nc.vector.memset(q_t[:], 0.25), writes=[c1_b])
        v_ts(bh[:, 0:KC], P[:, :, R_BA], 0.5, None, ALU.mult, ALU.bypass, [P_b], [c1_b])
        v_ts(bh[:, KC:2 * KC], P[:, :, R_BX], 0.5, None, ALU.mult, ALU.bypass, [P_b], [c1_b])

        def stats_mm(sbank, nb, src_ap, src_b, first, last):
            st, sbuf_ = sbank
            pe_mm(st[:, :nb], [(ones[:, :], src_ap)], [const_b, src_b], [sbuf_], start=first, stop=last)

        def rstd_from(sbank, b0, nb, f):
            st, sbuf_ = sbank
            t, tb_ = tmpf()
            a_act(t[:, :nb], st[:, :nb], AF.Ln, [sbuf_, c1_b], [tb_], scale=1.0 / D, bias=eps_t[:, 0:1])
            a_act(rstd[:, b0:b0 + nb], t[:, :nb], AF.Exp, [tb_], [rstd_b], scale=-0.5, bias=float(np.log(f)))

        def prenorm(TG, blocks, grow, out_t, out_b):
            sbk = take_banks(len(blocks))
            for c in range(KC):
                q, qb = tmpb()
                a_act(q[:, :TG], x[:, c, :TG], AF.Square, [x_bs[c]], [qb])
                for bi, (b0, nb) in enumerate(blocks):
                    stats_mm(sbk[bi], nb, q[:, b0:b0 + nb], qb, c == 0, c == KC - 1)
            for bi, (b0, nb) in enumerate(blocks):
                rstd_from(sbk[bi], b0, nb, 1.0)
            give_banks(sbk)
            for c in range(KC):
                v_stt(out_t[:, c, :TG], x[:, c, :TG], prm(c, grow), rstd[:, :TG], ALU.mult, ALU.mult, [x_bs[c], P_b, rstd_b], [out_b[c] if isinstance(out_b, list) else out_b])

        def postnorm(TG, blocks, sbk, grow, f):
            for bi, (b0, nb) in enumerate(blocks):
                rstd_from(sbk[bi], b0, nb, f)
            for c in range(KC):
                v_tt(ffo[:, c, :TG], ffo[:, c, :TG], rstd[:, :TG], ALU.mult, [ffo_bs[c], rstd_b], [ffo_bs[c]])
                v_stt(x[:, c, :TG], ffo[:, c, :TG], prm(c, grow), x[:, c, :TG], ALU.mult, ALU.add, [ffo_bs[c], P_b, x_bs[c]], [x_bs[c]])

        def first_unit(wt, wb, blocks):
            res = [(bank(), bank()) for _ in blocks]
            for kc in range(KC):
                for (b0, nb), ((a_t, a_b), (b_t, b_b)) in zip(blocks, res):
                    pe_mm(a_t[:, :nb], [(wt[:, kc * 128:(kc + 1) * 128], xg[:, kc, b0:b0 + nb])], [wb, xg_bs[kc]], [a_b],
                          start=(kc == 0), stop=(kc == KC - 1))
                    pe_mm(b_t[:, :nb], [(wt[:, 1024 + kc * 128:1024 + (kc + 1) * 128], xg[:, kc, b0:b0 + nb])], [wb, xg_bs[kc]], [b_b],
                          start=(kc == 0), stop=(kc == KC - 1))
            return res

        def ffn(TG, blocks, grow_pre, grow_post, do_prenorm=True):
            if do_prenorm:
                prenorm(TG, blocks, grow_pre, xg, xg_bs)
            for m in range(FC):
                wt, wb, _ = next_unit()
                pre = first_unit(wt, wb, blocks) if m == 0 else None
                for bi, (b0, nb) in enumerate(blocks):
                    if pre is not None:
                        (gt, gb), (upt, upb) = pre[bi]
                    else:
                        gt, gb = bank()
                        upt, upb = bank()
                        pe_mm(gt[:, :nb], [(wt[:, kc * 128:(kc + 1) * 128], xg[:, kc, b0:b0 + nb]) for kc in range(KC)], [wb] + xg_bs, [gb])
                        pe_mm(upt[:, :nb], [(wt[:, 1024 + kc * 128:1024 + (kc + 1) * 128], xg[:, kc, b0:b0 + nb]) for kc in range(KC)], [wb] + xg_bs, [upb])
                    t, tb_ = tmpf()
                    a_act(t[:, :nb], gt[:, :nb], AF.Silu, [gb], [tb_])
                    v_tt(act[:, m, b0:b0 + nb], upt[:, :nb], t[:, :nb], ALU.mult, [upb, tb_], [act_b, gh_b, cvn_b])
            sbk = take_banks(len(blocks))
            pending = []
            for o in range(KC):
                w0 = next_unit()
                w1 = next_unit()
                for bi, (b0, nb) in enumerate(blocks):
                    bt, bb = bank()
                    pairs = []
                    for kc in range(FC):
                        wt = (w0 if kc < 11 else w1)[0]
                        kk = kc % 11
                        pairs.append((wt[:, kk * 128:(kk + 1) * 128], act[:, kc, b0:b0 + nb]))
                    pe_mm(bt[:, :nb], pairs, [w0[1], w1[1], act_b, gh_b, cvn_b], [bb])
                    for fn_ in pending:
                        fn_()
                    pending = []
                    a_act(ffo[:, o, b0:b0 + nb], bt[:, :nb], AF.Copy, [bb], [ffo_bs[o]])
                    q, qb = tmpb()
                    a_act(q[:, :nb], bt[:, :nb], AF.Square, [bb], [qb])
                    pending.append(lambda bi=bi, nb=nb, q=q, qb=qb, o=o: stats_mm(sbk[bi], nb, q[:, :nb], qb, o == 0, o == KC - 1))
            for fn_ in pending:
                fn_()
            postnorm(TG, blocks, sbk, grow_post, 0.5)
            give_banks(sbk)

        def mixer(TG, TGp, has_s, blocks, last):
            prenorm(TG, blocks, R_GMPRE, xg, xg_bs)

            def split(b0, nb):
                npr = min(nb, TGp - b0)
                return npr, nb - npr

            for c in range(KC):
                wt, wb, _ = next_unit()
                u, ub = ut[c]
                v_copy(u[:, 0:30], uh[:, c, :], [uh_b], [ub])
                if has_s:
                    v_copy(u[:, U_S:U_S + 30], P[:, c, R_SCONV:R_SCONV + 30], [P_b], [ub])
                pre = first_unit(wt, wb, blocks) if c == 0 else None
                for bi, (b0, nb) in enumerate(blocks):
                    npr, nsm = split(b0, nb)
                    if pre is not None:
                        (vt, vb), (gt, gb) = pre[bi]
                    else:
                        vt, vb = bank()
                        gt, gb = bank()
                        pe_mm(vt[:, :nb], [(wt[:, kc * 128:(kc + 1) * 128], xg[:, kc, b0:b0 + nb]) for kc in range(KC)], [wb] + xg_bs, [vb])
                        pe_mm(gt[:, :nb], [(wt[:, 1024 + kc * 128:1024 + (kc + 1) * 128], xg[:, kc, b0:b0 + nb]) for kc in range(KC)], [wb] + xg_bs, [gb])
                    t, tb_ = tmpf()
                    a_act(t[:, :nb], gt[:, :nb], AF.Sigmoid, [gb], [tb_])
                    v_tt(u[:, 30 + b0:30 + b0 + npr], vt[:, :npr], t[:, :npr], ALU.mult, [vb, tb_], [ub])
                    if nsm:
                        v_tt(u[:, U_S + 30:U_S + 30 + SS], vt[:, npr:nb], t[:, npr:nb], ALU.mult, [vb, tb_], [ub])
                    if last and bi == len(blocks) - 1:
                        v_tt(SO[:, c, O_CP:O_CP + 30], vt[:, npr - 30:npr], t[:, npr - 30:npr], ALU.mult, [vb, tb_], [SO_b])
                        v_tt(SO[:, c, O_CS:O_CS + 30], vt[:, npr + 2:npr + 32], t[:, npr + 2:npr + 32], ALU.mult, [vb, tb_], [SO_b])
                v_copy(uh[:, c, :], u[:, TGp:TGp + 30], [ub], [uh_b])

            def build_dg(c):
                dg, dgb = Dg[c % 2]
                v_tt(dg[:, :, :], identb[:, :].unsqueeze(1).broadcast_to([128, CK, 128]),
                     P[:, c, R_WDW:R_WDW + CK].unsqueeze(2).broadcast_to([128, CK, 128]), ALU.mult, [const_b, P_b], [dgb])

            build_dg(0)

            spieces = [(q0, min(256, TG - q0)) for q0 in range(0, TG, 256)]
            sPQ = take_banks(len(spieces))
            def ln_stats(c):
                cbq, cbq_b = cbqs[c % 2]
                for pi, (q0, w) in enumerate(spieces):
                    st, stb = sPQ[pi]
                    pe_mm(st[:, 0:2 * w].rearrange("p (a b) -> p a b", a=2), [(ones[:, :], cbq[:, :, q0:q0 + w])], [const_b, cbq_b], [stb],
                          start=(c == 0), stop=(c == KC - 1))

            st8 = [dict() for _ in range(KC)]

            def H1(c):
                d = st8[c]
                wt, wb, _ = next_unit()
                ri, rib = rr("rxi", rxi)
                d["ri"] = (ri, rib)
                v_copy(ri[:, 0:3], rh[:, c, :], [rh_b], [rib])
                if has_s:
                    v_copy(ri[:, RX_S:RX_S + 3], P[:, c, R_SRC:R_SRC + 3], [P_b], [rib])
                rg, rgb = tmpb()
                d["rg"] = (rg, rgb)
                for bi, (b0, nb) in enumerate(blocks):
                    npr, nsm = split(b0, nb)
                    xt_, xb_ = bank()
                    gt, gb = bank()
                    pe_mm(xt_[:, :nb], [(wt[:, kc * 128:(kc + 1) * 128], xg[:, kc, b0:b0 + nb]) for kc in range(KC)], [wb] + xg_bs, [xb_])
                    pe_mm(gt[:, :nb], [(wt[:, 1024 + kc * 128:1024 + (kc + 1) * 128], xg[:, kc, b0:b0 + nb]) for kc in range(KC)], [wb] + xg_bs, [gb])
                    v_copy(ri[:, 3 + b0:3 + b0 + npr], xt_[:, :npr], [xb_], [rib])
                    if nsm:
                        v_copy(ri[:, RX_S + 3:RX_S + 3 + SS], xt_[:, npr:nb], [xb_], [rib])
                    a_act(rg[:, b0:b0 + nb], gt[:, :nb], AF.Gelu_apprx_tanh, [gb], [rgb])

            def H2(c):
                d = st8[c]
                ri, rib = d["ri"]
                v_copy(rh[:, c, :], ri[:, TGp:TGp + 3], [rib], [rh_b])
                if last:
                    v_copy(SO[:, c, O_RP:O_RP + 3], ri[:, TGp:TGp + 3], [rib], [SO_b])
                    v_copy(SO[:, c, O_RS:O_RS + 3], ri[:, RX_S + SS:RX_S + SS + 3], [rib], [SO_b])
                rx, rxb = tmpf()
                d["rx"] = (rx, rxb)
                segs = [(0, 0, TGp)] + ([(RX_S, TGp, SS)] if has_s else [])
                for (src0, dst0, n) in segs:
                    v_ts(rx[:, dst0:dst0 + n], ri[:, src0:src0 + n], prm(c, R_WRC), prm(c, R_BRC), ALU.mult, ALU.add, [rib, P_b], [rxb])
                    for k in range(1, RK):
                        v_stt(rx[:, dst0:dst0 + n], ri[:, src0 + k:src0 + k + n], prm(c, R_WRC + k), rx[:, dst0:dst0 + n], ALU.mult, ALU.add, [rib, P_b, rxb], [rxb])
                rxq, rxqb = tmpb()
                d["rxq"] = (rxq, rxqb)
                a_act(rxq[:, :TG], rx[:, :TG], AF.Copy, [rxb], [rxqb])

            def H3(c):
                u, ub = ut[c]
                dg, dgb = Dg[c % 2]
                for bi, (b0, nb) in enumerate(blocks):
                    npr, nsm = split(b0, nb)
                    bt, bb = bank()
                    pe_mm(bt[:, :npr], [(dg[:, k, :], u[:, b0 + k:b0 + k + npr]) for k in range(CK)], [dgb, ub], [bb])
                    if nsm:
                        pe_mm(bt[:, npr:nb], [(dg[:, k, :], u[:, U_S + k:U_S + k + SS]) for k in range(CK)], [dgb, ub], [bb])
                    a_act(ffo[:, c, b0:b0 + nb], bt[:, :nb], AF.Identity, [bb, P_b], [ffo_bs[c]], bias=prm(c, R_BDW))
                if c + 2 < KC:
                    build_dg(c + 2)

            def H4(c):
                d = st8[c]
                rxq, rxqb = d["rxq"]
                rt, rtb = tmpf()
                it, itb = tmpf()
                d["rt"], d["it"] = (rt, rtb), (it, itb)
                for bi, (b0, nb) in enumerate(blocks):
                    rp, rpb = bank()
                    ip, ipb = bank()
                    pe_mm(rp[:, :nb], [(gw[:, c * 128:(c + 1) * 128], rxq[:, b0:b0 + nb])], [gwb, rxqb], [rpb])
                    pe_mm(ip[:, :nb], [(gw[:, 1024 + c * 128:1024 + (c + 1) * 128], rxq[:, b0:b0 + nb])], [gwb, rxqb], [ipb])
                    a_act(rt[:, b0:b0 + nb], rp[:, :nb], AF.Tanh, [rpb, c1_b], [rtb], scale=0.5, bias=bh[:, c:c + 1])
                    a_act(it[:, b0:b0 + nb], ip[:, :nb], AF.Tanh, [ipb, c1_b], [itb], scale=0.5, bias=bh[:, KC + c:KC + c + 1])
                if c >= 1:
                    ln_stats(c - 1)
                cbq, cbq_b = cbqs[c % 2]
                a_act(cbq[:, 0, :TG], ffo[:, c, :TG], AF.Copy, [ffo_bs[c]], [cbq_b])
                a_act(cbq[:, 1, :TG], ffo[:, c, :TG], AF.Square, [ffo_bs[c]], [cbq_b])

            def T1(c):
                d = st8[c]
                rt, rtb = d["rt"]
                at, atb = tmpf()
                d["at"] = (at, atb)
                a_act(at[:, :TG], rt[:, :TG], AF.Exp, [rtb, c1_b], [atb], scale=c1h[:, c:c + 1], bias=c1h[:, c:c + 1])
                a_act(rt[:, :TG], at[:, :TG], AF.Square, [atb], [rtb])
                a_act(rt[:, :TG], rt[:, :TG], AF.Sqrt, [rtb, c1_b], [rtb], scale=-0.25, bias=q_t[:, 0:1])

            def T2(c):
                d = st8[c]
                rt, rtb = d["rt"]
                it, itb = d["it"]
                at, atb = d["at"]
                rx, rxb = d["rx"]
                rg, rgb = d["rg"]
                ri, rib = d["ri"]
                v_stt(it[:, :TG], it[:, :TG], 1.0, rx[:, :TG], ALU.add, ALU.mult, [itb, rxb], [itb])
                v_tt(it[:, :TG], it[:, :TG], rt[:, :TG], ALU.mult, [itb, rtb], [itb])
                hs, hsb = tmpf()
                op(DVE, lambda: nc.vector.tensor_tensor_scan(out=hs[:, :TGp], data0=at[:, :TGp], data1=it[:, :TGp],
                                                             initial=hprev[:, c:c + 1], op0=ALU.mult, op1=ALU.add),
                   [atb, itb, hprev_b], [hsb])
                if has_s:
                    op(DVE, lambda: nc.vector.tensor_tensor_scan(out=hs[:, TGp:TG], data0=at[:, TGp:TG], data1=it[:, TGp:TG],
                                                                 initial=prm(c, R_SH), op0=ALU.mult, op1=ALU.add),
                       [atb, itb, P_b], [hsb])
                if dbg and c == int(os.environ.get('DBGC', '0')) and not has_s and TGp == 704 and cnt.get("dumped") is None:
                    cnt["dumped"] = 1
                    dump(0, rx[:, :], rxb); dump(1, rt[:, :], rtb); dump(2, it[:, :], itb); dump(3, at[:, :], atb); dump(4, hs[:, :], hsb)
                    dump(5, c1[:, :], c1_b, KC); dump(7, ffo[:, c, :], ffo_bs[c]); dump(8, x[:, c, :], x_bs[c])
                v_copy(hprev[:, c:c + 1], hs[:, TGp - 1:TGp], [hsb], [hprev_b])
                if last:
                    v_copy(SO[:, c, O_HP:O_HP + 1], hs[:, TGp - 1:TGp], [hsb], [SO_b])
                    v_copy(SO[:, c, O_HS:O_HS + 1], hs[:, TG - 1:TG], [hsb], [SO_b])
                v_tt(gh[:, c, :TG], hs[:, :TG], rg[:, :TG], ALU.mult, [hsb, rgb], [gh_b])
                st8[c].clear()

            build_dg(1)
            H1(0); H2(0); H3(0); H4(0)
            for c in range(KC - 1):
                H1(c + 1)
                T1(c)
                H2(c + 1)
                T2(c)
                H3(c + 1)
                H4(c + 1)
            ln_stats(KC - 1)

            for pi, (q0, w) in enumerate(spieces):
                st, stb = sPQ[pi]
                rs = rstd[:, q0:q0 + w]
                a_act(mean[:, q0:q0 + w], st[:, 0:w], AF.Copy, [stb], [mean_b], scale=1.0 / D)
                v_tt(rs, mean[:, q0:q0 + w], mean[:, q0:q0 + w], ALU.mult, [mean_b], [rstd_b])
                v_stt(rs, st[:, w:2 * w], 1.0 / D, rs, ALU.mult, ALU.subtract, [stb, rstd_b], [rstd_b])
                a_act(rs, rs, AF.Ln, [rstd_b, c1_b], [rstd_b], bias=eps_t[:, 0:1])
                a_act(rs, rs, AF.Exp, [rstd_b], [rstd_b], scale=-0.5)
            give_banks(sPQ)
            if dbg and cnt.get("dumped3") is None:
                cnt["dumped3"] = 1
                dump(12, mean[:, :], mean_b); dump(13, rstd[:, :], rstd_b)

            def ln_norm(c):
                v_tt(ffo[:, c, :TG], ffo[:, c, :TG], mean[:, :TG], ALU.subtract, [ffo_bs[c], mean_b], [ffo_bs[c]])
                v_tt(ffo[:, c, :TG], ffo[:, c, :TG], rstd[:, :TG], ALU.mult, [ffo_bs[c], rstd_b], [ffo_bs[c]])
                a_act(cvn[:, c, :TG], ffo[:, c, :TG], AF.Silu, [ffo_bs[c], P_b], [cvn_b], scale=prm(c, R_LNG), bias=prm(c, R_LNB))

            T1(KC - 1)
            for c in range(0, 4):
                ln_norm(c)
            T2(KC - 1)
            for c in range(4, KC):
                ln_norm(c)

            for o in range(KC):
                w1 = next_unit()
                w2 = next_unit()
                srcs = [(w1, 0, xg, xg_bs), (w1, 1024, xg, xg_bs), (w2, 0, cvn, [cvn_b]), (w2, 1024, gh, [gh_b])]
                pss = [[bank() for _ in range(4)] for _ in blocks]
                order = ([(bi, j) for bi in range(len(blocks)) for j in (0, 1, 3)] + [(bi, 2) for bi in range(len(blocks))]) if o == 0 \
                    else [(bi, j) for bi in range(len(blocks)) for j in range(4)]
                for bi, j in order:
                    b0, nb = blocks[bi]
                    pt, pb = pss[bi][j]
                    wu, off, rhs_t, rhs_b = srcs[j]
                    pe_mm(pt[:, :nb], [(wu[0][:, off + kc * 128:off + (kc + 1) * 128], rhs_t[:, kc, b0:b0 + nb]) for kc in range(KC)],
                          [wu[1]] + rhs_b, [pb])
                for bi, (b0, nb) in enumerate(blocks):
                    ps = pss[bi]
                    ta, tab = tmpf()
                    tr_, trb = tmpf()
                    a_act(ta[:, :nb], ps[0][0][:, :nb], AF.Sigmoid, [ps[0][1]], [tab])
                    a_act(tr_[:, :nb], ps[1][0][:, :nb], AF.Sigmoid, [ps[1][1]], [trb])
                    v_tt(ta[:, :nb], ps[2][0][:, :nb], ta[:, :nb], ALU.mult, [ps[2][1], tab], [tab])
                    v_tt(tr_[:, :nb], ps[3][0][:, :nb], tr_[:, :nb], ALU.mult, [ps[3][1], trb], [trb])
                    v_tt(mg[:, o, b0:b0 + nb], ta[:, :nb], tr_[:, :nb], ALU.add, [tab, trb], [mg_b])

            sbk = take_banks(len(blocks))
            pending = []
            for oo in range(4):
                wu = next_unit()
                for o in (2 * oo, 2 * oo + 1):
                    off = (o % 2) * 1024
                    for bi, (b0, nb) in enumerate(blocks):
                        bt, bb = bank()
                        pe_mm(bt[:, :nb], [(wu[0][:, off + kc * 128:off + (kc + 1) * 128], mg[:, kc, b0:b0 + nb]) for kc in range(KC)],
                              [wu[1], mg_b], [bb])
                        for fn_ in pending:
                            fn_()
                        pending = []
                        a_act(ffo[:, o, b0:b0 + nb], bt[:, :nb], AF.Copy, [bb], [ffo_bs[o]])
                        q, qb = tmpb()
                        a_act(q[:, :nb], bt[:, :nb], AF.Square, [bb], [qb])
                        pending.append(lambda bi=bi, nb=nb, q=q, qb=qb, o=o: stats_mm(sbk[bi], nb, q[:, :nb], qb, o == 0, o == KC - 1))
            for fn_ in pending:
                fn_()
            postnorm(TG, blocks, sbk, R_GMPOST, 1.0)
            give_banks(sbk)

        def group_cfg(gi):
            p0, TGp, has_s = GROUPS[gi]
            TG = TGp + (SS if has_s else 0)
            h = TGp // 2
            return p0, TGp, has_s, TG, [(0, h), (h, TG - h)]

        def load_tile(src, r0, c0, nt):
            st_t, st_b, st_ds = rr("stg", stage)
            dma(SP, st_ds, st_t[:nt, :], src[r0:r0 + nt, :], writes=[st_b])
            for half in range(2):
                bt, bb = bank()
                for q in range(4):
                    kc = half * 4 + q
                    pe_tr(bt[:, q * 128:q * 128 + nt], st_t[:nt, kc * 128:(kc + 1) * 128], ident[:nt, :nt], [st_b, const_b], [bb])
                a_act(x[:, half * 4:half * 4 + 4, c0:c0 + nt], bt[:, :].rearrange("p (q n) -> p q n", q=4)[:, :, :nt], AF.Copy, [bb], x_bs[half * 4:half * 4 + 4])
            cq_t, cq_b = cbqs[cnt["ldt"] % 2]
            cnt["ldt"] += 1
            q3 = cq_t[:, :, :].rearrange("p a b -> p (a b)")[:, 0:KC * 128].rearrange("p (k n) -> p k n", k=KC)[:, :, :nt]
            a_act(q3, x[:, :, c0:c0 + nt], AF.Square, x_bs, [cq_b])
            for fn_ in load_pend:
                fn_()
            load_pend.clear()

            def rest(c0=c0, nt=nt, q3=q3, cq_b=cq_b):
                sbank = bank()
                for kc in range(KC):
                    stats_mm(sbank, nt, q3[:, kc, :], cq_b, kc == 0, kc == KC - 1)
                rstd_from(sbank, c0, nt, 1.0)
                for c in range(KC):
                    v_stt(xg[:, c, c0:c0 + nt], x[:, c, c0:c0 + nt], prm(c, R_G1PRE), rstd[:, c0:c0 + nt], ALU.mult, ALU.mult,
                          [x_bs[c], P_b, rstd_b], [xg_bs[c]])
            load_pend.append(rest)

        def flush_load():
            for fn_ in load_pend:
                fn_()
            load_pend.clear()

        def store_tile(dst, r0, c0, nt):
            st_t, st_b, st_ds = rr("stg", stage)
            for half in range(2):
                bt, bb = bank()
                for q in range(4):
                    kc = half * 4 + q
                    pe_tr(bt[:nt, q * 128:(q + 1) * 128], ffo[:, kc, c0:c0 + nt], ident[:, :], [ffo_bs[kc], const_b], [bb])
                a_act(st_t[:nt, half * 512:(half + 1) * 512], bt[:nt, :], AF.Copy, [bb], [st_b])
            dma(ACT, st_ds, dst[r0:r0 + nt, :], st_t[:nt, :], reads=[st_b])

        def tiles_of(gi, dp, ds_):
            p0, TGp, has_s, TG, blocks = group_cfg(gi)
            tl = [(dp, p0 + t0, t0, min(128, TGp - t0)) for t0 in range(0, TGp, 128)]
            if has_s:
                tl.append((ds_, 0, TGp, SS))
            return tl

        for tl in tiles_of(0, xp_d, xs_d):
            load_tile(*tl)
        flush_load()
        for gi in range(len(GROUPS)):
            p0, TGp, has_s, TG, blocks = group_cfg(gi)
            last = gi == len(GROUPS) - 1
            ffn(TG, blocks, R_G1PRE, R_G1POST, do_prenorm=False)
            mixer(TG, TGp, has_s, blocks, last)
            ffn(TG, blocks, R_G2PRE, R_G2POST)
            prenorm(TG, blocks, R_GFIN, ffo, ffo_bs)
            outs = tiles_of(gi, yp_d, ys_d)
            ins = tiles_of(gi + 1, xp_d, xs_d) if not last else []
            for i in range(max(len(outs), len(ins))):
                if i < len(outs):
                    store_tile(*outs[i])
                if i < len(ins):
                    load_tile(*ins[i])
            flush_load()

        st_t, st_b, st_ds = rr("stg", stage)
        for half in range(2):
            bt, bb = bank()
            for q in range(4):
                kc = half * 4 + q
                pe_tr(bt[:NSO, q * 128:(q + 1) * 128], SO[:, kc, :], ident[:, :], [SO_b, const_b], [bb])
            a_act(st_t[:NSO, half * 512:(half + 1) * 512], bt[:NSO, :], AF.Copy, [bb], [st_b])
        dma(SP, st_ds, so_d, st_t[:NSO, :], reads=[st_b])
        for (_, _, ds) in stage:
            nc.sync.wait_ge(ds.sem, ds.count)
        for ds in dbg_dss:
            if ds.count:
                nc.sync.wait_ge(ds.sem, ds.count)
    return nc


_CACHE = {}


def kernel(x_prompt, x_sample, state_conv, state_rconv, state_h,
           g_ffn1_pre, g_ffn1_post, w_ffn1_in, w_ffn1_out,
           g_mix_pre, g_mix_post, w_in,
           w_dw, b_dw, ln_g, ln_b, w_conv_out,
           w_rconv, b_rconv, w_rg_a, b_rg_a, w_rg_x, b_rg_x, lam, w_rnn_out,
           w_out,
           g_ffn2_pre, g_ffn2_post, w_ffn2_in, w_ffn2_out, g_final):
    f = lambda a: np.ascontiguousarray(np.asarray(a, dtype=np.float32))
    if "nc" not in _CACHE:
        _CACHE["nc"] = build_program()
    nc = _CACHE["nc"]
    vec_rows = [g_ffn1_pre, g_ffn1_post, g_mix_pre, g_mix_post, b_dw, ln_g, ln_b, b_rconv, b_rg_a, b_rg_x, lam,
                g_ffn2_pre, g_ffn2_post, g_final]
    common = np.concatenate([f(v)[0][None, :] for v in vec_rows] + [f(w_dw)[0], f(w_rconv)[0]], axis=0)
    shared = {
        "ident": np.eye(128, dtype=np.float32),
        "w1i": f(w_ffn1_in)[0], "w1o": f(w_ffn1_out)[0], "wi": f(w_in)[0],
        "wco": f(w_conv_out)[0], "wro": f(w_rnn_out)[0], "wo": f(w_out)[0],
        "wga": f(w_rg_a)[0], "wgx": f(w_rg_x)[0],
        "w2i": f(w_ffn2_in)[0], "w2o": f(w_ffn2_out)[0],
    }
    xp, xs = f(x_prompt), f(x_sample)
    sc, sr, sh = f(state_conv)[0], f(state_rconv)[0], f(state_h)[0]
    in_maps = []
    for i in range(NCORES):
        rows = np.ascontiguousarray(np.concatenate([common, sc[i], sr[i], sh[i][None, :]], axis=0))
        m = dict(shared)
        m.update({"xp": xp[i], "xs": xs[i], "rows": rows})
        in_maps.append(m)
    res = run_bass_kernel_spmd(nc, in_maps, core_ids=list(range(NCORES)))
    r = res.results
    yp = np.stack([r[i]["yp"] for i in range(NCORES)])
    ys = np.stack([r[i]["ys"] for i in range(NCORES)])
    so = np.stack([r[i]["so"] for i in range(NCORES)])
    return (yp.astype(np.float32), ys.astype(np.float32),
            so[None, :, O_CP:O_CP + 30, :], so[None, :, O_RP:O_RP + 3, :], so[None, :, O_HP, :],
            so[None, :, O_CS:O_CS + 30, :], so[None, :, O_RS:O_RS + 3, :], so[None, :, O_HS, :])
```

```python
import os
from contextlib import ExitStack

import numpy as np
import concourse.bass as bass
import concourse.mybir as mybir
from concourse.bass_utils import run_bass_kernel_spmd

F32 = mybir.dt.float32
BF16 = mybir.dt.bfloat16
AF = mybir.ActivationFunctionType
ALU = mybir.AluOpType

D = 1024
KC = 8
DFF = 2816
FC = 22
S = 2048
SS = 32
CK = 31
RK = 4
EPS = 1e-6
NCORES = 8

R_G1PRE, R_G1POST, R_GMPRE, R_GMPOST, R_BDW, R_LNG, R_LNB, R_BRC, R_BA, R_BX, R_LAM, R_G2PRE, R_G2POST, R_GFIN = range(14)
R_WDW = 14
R_WRC = R_WDW + CK
R_SCONV = R_WRC + RK
R_SRC = R_SCONV + 30
R_SH = R_SRC + 3
NR = R_SH + 1
O_CP, O_RP, O_HP, O_CS, O_RS, O_HS = 0, 30, 33, 34, 64, 67
NSO = 68

GROUPS = [(0, 704, False), (704, 704, False), (1408, 640, True)]
TGM = 704
U_S = 30 + TGM
UW = U_S + 30 + SS
RX_S = 3 + TGM
RXW = RX_S + 3 + SS
NSLOT = 6


class Eng:
    def __init__(self, name, h, sem):
        self.name, self.h, self.sem = name, h, sem
        self.count = 0
        self.seen = {}
        self.strict = name in ("act", "dve", "pool")


class DSem:
    def __init__(self, name, sem):
        self.name, self.sem = name, sem
        self.count = 0


class Buf:
    def __init__(self, name):
        self.name = name
        self.w = {}
        self.r = {}


def _deps(eng, reads, writes):
    deps = {}
    for b in reads:
        for o, c in b.w.items():
            if c > deps.get(o, 0):
                deps[o] = c
    for b in writes:
        for o, c in b.w.items():
            if (o is not eng or eng.strict) and c > deps.get(o, 0):
                deps[o] = c
        for o, c in b.r.items():
            if (o is not eng or eng.strict) and c > deps.get(o, 0):
                deps[o] = c
    return deps


def _wait(eng, deps):
    for o, c in deps.items():
        if c <= eng.seen.get(o, 0):
            continue
        eng.h.wait_ge(o.sem, c)
        eng.seen[o] = c


def op(eng, fn, reads=(), writes=()):
    _wait(eng, _deps(eng, reads, writes))
    inst = fn()
    eng.count += 1
    inst.then_inc(eng.sem, 1)
    for b in reads:
        b.r[eng] = eng.count
    for b in writes:
        b.w[eng] = eng.count


def dma(q, dsem, out_ap, in_ap, reads=(), writes=()):
    deps = {}
    for b in reads:
        for o, c in b.w.items():
            if c > deps.get(o, 0):
                deps[o] = c
    for b in writes:
        for o, c in b.w.items():
            if o is not q and o is not dsem and c > deps.get(o, 0):
                deps[o] = c
        for o, c in b.r.items():
            if o is not q and c > deps.get(o, 0):
                deps[o] = c
    _wait(q, deps)
    q.h.dma_start(out=out_ap, in_=in_ap).then_inc(dsem.sem, 16)
    dsem.count += 16
    for b in reads:
        b.r[dsem] = dsem.count
    for b in writes:
        b.w[dsem] = dsem.count


def build_program(dbg=False):
    nc = bass.Bass("TRN2", target_bir_lowering=False)

    def din(name, shape):
        return nc.dram_tensor(name, shape, F32, kind="ExternalInput").ap()

    def dout(name, shape):
        return nc.dram_tensor(name, shape, F32, kind="ExternalOutput").ap()

    xp_d = din("xp", [S, D])
    xs_d = din("xs", [SS, D])
    rows_d = din("rows", [NR, D])
    ident_d = din("ident", [128, 128])
    w1i_d = din("w1i", [D, 2 * DFF])
    w1o_d = din("w1o", [DFF, D])
    wi_d = din("wi", [D, 6 * D])
    wco_d = din("wco", [D, D])
    wro_d = din("wro", [D, D])
    wo_d = din("wo", [D, D])
    wga_d = din("wga", [8, 128, 128])
    wgx_d = din("wgx", [8, 128, 128])
    w2i_d = din("w2i", [D, 2 * DFF])
    w2o_d = din("w2o", [DFF, D])
    yp_d = dout("yp", [S, D])
    ys_d = dout("ys", [SS, D])
    so_d = dout("so", [NSO, D])
    dbg_d = dout("dbg", [16, 128, TGM]) if dbg else None

    es = ExitStack()
    with es:
        def sb(name, shape, dt):
            return es.enter_context(nc.sbuf_tensor(name, shape, dt))

        def sem(name):
            return es.enter_context(nc.semaphore(name))

        PE = Eng("pe", nc.tensor, sem("s_pe"))
        ACT = Eng("act", nc.scalar, sem("s_act"))
        DVE = Eng("dve", nc.vector, sem("s_dve"))
        POOL = Eng("pool", nc.gpsimd, sem("s_pool"))
        SP = Eng("sp", nc.sync, sem("s_sp"))

        x = sb("x", [128, KC, TGM], F32)
        xg = sb("xg", [128, KC, TGM], BF16)
        mg = sb("mg", [128, KC, TGM], BF16)
        ffo = sb("ffo", [128, KC, TGM], F32)
        act = sb("act", [128, FC, TGM], BF16)
        gh = act[:, 0:8, :]
        cvn = act[:, 8:16, :]
        mg_b, act_b, gh_b, cvn_b = Buf("mg"), Buf("act"), Buf("gh"), Buf("cvn")
        xg_bs = [Buf(f"xg{c}") for c in range(KC)]
        x_bs = [Buf(f"x{c}") for c in range(KC)]
        ffo_bs = [Buf(f"ffo{c}") for c in range(KC)]
        rstd = sb("rstd", [128, TGM], F32)
        mean = sb("mean", [128, TGM], F32)
        rstd_b, mean_b = Buf("rstd"), Buf("mean")
        P = sb("P", [128, KC, NR], F32)
        P_b = Buf("P")
        c1 = sb("c1", [128, KC], F32)
        c1_b = Buf("c1")
        c1h = sb("c1h", [128, KC], F32)
        eps_t = sb("eps_t", [128, 1], F32)
        q_t = sb("q_t", [128, 1], F32)
        bh = sb("bh", [128, 2 * KC], F32)
        SO = sb("SO", [128, KC, NSO], F32)
        SO_b = Buf("SO")
        uh = sb("uh", [128, KC, 30], BF16)
        rh = sb("rh", [128, KC, 3], F32)
        hprev = sb("hprev", [128, KC], F32)
        uh_b, rh_b, hprev_b = Buf("uh"), Buf("rh"), Buf("hprev")
        ident = sb("identf", [128, 128], F32)
        identb = sb("identb", [128, 128], BF16)
        ones = sb("ones", [128, 128], BF16)
        const_b = Buf("const")
        NTF, NTB = 7, 4
        tf = [(sb(f"tf{i}", [128, TGM], F32), Buf(f"tf{i}")) for i in range(NTF)]
        tb = [(sb(f"tb{i}", [128, TGM], BF16), Buf(f"tb{i}")) for i in range(NTB)]
        ut = [(sb(f"ut{i}", [128, UW], BF16), Buf(f"ut{i}")) for i in range(KC)]
        rxi = [(sb(f"rxi{i}", [128, RXW], F32), Buf(f"rxi{i}")) for i in range(1)]
        cbqs = [(sb(f"cbq{i}", [128, 2, TGM], BF16), Buf(f"cbq{i}")) for i in range(2)]
        Dg = [(sb(f"Dg{i}", [128, CK, 128], BF16), Buf(f"Dg{i}")) for i in range(2)]
        stage = [(sb(f"stg{i}", [128, D], F32), Buf(f"stg{i}"), DSem(f"stg{i}", sem(f"d_stg{i}"))) for i in range(2)]
        ring = [(sb(f"wr{i}", [128, 2048], BF16), Buf(f"wr{i}"), DSem(f"wr{i}", sem(f"d_wr{i}"))) for i in range(NSLOT)]
        banks = [(es.enter_context(nc.psum_tensor(f"bk{i}", [128, 512], F32)), Buf(f"bk{i}")) for i in range(8)]
        misc_ds = DSem("misc", sem("d_misc"))
        dbg_dss = [DSem(f"dbg{i}", sem(f"d_dbg{i}")) for i in range(16)] if dbg else []

        def dump(i, ap, buf, n=TGM):
            if dbg:
                dma(SP, dbg_dss[i], dbg_d[i, :, :n], ap, reads=[buf])

        cnt = {"tf": 0, "tb": 0, "ut": 0, "rxi": 0, "dg": 0, "stg": 0, "bank": 0, "ldt": 0}
        load_pend = []
        pool = list(range(8))

        def rr(key, lst):
            i = cnt[key] % len(lst)
            cnt[key] += 1
            return lst[i]

        def tmpf():
            return rr("tf", tf)

        def tmpb():
            return rr("tb", tb)

        def bank():
            i = pool[cnt["bank"] % len(pool)]
            cnt["bank"] += 1
            return banks[i]

        def take_banks(n):
            out = [banks[pool.pop()] for _ in range(n)]
            return out

        def give_banks(bs):
            for b in bs:
                pool.append([i for i in range(8) if banks[i] is b][0])

        def pe_mm(out_ap, pairs, reads, writes, start=True, stop=True):
            def fn():
                n = len(pairs)
                ins = None
                for i, (l, r) in enumerate(pairs):
                    ins = nc.tensor.matmul(out_ap, lhsT=l, rhs=r, start=(start and i == 0), stop=(stop and i == n - 1))
                return ins
            op(PE, fn, reads, writes)

        def pe_tr(out_ap, in_ap, idn, reads, writes):
            op(PE, lambda: nc.tensor.transpose(out=out_ap, in_=in_ap, identity=idn), reads, writes)

        def a_act(out_ap, in_ap, func, reads, writes, scale=None, bias=None):
            kw = {}
            if scale is not None:
                kw["scale"] = scale
            if bias is not None:
                kw["bias"] = bias
            op(ACT, lambda: nc.scalar.activation(out=out_ap, in_=in_ap, func=func, **kw), reads, writes)

        def v_tt(out_ap, in0, in1, alu, reads, writes):
            op(DVE, lambda: nc.vector.tensor_tensor(out=out_ap, in0=in0, in1=in1, op=alu), reads, writes)

        def v_stt(out_ap, in0, scalar, in1, op0, op1, reads, writes):
            op(DVE, lambda: nc.vector.scalar_tensor_tensor(out=out_ap, in0=in0, scalar=scalar, in1=in1, op0=op0, op1=op1), reads, writes)

        def v_ts(out_ap, in0, s1, s2, op0, op1, reads, writes):
            op(DVE, lambda: nc.vector.tensor_scalar(out=out_ap, in0=in0, scalar1=s1, scalar2=s2, op0=op0, op1=op1), reads, writes)

        def v_copy(out_ap, in_ap, reads, writes):
            op(DVE, lambda: nc.vector.tensor_copy(out=out_ap, in_=in_ap), reads, writes)

        def v_recip(out_ap, in_ap, reads, writes):
            op(DVE, lambda: nc.vector.reciprocal(out=out_ap, in_=in_ap), reads, writes)

        def prm(c, r):
            return P[:, c, r:r + 1]

        def wv(w, kc):
            return w.rearrange("(kc p) n -> p kc n", p=128)

        units = []

        def ffn_units(wi_, wo_):
            wiv = wv(wi_, KC)
            wov = wv(wo_, FC)
            for m in range(FC):
                units.append([(0, KC, wiv[:, :, m * 128:(m + 1) * 128]),
                              (1024, KC, wiv[:, :, DFF + m * 128:DFF + (m + 1) * 128])])
            for o in range(KC):
                for h in range(2):
                    units.append([(0, 11, wov[:, 11 * h:11 * h + 11, o * 128:(o + 1) * 128])])

        def mixer_units():
            wiv = wv(wi_d, KC)
            for c in range(KC):
                units.append([(0, KC, wiv[:, :, c * 128:(c + 1) * 128]),
                              (1024, KC, wiv[:, :, D + c * 128:D + (c + 1) * 128])])
            for c in range(KC):
                units.append([(0, KC, wiv[:, :, 2 * D + c * 128:2 * D + (c + 1) * 128]),
                              (1024, KC, wiv[:, :, 3 * D + c * 128:3 * D + (c + 1) * 128])])
            wcov, wrov, wov = wv(wco_d, KC), wv(wro_d, KC), wv(wo_d, KC)
            for o in range(KC):
                units.append([(0, KC, wiv[:, :, 4 * D + o * 128:4 * D + (o + 1) * 128]),
                              (1024, KC, wiv[:, :, 5 * D + o * 128:5 * D + (o + 1) * 128])])
                units.append([(0, KC, wcov[:, :, o * 128:(o + 1) * 128]),
                              (1024, KC, wrov[:, :, o * 128:(o + 1) * 128])])
            for oo in range(4):
                units.append([(0, KC, wov[:, :, (2 * oo) * 128:(2 * oo + 1) * 128]),
                              (1024, KC, wov[:, :, (2 * oo + 1) * 128:(2 * oo + 2) * 128])])

        for _ in GROUPS:
            ffn_units(w1i_d, w1o_d)
            mixer_units()
            ffn_units(w2i_d, w2o_d)

        wstate = {"issued": 0, "used": 0}

        def issue_unit():
            u = wstate["issued"]
            if u >= len(units):
                return
            t, b, ds = ring[u % NSLOT]
            for (off, a, src) in units[u]:
                dst = t[:, off:off + a * 128].rearrange("p (a b) -> p a b", a=a)
                dma(POOL, ds, dst, src, writes=[b])
            wstate["issued"] += 1

        def next_unit():
            u = wstate["used"]
            while wstate["issued"] < min(len(units), u + NSLOT - 1):
                issue_unit()
            wstate["used"] += 1
            return ring[u % NSLOT]

        gw = sb("gw", [128, 2048], BF16)
        gwb = Buf("gw")
        gw_ds = DSem("gw", sem("d_gw"))
        dma(POOL, gw_ds, gw[:, 0:1024].rearrange("p (a b) -> p a b", a=8), wga_d.rearrange("n h k -> h n k"), writes=[gwb])
        dma(POOL, gw_ds, gw[:, 1024:2048].rearrange("p (a b) -> p a b", a=8), wgx_d.rearrange("n h k -> h n k"), writes=[gwb])
        dma(SP, misc_ds, ident[:], ident_d, writes=[const_b])
        st_t, st_b, st_ds = stage[0]
        dma(SP, st_ds, st_t[:NR, :], rows_d, writes=[st_b])
        op(DVE, lambda: nc.vector.memset(ones[:], 1.0), writes=[const_b])
        v_copy(identb[:], ident[:], [const_b], [const_b])
        op(DVE, lambda: nc.vector.memset(uh[:], 0.0), writes=[uh_b])
        op(DVE, lambda: nc.vector.memset(rh[:], 0.0), writes=[rh_b])
        op(DVE, lambda: nc.vector.memset(hprev[:], 0.0), writes=[hprev_b])
        for half in range(2):
            bt, bb = bank()
            for q in range(4):
                kc = half * 4 + q
                pe_tr(bt[:, q * 128:q * 128 + NR], st_t[:NR, kc * 128:(kc + 1) * 128], ident[:NR, :NR], [st_b, const_b], [bb])
            a_act(P[:, half * 4:half * 4 + 4, :], bt[:, :].rearrange("p (q n) -> p q n", q=4)[:, :, :NR], AF.Copy, [bb], [P_b])
        t1, t1b = tmpf()
        a_act(t1[:, 0:KC], P[:, :, R_LAM], AF.Exp, [P_b], [t1b], scale=-1.0)
        a_act(t1[:, 8:8 + KC], t1[:, 0:KC], AF.Ln, [t1b], [t1b], bias=1.0)
        v_ts(c1[:, :], t1[:, 8:8 + KC], -8.0, None, ALU.mult, ALU.bypass, [t1b], [c1_b])
        v_ts(c1h[:, :], t1[:, 8:8 + KC], -4.0, None, ALU.mult, ALU.bypass, [t1b], [c1_b])
        op(DVE, lambda: nc.vector.memset(eps_t[:], EPS), writes=[c1_b])
        op(DVE, lambda: nc.vector.memset(q_t[:], 0.25), writes=[c1_b])
        v_ts(bh[:, 0:KC], P[:, :, R_BA], 0.5, None, ALU.mult, ALU.bypass, [P_b], [c1_b])
        v_ts(bh[:, KC:2 * KC], P[:, :, R_BX], 0.5, None, ALU.mult, ALU.bypass, [P_b], [c1_b])

        def stats_mm(sbank, nb, src_ap, src_b, first, last):
            st, sbuf_ = sbank
            pe_mm(st[:, :nb], [(ones[:, :], src_ap)], [const_b, src_b], [sbuf_], start=first, stop=last)

        def rstd_from(sbank, b0, nb, f):
            st, sbuf_ = sbank
            t, tb_ = tmpf()
            a_act(t[:, :nb], st[:, :nb], AF.Ln, [sbuf_, c1_b], [tb_], scale=1.0 / D, bias=eps_t[:, 0:1])
            a_act(rstd[:, b0:b0 + nb], t[:, :nb], AF.Exp, [tb_], [rstd_b], scale=-0.5, bias=float(np.log(f)))

        def prenorm(TG, blocks, grow, out_t, out_b):
            sbk = take_banks(len(blocks))
            for c in range(KC):
                q, qb = tmpb()
                a_act(q[:, :TG], x[:, c, :TG], AF.Square, [x_bs[c]], [qb])
                for bi, (b0, nb) in enumerate(blocks):
                    stats_mm(sbk[bi], nb, q[:, b0:b0 + nb], qb, c == 0, c == KC - 1)
            for bi, (b0, nb) in enumerate(blocks):
                rstd_from(sbk[bi], b0, nb, 1.0)
            give_banks(sbk)
            for c in range(KC):
                v_stt(out_t[:, c, :TG], x[:, c, :TG], prm(c, grow), rstd[:, :TG], ALU.mult, ALU.mult, [x_bs[c], P_b, rstd_b], [out_b[c] if isinstance(out_b, list) else out_b])

        def postnorm(TG, blocks, sbk, grow, f):
            for bi, (b0, nb) in enumerate(blocks):
                rstd_from(sbk[bi], b0, nb, f)
            for c in range(KC):
                v_tt(ffo[:, c, :TG], ffo[:, c, :TG], rstd[:, :TG], ALU.mult, [ffo_bs[c], rstd_b], [ffo_bs[c]])
                v_stt(x[:, c, :TG], ffo[:, c, :TG], prm(c, grow), x[:, c, :TG], ALU.mult, ALU.add, [ffo_bs[c], P_b, x_bs[c]], [x_bs[c]])

        def first_unit(wt, wb, blocks):
            res = [(bank(), bank()) for _ in blocks]
            for kc in range(KC):
                for (b0, nb), ((a_t, a_b), (b_t, b_b)) in zip(blocks, res):
                    pe_mm(a_t[:, :nb], [(wt[:, kc * 128:(kc + 1) * 128], xg[:, kc, b0:b0 + nb])], [wb, xg_bs[kc]], [a_b],
                          start=(kc == 0), stop=(kc == KC - 1))
                    pe_mm(b_t[:, :nb], [(wt[:, 1024 + kc * 128:1024 + (kc + 1) * 128], xg[:, kc, b0:b0 + nb])], [wb, xg_bs[kc]], [b_b],
                          start=(kc == 0), stop=(kc == KC - 1))
            return res

        def ffn(TG, blocks, grow_pre, grow_post, do_prenorm=True, bg=None):
            if do_prenorm:
                prenorm(TG, blocks, grow_pre, xg, xg_bs)
            for m in range(FC):
                wt, wb, _ = next_unit()
                pre = first_unit(wt, wb, blocks) if m == 0 else None
                for bi, (b0, nb) in enumerate(blocks):
                    if pre is not None:
                        (gt, gb), (upt, upb) = pre[bi]
                    else:
                        gt, gb = bank()
                        upt, upb = bank()
                        pe_mm(gt[:, :nb], [(wt[:, kc * 128:(kc + 1) * 128], xg[:, kc, b0:b0 + nb]) for kc in range(KC)], [wb] + xg_bs, [gb])
                        pe_mm(upt[:, :nb], [(wt[:, 1024 + kc * 128:1024 + (kc + 1) * 128], xg[:, kc, b0:b0 + nb]) for kc in range(KC)], [wb] + xg_bs, [upb])
                    t, tb_ = tmpf()
                    a_act(t[:, :nb], gt[:, :nb], AF.Silu, [gb], [tb_])
                    v_tt(act[:, m, b0:b0 + nb], upt[:, :nb], t[:, :nb], ALU.mult, [upb, tb_], [act_b, gh_b, cvn_b])
                if bg and m >= 1 and m % 2 == 1:
                    bg.pop(0)()
            while bg:
                bg.pop(0)()
            sbk = take_banks(len(blocks))
            pending = []
            for o in range(KC):
                w0 = next_unit()
                w1 = next_unit()
                for bi, (b0, nb) in enumerate(blocks):
                    bt, bb = bank()
                    pairs = []
                    for kc in range(FC):
                        wt = (w0 if kc < 11 else w1)[0]
                        kk = kc % 11
                        pairs.append((wt[:, kk * 128:(kk + 1) * 128], act[:, kc, b0:b0 + nb]))
                    pe_mm(bt[:, :nb], pairs, [w0[1], w1[1], act_b, gh_b, cvn_b], [bb])
                    for fn_ in pending:
                        fn_()
                    pending = []
                    a_act(ffo[:, o, b0:b0 + nb], bt[:, :nb], AF.Copy, [bb], [ffo_bs[o]])
                    q, qb = tmpb()
                    a_act(q[:, :nb], bt[:, :nb], AF.Square, [bb], [qb])
                    pending.append(lambda bi=bi, nb=nb, q=q, qb=qb, o=o: stats_mm(sbk[bi], nb, q[:, :nb], qb, o == 0, o == KC - 1))
            for fn_ in pending:
                fn_()
            postnorm(TG, blocks, sbk, grow_post, 0.5)
            give_banks(sbk)

        def mixer(TG, TGp, has_s, blocks, last):
            prenorm(TG, blocks, R_GMPRE, xg, xg_bs)

            def split(b0, nb):
                npr = min(nb, TGp - b0)
                return npr, nb - npr

            for c in range(KC):
                wt, wb, _ = next_unit()
                u, ub = ut[c]
                v_copy(u[:, 0:30], uh[:, c, :], [uh_b], [ub])
                if has_s:
                    v_copy(u[:, U_S:U_S + 30], P[:, c, R_SCONV:R_SCONV + 30], [P_b], [ub])
                pre = first_unit(wt, wb, blocks) if c == 0 else None
                for bi, (b0, nb) in enumerate(blocks):
                    npr, nsm = split(b0, nb)
                    if pre is not None:
                        (vt, vb), (gt, gb) = pre[bi]
                    else:
                        vt, vb = bank()
                        gt, gb = bank()
                        pe_mm(vt[:, :nb], [(wt[:, kc * 128:(kc + 1) * 128], xg[:, kc, b0:b0 + nb]) for kc in range(KC)], [wb] + xg_bs, [vb])
                        pe_mm(gt[:, :nb], [(wt[:, 1024 + kc * 128:1024 + (kc + 1) * 128], xg[:, kc, b0:b0 + nb]) for kc in range(KC)], [wb] + xg_bs, [gb])
                    t, tb_ = tmpf()
                    a_act(t[:, :nb], gt[:, :nb], AF.Sigmoid, [gb], [tb_])
                    v_tt(u[:, 30 + b0:30 + b0 + npr], vt[:, :npr], t[:, :npr], ALU.mult, [vb, tb_], [ub])
                    if nsm:
                        v_tt(u[:, U_S + 30:U_S + 30 + SS], vt[:, npr:nb], t[:, npr:nb], ALU.mult, [vb, tb_], [ub])
                    if last and bi == len(blocks) - 1:
                        v_tt(SO[:, c, O_CP:O_CP + 30], vt[:, npr - 30:npr], t[:, npr - 30:npr], ALU.mult, [vb, tb_], [SO_b])
                        v_tt(SO[:, c, O_CS:O_CS + 30], vt[:, npr + 2:npr + 32], t[:, npr + 2:npr + 32], ALU.mult, [vb, tb_], [SO_b])
                v_copy(uh[:, c, :], u[:, TGp:TGp + 30], [ub], [uh_b])

            def build_dg(c):
                dg, dgb = Dg[c % 2]
                v_tt(dg[:, :, :], identb[:, :].unsqueeze(1).broadcast_to([128, CK, 128]),
                     P[:, c, R_WDW:R_WDW + CK].unsqueeze(2).broadcast_to([128, CK, 128]), ALU.mult, [const_b, P_b], [dgb])

            build_dg(0)

            spieces = [(q0, min(256, TG - q0)) for q0 in range(0, TG, 256)]
            sPQ = take_banks(len(spieces))
            def ln_stats(c):
                cbq, cbq_b = cbqs[c % 2]
                for pi, (q0, w) in enumerate(spieces):
                    st, stb = sPQ[pi]
                    pe_mm(st[:, 0:2 * w].rearrange("p (a b) -> p a b", a=2), [(ones[:, :], cbq[:, :, q0:q0 + w])], [const_b, cbq_b], [stb],
                          start=(c == 0), stop=(c == KC - 1))

            st8 = [dict() for _ in range(KC)]

            def H1(c):
                d = st8[c]
                wt, wb, _ = next_unit()
                ri, rib = rr("rxi", rxi)
                d["ri"] = (ri, rib)
                v_copy(ri[:, 0:3], rh[:, c, :], [rh_b], [rib])
                if has_s:
                    v_copy(ri[:, RX_S:RX_S + 3], P[:, c, R_SRC:R_SRC + 3], [P_b], [rib])
                rg, rgb = tmpb()
                d["rg"] = (rg, rgb)
                for bi, (b0, nb) in enumerate(blocks):
                    npr, nsm = split(b0, nb)
                    xt_, xb_ = bank()
                    gt, gb = bank()
                    pe_mm(xt_[:, :nb], [(wt[:, kc * 128:(kc + 1) * 128], xg[:, kc, b0:b0 + nb]) for kc in range(KC)], [wb] + xg_bs, [xb_])
                    pe_mm(gt[:, :nb], [(wt[:, 1024 + kc * 128:1024 + (kc + 1) * 128], xg[:, kc, b0:b0 + nb]) for kc in range(KC)], [wb] + xg_bs, [gb])
                    v_copy(ri[:, 3 + b0:3 + b0 + npr], xt_[:, :npr], [xb_], [rib])
                    if nsm:
                        v_copy(ri[:, RX_S + 3:RX_S + 3 + SS], xt_[:, npr:nb], [xb_], [rib])
                    a_act(rg[:, b0:b0 + nb], gt[:, :nb], AF.Gelu_apprx_tanh, [gb], [rgb])

            def H2(c):
                d = st8[c]
                ri, rib = d["ri"]
                v_copy(rh[:, c, :], ri[:, TGp:TGp + 3], [rib], [rh_b])
                if last:
                    v_copy(SO[:, c, O_RP:O_RP + 3], ri[:, TGp:TGp + 3], [rib], [SO_b])
                    v_copy(SO[:, c, O_RS:O_RS + 3], ri[:, RX_S + SS:RX_S + SS + 3], [rib], [SO_b])
                rx, rxb = tmpf()
                d["rx"] = (rx, rxb)
                segs = [(0, 0, TGp)] + ([(RX_S, TGp, SS)] if has_s else [])
                for (src0, dst0, n) in segs:
                    v_ts(rx[:, dst0:dst0 + n], ri[:, src0:src0 + n], prm(c, R_WRC), prm(c, R_BRC), ALU.mult, ALU.add, [rib, P_b], [rxb])
                    for k in range(1, RK):
                        v_stt(rx[:, dst0:dst0 + n], ri[:, src0 + k:src0 + k + n], prm(c, R_WRC + k), rx[:, dst0:dst0 + n], ALU.mult, ALU.add, [rib, P_b, rxb], [rxb])
                rxq, rxqb = tmpb()
                d["rxq"] = (rxq, rxqb)
                a_act(rxq[:, :TG], rx[:, :TG], AF.Copy, [rxb], [rxqb])

            def H3(c):
                u, ub = ut[c]
                dg, dgb = Dg[c % 2]
                for bi, (b0, nb) in enumerate(blocks):
                    npr, nsm = split(b0, nb)
                    bt, bb = bank()
                    pe_mm(bt[:, :npr], [(dg[:, k, :], u[:, b0 + k:b0 + k + npr]) for k in range(CK)], [dgb, ub], [bb])
                    if nsm:
                        pe_mm(bt[:, npr:nb], [(dg[:, k, :], u[:, U_S + k:U_S + k + SS]) for k in range(CK)], [dgb, ub], [bb])
                    a_act(ffo[:, c, b0:b0 + nb], bt[:, :nb], AF.Identity, [bb, P_b], [ffo_bs[c]], bias=prm(c, R_BDW))
                if c + 2 < KC:
                    build_dg(c + 2)

            def H4(c):
                d = st8[c]
                rxq, rxqb = d["rxq"]
                rt, rtb = tmpf()
                it, itb = tmpf()
                d["rt"], d["it"] = (rt, rtb), (it, itb)
                for bi, (b0, nb) in enumerate(blocks):
                    rp, rpb = bank()
                    ip, ipb = bank()
                    pe_mm(rp[:, :nb], [(gw[:, c * 128:(c + 1) * 128], rxq[:, b0:b0 + nb])], [gwb, rxqb], [rpb])
                    pe_mm(ip[:, :nb], [(gw[:, 1024 + c * 128:1024 + (c + 1) * 128], rxq[:, b0:b0 + nb])], [gwb, rxqb], [ipb])
                    a_act(rt[:, b0:b0 + nb], rp[:, :nb], AF.Tanh, [rpb, c1_b], [rtb], scale=0.5, bias=bh[:, c:c + 1])
                    a_act(it[:, b0:b0 + nb], ip[:, :nb], AF.Tanh, [ipb, c1_b], [itb], scale=0.5, bias=bh[:, KC + c:KC + c + 1])
                if c >= 1:
                    ln_stats(c - 1)
                cbq, cbq_b = cbqs[c % 2]
                a_act(cbq[:, 0, :TG], ffo[:, c, :TG], AF.Copy, [ffo_bs[c]], [cbq_b])
                a_act(cbq[:, 1, :TG], ffo[:, c, :TG], AF.Square, [ffo_bs[c]], [cbq_b])

            def T1(c):
                d = st8[c]
                rt, rtb = d["rt"]
                at, atb = tmpf()
                d["at"] = (at, atb)
                a_act(at[:, :TG], rt[:, :TG], AF.Exp, [rtb, c1_b], [atb], scale=c1h[:, c:c + 1], bias=c1h[:, c:c + 1])
                a_act(rt[:, :TG], at[:, :TG], AF.Square, [atb], [rtb])
                a_act(rt[:, :TG], rt[:, :TG], AF.Sqrt, [rtb, c1_b], [rtb], scale=-0.25, bias=q_t[:, 0:1])

            def T2(c):
                d = st8[c]
                rt, rtb = d["rt"]
                it, itb = d["it"]
                at, atb = d["at"]
                rx, rxb = d["rx"]
                rg, rgb = d["rg"]
                ri, rib = d["ri"]
                v_stt(it[:, :TG], it[:, :TG], 1.0, rx[:, :TG], ALU.add, ALU.mult, [itb, rxb], [itb])
                v_tt(it[:, :TG], it[:, :TG], rt[:, :TG], ALU.mult, [itb, rtb], [itb])
                hs, hsb = tmpf()
                op(DVE, lambda: nc.vector.tensor_tensor_scan(out=hs[:, :TGp], data0=at[:, :TGp], data1=it[:, :TGp],
                                                             initial=hprev[:, c:c + 1], op0=ALU.mult, op1=ALU.add),
                   [atb, itb, hprev_b], [hsb])
                if has_s:
                    op(DVE, lambda: nc.vector.tensor_tensor_scan(out=hs[:, TGp:TG], data0=at[:, TGp:TG], data1=it[:, TGp:TG],
                                                                 initial=prm(c, R_SH), op0=ALU.mult, op1=ALU.add),
                       [atb, itb, P_b], [hsb])
                if dbg and c == int(os.environ.get('DBGC', '0')) and not has_s and TGp == 704 and cnt.get("dumped") is None:
                    cnt["dumped"] = 1
                    dump(0, rx[:, :], rxb); dump(1, rt[:, :], rtb); dump(2, it[:, :], itb); dump(3, at[:, :], atb); dump(4, hs[:, :], hsb)
                    dump(5, c1[:, :], c1_b, KC); dump(7, ffo[:, c, :], ffo_bs[c]); dump(8, x[:, c, :], x_bs[c])
                v_copy(hprev[:, c:c + 1], hs[:, TGp - 1:TGp], [hsb], [hprev_b])
                if last:
                    v_copy(SO[:, c, O_HP:O_HP + 1], hs[:, TGp - 1:TGp], [hsb], [SO_b])
                    v_copy(SO[:, c, O_HS:O_HS + 1], hs[:, TG - 1:TG], [hsb], [SO_b])
                v_tt(gh[:, c, :TG], hs[:, :TG], rg[:, :TG], ALU.mult, [hsb, rgb], [gh_b])
                st8[c].clear()

            build_dg(1)
            H1(0); H2(0); H3(0); H4(0)
            for c in range(KC - 1):
                H1(c + 1)
                T1(c)
                H2(c + 1)
                T2(c)
                H3(c + 1)
                H4(c + 1)
            ln_stats(KC - 1)

            for pi, (q0, w) in enumerate(spieces):
                st, stb = sPQ[pi]
                rs = rstd[:, q0:q0 + w]
                a_act(mean[:, q0:q0 + w], st[:, 0:w], AF.Copy, [stb], [mean_b], scale=1.0 / D)
                v_tt(rs, mean[:, q0:q0 + w], mean[:, q0:q0 + w], ALU.mult, [mean_b], [rstd_b])
                v_stt(rs, st[:, w:2 * w], 1.0 / D, rs, ALU.mult, ALU.subtract, [stb, rstd_b], [rstd_b])
                a_act(rs, rs, AF.Ln, [rstd_b, c1_b], [rstd_b], bias=eps_t[:, 0:1])
                a_act(rs, rs, AF.Exp, [rstd_b], [rstd_b], scale=-0.5)
            give_banks(sPQ)
            if dbg and cnt.get("dumped3") is None:
                cnt["dumped3"] = 1
                dump(12, mean[:, :], mean_b); dump(13, rstd[:, :], rstd_b)

            def ln_norm(c):
                v_tt(ffo[:, c, :TG], ffo[:, c, :TG], mean[:, :TG], ALU.subtract, [ffo_bs[c], mean_b], [ffo_bs[c]])
                v_tt(ffo[:, c, :TG], ffo[:, c, :TG], rstd[:, :TG], ALU.mult, [ffo_bs[c], rstd_b], [ffo_bs[c]])
                a_act(cvn[:, c, :TG], ffo[:, c, :TG], AF.Silu, [ffo_bs[c], P_b], [cvn_b], scale=prm(c, R_LNG), bias=prm(c, R_LNB))

            T1(KC - 1)
            for c in range(0, 4):
                ln_norm(c)
            T2(KC - 1)
            for c in range(4, KC):
                ln_norm(c)

            for o in range(KC):
                w1 = next_unit()
                w2 = next_unit()
                srcs = [(w1, 0, xg, xg_bs), (w1, 1024, xg, xg_bs), (w2, 0, cvn, [cvn_b]), (w2, 1024, gh, [gh_b])]
                pss = [[bank() for _ in range(4)] for _ in blocks]
                order = ([(bi, j) for bi in range(len(blocks)) for j in (0, 1, 3)] + [(bi, 2) for bi in range(len(blocks))]) if o == 0 \
                    else [(bi, j) for bi in range(len(blocks)) for j in range(4)]
                for bi, j in order:
                    b0, nb = blocks[bi]
                    pt, pb = pss[bi][j]
                    wu, off, rhs_t, rhs_b = srcs[j]
                    pe_mm(pt[:, :nb], [(wu[0][:, off + kc * 128:off + (kc + 1) * 128], rhs_t[:, kc, b0:b0 + nb]) for kc in range(KC)],
                          [wu[1]] + rhs_b, [pb])
                for bi, (b0, nb) in enumerate(blocks):
                    ps = pss[bi]
                    ta, tab = tmpf()
                    tr_, trb = tmpf()
                    a_act(ta[:, :nb], ps[0][0][:, :nb], AF.Sigmoid, [ps[0][1]], [tab])
                    a_act(tr_[:, :nb], ps[1][0][:, :nb], AF.Sigmoid, [ps[1][1]], [trb])
                    v_tt(ta[:, :nb], ps[2][0][:, :nb], ta[:, :nb], ALU.mult, [ps[2][1], tab], [tab])
                    v_tt(tr_[:, :nb], ps[3][0][:, :nb], tr_[:, :nb], ALU.mult, [ps[3][1], trb], [trb])
                    v_tt(mg[:, o, b0:b0 + nb], ta[:, :nb], tr_[:, :nb], ALU.add, [tab, trb], [mg_b])

            sbk = take_banks(len(blocks))
            pending = []
            for oo in range(4):
                wu = next_unit()
                for o in (2 * oo, 2 * oo + 1):
                    off = (o % 2) * 1024
                    for bi, (b0, nb) in enumerate(blocks):
                        bt, bb = bank()
                        pe_mm(bt[:, :nb], [(wu[0][:, off + kc * 128:off + (kc + 1) * 128], mg[:, kc, b0:b0 + nb]) for kc in range(KC)],
                              [wu[1], mg_b], [bb])
                        for fn_ in pending:
                            fn_()
                        pending = []
                        a_act(ffo[:, o, b0:b0 + nb], bt[:, :nb], AF.Copy, [bb], [ffo_bs[o]])
                        q, qb = tmpb()
                        a_act(q[:, :nb], bt[:, :nb], AF.Square, [bb], [qb])
                        pending.append(lambda bi=bi, nb=nb, q=q, qb=qb, o=o: stats_mm(sbk[bi], nb, q[:, :nb], qb, o == 0, o == KC - 1))
            for fn_ in pending:
                fn_()
            postnorm(TG, blocks, sbk, R_GMPOST, 1.0)
            give_banks(sbk)

        def group_cfg(gi):
            p0, TGp, has_s = GROUPS[gi]
            TG = TGp + (SS if has_s else 0)
            h = TGp // 2
            return p0, TGp, has_s, TG, [(0, h), (h, TG - h)]

        def load_tile(src, r0, c0, nt):
            st_t, st_b, st_ds = rr("stg", stage)
            dma(SP, st_ds, st_t[:nt, :], src[r0:r0 + nt, :], writes=[st_b])
            for half in range(2):
                bt, bb = bank()
                for q in range(4):
                    kc = half * 4 + q
                    pe_tr(bt[:, q * 128:q * 128 + nt], st_t[:nt, kc * 128:(kc + 1) * 128], ident[:nt, :nt], [st_b, const_b], [bb])
                a_act(x[:, half * 4:half * 4 + 4, c0:c0 + nt], bt[:, :].rearrange("p (q n) -> p q n", q=4)[:, :, :nt], AF.Copy, [bb], x_bs[half * 4:half * 4 + 4])
            cq_t, cq_b = cbqs[cnt["ldt"] % 2]
            cnt["ldt"] += 1
            q3 = cq_t[:, :, :].rearrange("p a b -> p (a b)")[:, 0:KC * 128].rearrange("p (k n) -> p k n", k=KC)[:, :, :nt]
            a_act(q3, x[:, :, c0:c0 + nt], AF.Square, x_bs, [cq_b])
            for fn_ in load_pend:
                fn_()
            load_pend.clear()

            def rest(c0=c0, nt=nt, q3=q3, cq_b=cq_b):
                sbank = bank()
                for kc in range(KC):
                    stats_mm(sbank, nt, q3[:, kc, :], cq_b, kc == 0, kc == KC - 1)
                rstd_from(sbank, c0, nt, 1.0)
                for c in range(KC):
                    v_stt(xg[:, c, c0:c0 + nt], x[:, c, c0:c0 + nt], prm(c, R_G1PRE), rstd[:, c0:c0 + nt], ALU.mult, ALU.mult,
                          [x_bs[c], P_b, rstd_b], [xg_bs[c]])
            load_pend.append(rest)

        def flush_load():
            for fn_ in load_pend:
                fn_()
            load_pend.clear()

        def store_tile(dst, r0, c0, nt):
            st_t, st_b, st_ds = rr("stg", stage)
            for half in range(2):
                bt, bb = bank()
                for q in range(4):
                    kc = half * 4 + q
                    pe_tr(bt[:nt, q * 128:(q + 1) * 128], ffo[:, kc, c0:c0 + nt], ident[:, :], [ffo_bs[kc], const_b], [bb])
                a_act(st_t[:nt, half * 512:(half + 1) * 512], bt[:nt, :], AF.Copy, [bb], [st_b])
            dma(ACT, st_ds, dst[r0:r0 + nt, :], st_t[:nt, :], reads=[st_b])

        def tiles_of(gi, dp, ds_):
            p0, TGp, has_s, TG, blocks = group_cfg(gi)
            tl = [(dp, p0 + t0, t0, min(128, TGp - t0)) for t0 in range(0, TGp, 128)]
            if has_s:
                tl.append((ds_, 0, TGp, SS))
            return tl

        deferred = []
        for tl in tiles_of(0, xp_d, xs_d):
            load_tile(*tl)
        flush_load()
        for gi in range(len(GROUPS)):
            p0, TGp, has_s, TG, blocks = group_cfg(gi)
            last = gi == len(GROUPS) - 1
            ffn(TG, blocks, R_G1PRE, R_G1POST, do_prenorm=False, bg=deferred)
            mixer(TG, TGp, has_s, blocks, last)
            ffn(TG, blocks, R_G2PRE, R_G2POST)
            prenorm(TG, blocks, R_GFIN, ffo, ffo_bs)
            outs = tiles_of(gi, yp_d, ys_d)
            if last:
                for tl in outs:
                    store_tile(*tl)
            else:
                deferred.extend([(lambda tl=tl: store_tile(*tl)) for tl in outs])
                for tl in tiles_of(gi + 1, xp_d, xs_d):
                    load_tile(*tl)
                flush_load()

        st_t, st_b, st_ds = rr("stg", stage)
        for half in range(2):
            bt, bb = bank()
            for q in range(4):
                kc = half * 4 + q
                pe_tr(bt[:NSO, q * 128:(q + 1) * 128], SO[:, kc, :], ident[:, :], [SO_b, const_b], [bb])
            a_act(st_t[:NSO, half * 512:(half + 1) * 512], bt[:NSO, :], AF.Copy, [bb], [st_b])
        dma(SP, st_ds, so_d, st_t[:NSO, :], reads=[st_b])
        for (_, _, ds) in stage:
            nc.sync.wait_ge(ds.sem, ds.count)
        for ds in dbg_dss:
            if ds.count:
                nc.sync.wait_ge(ds.sem, ds.count)
    return nc


_CACHE = {}


def kernel(x_prompt, x_sample, state_conv, state_rconv, state_h,
           g_ffn1_pre, g_ffn1_post, w_ffn1_in, w_ffn1_out,
           g_mix_pre, g_mix_post, w_in,
           w_dw, b_dw, ln_g, ln_b, w_conv_out,
           w_rconv, b_rconv, w_rg_a, b_rg_a, w_rg_x, b_rg_x, lam, w_rnn_out,
           w_out,
           g_ffn2_pre, g_ffn2_post, w_ffn2_in, w_ffn2_out, g_final):
    f = lambda a: np.ascontiguousarray(np.asarray(a, dtype=np.float32))
    if "nc" not in _CACHE:
        _CACHE["nc"] = build_program()
    nc = _CACHE["nc"]
    vec_rows = [g_ffn1_pre, g_ffn1_post, g_mix_pre, g_mix_post, b_dw, ln_g, ln_b, b_rconv, b_rg_a, b_rg_x, lam,
                g_ffn2_pre, g_ffn2_post, g_final]
    common = np.concatenate([f(v)[0][None, :] for v in vec_rows] + [f(w_dw)[0], f(w_rconv)[0]], axis=0)
    shared = {
        "ident": np.eye(128, dtype=np.float32),
        "w1i": f(w_ffn1_in)[0], "w1o": f(w_ffn1_out)[0], "wi": f(w_in)[0],
        "wco": f(w_conv_out)[0], "wro": f(w_rnn_out)[0], "wo": f(w_out)[0],
        "wga": f(w_rg_a)[0], "wgx": f(w_rg_x)[0],
        "w2i": f(w_ffn2_in)[0], "w2o": f(w_ffn2_out)[0],
    }
    xp, xs = f(x_prompt), f(x_sample)
    sc, sr, sh = f(state_conv)[0], f(state_rconv)[0], f(state_h)[0]
    in_maps = []
    for i in range(NCORES):
        rows = np.ascontiguousarray(np.concatenate([common, sc[i], sr[i], sh[i][None, :]], axis=0))
        m = dict(shared)
        m.update({"xp": xp[i], "xs": xs[i], "rows": rows})
        in_maps.append(m)
    res = run_bass_kernel_spmd(nc, in_maps, core_ids=list(range(NCORES)))
    r = res.results
    yp = np.stack([r[i]["yp"] for i in range(NCORES)])
    ys = np.stack([r[i]["ys"] for i in range(NCORES)])
    so = np.stack([r[i]["so"] for i in range(NCORES)])
    return (yp.astype(np.float32), ys.astype(np.float32),
            so[None, :, O_CP:O_CP + 30, :], so[None, :, O_RP:O_RP + 3, :], so[None, :, O_HP, :],
            so[None, :, O_CS:O_CS + 30, :], so[None, :, O_RS:O_RS + 3, :], so[None, :, O_HS, :])
```

```python
import os
from contextlib import ExitStack

import numpy as np
import concourse.bass as bass
import concourse.mybir as mybir
from concourse.bass_utils import run_bass_kernel_spmd

F32 = mybir.dt.float32
BF16 = mybir.dt.bfloat16
AF = mybir.ActivationFunctionType
ALU = mybir.AluOpType

D = 1024
KC = 8
DFF = 2816
FC = 22
S = 2048
SS = 32
CK = 31
RK = 4
EPS = 1e-6
NCORES = 8

R_G1PRE, R_G1POST, R_GMPRE, R_GMPOST, R_BDW, R_LNG, R_LNB, R_BRC, R_BA, R_BX, R_LAM, R_G2PRE, R_G2POST, R_GFIN = range(14)
R_WDW = 14
R_WRC = R_WDW + CK
R_SCONV = R_WRC + RK
R_SRC = R_SCONV + 30
R_SH = R_SRC + 3
NR = R_SH + 1
O_CP, O_RP, O_HP, O_CS, O_RS, O_HS = 0, 30, 33, 34, 64, 67
NSO = 68

GROUPS = [(0, 704, False), (704, 704, False), (1408, 640, True)]
TGM = 704
U_S = 30 + TGM
UW = U_S + 30 + SS
RX_S = 3 + TGM
RXW = RX_S + 3 + SS
NSLOT = 6


class Eng:
    def __init__(self, name, h, sem):
        self.name, self.h, self.sem = name, h, sem
        self.count = 0
        self.seen = {}
        self.strict = name in ("act", "dve", "pool")


class DSem:
    def __init__(self, name, sem):
        self.name, self.sem = name, sem
        self.count = 0


class Buf:
    def __init__(self, name):
        self.name = name
        self.w = {}
        self.r = {}


def _deps(eng, reads, writes):
    deps = {}
    for b in reads:
        for o, c in b.w.items():
            if c > deps.get(o, 0):
                deps[o] = c
    for b in writes:
        for o, c in b.w.items():
            if (o is not eng or eng.strict) and c > deps.get(o, 0):
                deps[o] = c
        for o, c in b.r.items():
            if (o is not eng or eng.strict) and c > deps.get(o, 0):
                deps[o] = c
    return deps


def _wait(eng, deps):
    for o, c in deps.items():
        if c <= eng.seen.get(o, 0):
            continue
        eng.h.wait_ge(o.sem, c)
        eng.seen[o] = c


def op(eng, fn, reads=(), writes=()):
    _wait(eng, _deps(eng, reads, writes))
    inst = fn()
    eng.count += 1
    inst.then_inc(eng.sem, 1)
    for b in reads:
        b.r[eng] = eng.count
    for b in writes:
        b.w[eng] = eng.count


def dma(q, dsem, out_ap, in_ap, reads=(), writes=()):
    deps = {}
    for b in reads:
        for o, c in b.w.items():
            if c > deps.get(o, 0):
                deps[o] = c
    for b in writes:
        for o, c in b.w.items():
            if o is not q and o is not dsem and c > deps.get(o, 0):
                deps[o] = c
        for o, c in b.r.items():
            if o is not q and c > deps.get(o, 0):
                deps[o] = c
    _wait(q, deps)
    q.h.dma_start(out=out_ap, in_=in_ap).then_inc(dsem.sem, 16)
    dsem.count += 16
    for b in reads:
        b.r[dsem] = dsem.count
    for b in writes:
        b.w[dsem] = dsem.count


def build_program(dbg=False):
    nc = bass.Bass("TRN2", target_bir_lowering=False)

    def din(name, shape):
        return nc.dram_tensor(name, shape, F32, kind="ExternalInput").ap()

    def dout(name, shape):
        return nc.dram_tensor(name, shape, F32, kind="ExternalOutput").ap()

    xp_d = din("xp", [S, D])
    xs_d = din("xs", [SS, D])
    rows_d = din("rows", [NR, D])
    ident_d = din("ident", [128, 128])
    w1i_d = din("w1i", [D, 2 * DFF])
    w1o_d = din("w1o", [DFF, D])
    wi_d = din("wi", [D, 6 * D])
    wco_d = din("wco", [D, D])
    wro_d = din("wro", [D, D])
    wo_d = din("wo", [D, D])
    wga_d = din("wga", [8, 128, 128])
    wgx_d = din("wgx", [8, 128, 128])
    w2i_d = din("w2i", [D, 2 * DFF])
    w2o_d = din("w2o", [DFF, D])
    yp_d = dout("yp", [S, D])
    ys_d = dout("ys", [SS, D])
    so_d = dout("so", [NSO, D])
    dbg_d = dout("dbg", [16, 128, TGM]) if dbg else None

    es = ExitStack()
    with es:
        def sb(name, shape, dt):
            return es.enter_context(nc.sbuf_tensor(name, shape, dt))

        def sem(name):
            return es.enter_context(nc.semaphore(name))

        PE = Eng("pe", nc.tensor, sem("s_pe"))
        ACT = Eng("act", nc.scalar, sem("s_act"))
        DVE = Eng("dve", nc.vector, sem("s_dve"))
        POOL = Eng("pool", nc.gpsimd, sem("s_pool"))
        SP = Eng("sp", nc.sync, sem("s_sp"))

        x = sb("x", [128, KC, TGM], F32)
        xg = sb("xg", [128, KC, TGM], BF16)
        mg = sb("mg", [128, KC, TGM], BF16)
        ffo = sb("ffo", [128, KC, TGM], F32)
        act = sb("act", [128, FC, TGM], BF16)
        gh = act[:, 0:8, :]
        cvn = act[:, 8:16, :]
        mg_b, act_b, gh_b, cvn_b = Buf("mg"), Buf("act"), Buf("gh"), Buf("cvn")
        xg_bs = [Buf(f"xg{c}") for c in range(KC)]
        x_bs = [Buf(f"x{c}") for c in range(KC)]
        ffo_bs = [Buf(f"ffo{c}") for c in range(KC)]
        rstd = sb("rstd", [128, TGM], F32)
        mean = sb("mean", [128, TGM], F32)
        rstd_b, mean_b = Buf("rstd"), Buf("mean")
        P = sb("P", [128, KC, NR], F32)
        P_b = Buf("P")
        c1 = sb("c1", [128, KC], F32)
        c1_b = Buf("c1")
        c1h = sb("c1h", [128, KC], F32)
        eps_t = sb("eps_t", [128, 1], F32)
        q_t = sb("q_t", [128, 1], F32)
        bh = sb("bh", [128, 2 * KC], F32)
        SO = sb("SO", [128, KC, NSO], F32)
        SO_b = Buf("SO")
        uh = sb("uh", [128, KC, 30], BF16)
        rh = sb("rh", [128, KC, 3], F32)
        hprev = sb("hprev", [128, KC], F32)
        uh_b, rh_b, hprev_b = Buf("uh"), Buf("rh"), Buf("hprev")
        ident = sb("identf", [128, 128], F32)
        identb = sb("identb", [128, 128], BF16)
        ones = sb("ones", [128, 128], BF16)
        const_b = Buf("const")
        NTF, NTB = 7, 4
        tf = [(sb(f"tf{i}", [128, TGM], F32), Buf(f"tf{i}")) for i in range(NTF)]
        tb = [(sb(f"tb{i}", [128, TGM], BF16), Buf(f"tb{i}")) for i in range(NTB)]
        ut = [(sb(f"ut{i}", [128, UW], BF16), Buf(f"ut{i}")) for i in range(KC)]
        rxi = [(sb(f"rxi{i}", [128, RXW], F32), Buf(f"rxi{i}")) for i in range(1)]
        cbqs = [(sb(f"cbq{i}", [128, 2, TGM], BF16), Buf(f"cbq{i}")) for i in range(2)]
        Dg = [(sb(f"Dg{i}", [128, CK, 128], BF16), Buf(f"Dg{i}")) for i in range(2)]
        stage = [(sb(f"stg{i}", [128, D], F32), Buf(f"stg{i}"), DSem(f"stg{i}", sem(f"d_stg{i}"))) for i in range(2)]
        ring = [(sb(f"wr{i}", [128, 2048], BF16), Buf(f"wr{i}"), DSem(f"wr{i}", sem(f"d_wr{i}"))) for i in range(NSLOT)]
        banks = [(es.enter_context(nc.psum_tensor(f"bk{i}", [128, 512], F32)), Buf(f"bk{i}")) for i in range(8)]
        misc_ds = DSem("misc", sem("d_misc"))
        dbg_dss = [DSem(f"dbg{i}", sem(f"d_dbg{i}")) for i in range(16)] if dbg else []

        def dump(i, ap, buf, n=TGM):
            if dbg:
                dma(SP, dbg_dss[i], dbg_d[i, :, :n], ap, reads=[buf])

        cnt = {"tf": 0, "tb": 0, "ut": 0, "rxi": 0, "dg": 0, "stg": 0, "bank": 0, "ldt": 0}
        load_pend = []
        pool = list(range(8))

        def rr(key, lst):
            i = cnt[key] % len(lst)
            cnt[key] += 1
            return lst[i]

        def tmpf():
            return rr("tf", tf)

        def tmpb():
            return rr("tb", tb)

        def bank():
            i = pool[cnt["bank"] % len(pool)]
            cnt["bank"] += 1
            return banks[i]

        def take_banks(n):
            out = [banks[pool.pop()] for _ in range(n)]
            return out

        def give_banks(bs):
            for b in bs:
                pool.append([i for i in range(8) if banks[i] is b][0])

        def pe_mm(out_ap, pairs, reads, writes, start=True, stop=True):
            def fn():
                n = len(pairs)
                ins = None
                for i, (l, r) in enumerate(pairs):
                    ins = nc.tensor.matmul(out_ap, lhsT=l, rhs=r, start=(start and i == 0), stop=(stop and i == n - 1))
                return ins
            op(PE, fn, reads, writes)

        def pe_tr(out_ap, in_ap, idn, reads, writes):
            op(PE, lambda: nc.tensor.transpose(out=out_ap, in_=in_ap, identity=idn), reads, writes)

        def a_act(out_ap, in_ap, func, reads, writes, scale=None, bias=None):
            kw = {}
            if scale is not None:
                kw["scale"] = scale
            if bias is not None:
                kw["bias"] = bias
            op(ACT, lambda: nc.scalar.activation(out=out_ap, in_=in_ap, func=func, **kw), reads, writes)

        def v_tt(out_ap, in0, in1, alu, reads, writes):
            op(DVE, lambda: nc.vector.tensor_tensor(out=out_ap, in0=in0, in1=in1, op=alu), reads, writes)

        def v_stt(out_ap, in0, scalar, in1, op0, op1, reads, writes):
            op(DVE, lambda: nc.vector.scalar_tensor_tensor(out=out_ap, in0=in0, scalar=scalar, in1=in1, op0=op0, op1=op1), reads, writes)

        def v_ts(out_ap, in0, s1, s2, op0, op1, reads, writes):
            op(DVE, lambda: nc.vector.tensor_scalar(out=out_ap, in0=in0, scalar1=s1, scalar2=s2, op0=op0, op1=op1), reads, writes)

        def v_copy(out_ap, in_ap, reads, writes):
            op(DVE, lambda: nc.vector.tensor_copy(out=out_ap, in_=in_ap), reads, writes)

        def v_recip(out_ap, in_ap, reads, writes):
            op(DVE, lambda: nc.vector.reciprocal(out=out_ap, in_=in_ap), reads, writes)

        def prm(c, r):
            return P[:, c, r:r + 1]

        def wv(w, kc):
            return w.rearrange("(kc p) n -> p kc n", p=128)

        units = []

        def ffn_units(wi_, wo_):
            wiv = wv(wi_, KC)
            wov = wv(wo_, FC)
            for m in range(FC):
                units.append([(0, KC, wiv[:, :, m * 128:(m + 1) * 128]),
                              (1024, KC, wiv[:, :, DFF + m * 128:DFF + (m + 1) * 128])])
            for o in range(KC):
                for h in range(2):
                    units.append([(0, 11, wov[:, 11 * h:11 * h + 11, o * 128:(o + 1) * 128])])

        def mixer_units():
            wiv = wv(wi_d, KC)
            for c in range(KC):
                units.append([(0, KC, wiv[:, :, c * 128:(c + 1) * 128]),
                              (1024, KC, wiv[:, :, D + c * 128:D + (c + 1) * 128])])
            for c in range(KC):
                units.append([(0, KC, wiv[:, :, 2 * D + c * 128:2 * D + (c + 1) * 128]),
                              (1024, KC, wiv[:, :, 3 * D + c * 128:3 * D + (c + 1) * 128])])
            wcov, wrov, wov = wv(wco_d, KC), wv(wro_d, KC), wv(wo_d, KC)
            for o in range(KC):
                units.append([(0, KC, wiv[:, :, 4 * D + o * 128:4 * D + (o + 1) * 128]),
                              (1024, KC, wiv[:, :, 5 * D + o * 128:5 * D + (o + 1) * 128])])
                units.append([(0, KC, wcov[:, :, o * 128:(o + 1) * 128]),
                              (1024, KC, wrov[:, :, o * 128:(o + 1) * 128])])
            for oo in range(4):
                units.append([(0, KC, wov[:, :, (2 * oo) * 128:(2 * oo + 1) * 128]),
                              (1024, KC, wov[:, :, (2 * oo + 1) * 128:(2 * oo + 2) * 128])])

        for _ in GROUPS:
            ffn_units(w1i_d, w1o_d)
            mixer_units()
            ffn_units(w2i_d, w2o_d)

        wstate = {"issued": 0, "used": 0}

        def issue_unit():
            u = wstate["issued"]
            if u >= len(units):
                return
            t, b, ds = ring[u % NSLOT]
            for (off, a, src) in units[u]:
                dst = t[:, off:off + a * 128].rearrange("p (a b) -> p a b", a=a)
                dma(POOL, ds, dst, src, writes=[b])
            wstate["issued"] += 1

        def next_unit():
            u = wstate["used"]
            while wstate["issued"] < min(len(units), u + NSLOT - 1):
                issue_unit()
            wstate["used"] += 1
            return ring[u % NSLOT]

        gw = sb("gw", [128, 2048], BF16)
        gwb = Buf("gw")
        gw_ds = DSem("gw", sem("d_gw"))
        dma(POOL, gw_ds, gw[:, 0:1024].rearrange("p (a b) -> p a b", a=8), wga_d.rearrange("n h k -> h n k"), writes=[gwb])
        dma(POOL, gw_ds, gw[:, 1024:2048].rearrange("p (a b) -> p a b", a=8), wgx_d.rearrange("n h k -> h n k"), writes=[gwb])
        dma(SP, misc_ds, ident[:], ident_d, writes=[const_b])
        st_t, st_b, st_ds = stage[0]
        dma(SP, st_ds, st_t[:NR, :], rows_d, writes=[st_b])
        op(DVE, lambda: nc.vector.memset(ones[:], 1.0), writes=[const_b])
        v_copy(identb[:], ident[:], [const_b], [const_b])
        op(DVE, lambda: nc.vector.memset(uh[:], 0.0), writes=[uh_b])
        op(DVE, lambda: nc.vector.memset(rh[:], 0.0), writes=[rh_b])
        op(DVE, lambda: nc.vector.memset(hprev[:], 0.0), writes=[hprev_b])
        for half in range(2):
            bt, bb = bank()
            for q in range(4):
                kc = half * 4 + q
                pe_tr(bt[:, q * 128:q * 128 + NR], st_t[:NR, kc * 128:(kc + 1) * 128], ident[:NR, :NR], [st_b, const_b], [bb])
            a_act(P[:, half * 4:half * 4 + 4, :], bt[:, :].rearrange("p (q n) -> p q n", q=4)[:, :, :NR], AF.Copy, [bb], [P_b])
        t1, t1b = tmpf()
        a_act(t1[:, 0:KC], P[:, :, R_LAM], AF.Exp, [P_b], [t1b], scale=-1.0)
        a_act(t1[:, 8:8 + KC], t1[:, 0:KC], AF.Ln, [t1b], [t1b], bias=1.0)
        v_ts(c1[:, :], t1[:, 8:8 + KC], -8.0, None, ALU.mult, ALU.bypass, [t1b], [c1_b])
        v_ts(c1h[:, :], t1[:, 8:8 + KC], -4.0, None, ALU.mult, ALU.bypass, [t1b], [c1_b])
        op(DVE, lambda: nc.vector.memset(eps_t[:], EPS), writes=[c1_b])
        op(DVE, lambda: nc.vector.memset(q_t[:], 0.25), writes=[c1_b])
        v_ts(bh[:, 0:KC], P[:, :, R_BA], 0.5, None, ALU.mult, ALU.bypass, [P_b], [c1_b])
        v_ts(bh[:, KC:2 * KC], P[:, :, R_BX], 0.5, None, ALU.mult, ALU.bypass, [P_b], [c1_b])

        def stats_mm(sbank, nb, src_ap, src_b, first, last):
            st, sbuf_ = sbank
            pe_mm(st[:, :nb], [(ones[:, :], src_ap)], [const_b, src_b], [sbuf_], start=first, stop=last)

        def rstd_from(sbank, b0, nb, f, wb=None):
            st, sbuf_ = sbank
            t, tb_ = tmpf()
            a_act(t[:, :nb], st[:, :nb], AF.Ln, [sbuf_, c1_b], [tb_], scale=1.0 / D, bias=eps_t[:, 0:1])
            a_act(rstd[:, b0:b0 + nb], t[:, :nb], AF.Exp, [tb_], [rstd_b] if wb is None else wb, scale=-0.5, bias=float(np.log(f)))

        def prenorm(TG, blocks, grow, out_t, out_b):
            sbk = take_banks(len(blocks))
            for c in range(KC):
                q, qb = tmpb()
                a_act(q[:, :TG], x[:, c, :TG], AF.Square, [x_bs[c]], [qb])
                for bi, (b0, nb) in enumerate(blocks):
                    stats_mm(sbk[bi], nb, q[:, b0:b0 + nb], qb, c == 0, c == KC - 1)
            for bi, (b0, nb) in enumerate(blocks):
                rstd_from(sbk[bi], b0, nb, 1.0)
            give_banks(sbk)
            for c in range(KC):
                v_stt(out_t[:, c, :TG], x[:, c, :TG], prm(c, grow), rstd[:, :TG], ALU.mult, ALU.mult, [x_bs[c], P_b, rstd_b], [out_b[c] if isinstance(out_b, list) else out_b])

        def postnorm(TG, blocks, sbk, grow, f):
            for bi, (b0, nb) in enumerate(blocks):
                rstd_from(sbk[bi], b0, nb, f)
            def p1(c):
                v_tt(ffo[:, c, :TG], ffo[:, c, :TG], rstd[:, :TG], ALU.mult, [ffo_bs[c], rstd_b], [ffo_bs[c]])

            p1(0)
            for c in range(KC):
                if c + 1 < KC:
                    p1(c + 1)
                v_stt(x[:, c, :TG], ffo[:, c, :TG], prm(c, grow), x[:, c, :TG], ALU.mult, ALU.add, [ffo_bs[c], P_b, x_bs[c]], [x_bs[c]])

        def first_unit(wt, wb, blocks):
            res = [(bank(), bank()) for _ in blocks]
            for kc in range(KC):
                for (b0, nb), ((a_t, a_b), (b_t, b_b)) in zip(blocks, res):
                    pe_mm(a_t[:, :nb], [(wt[:, kc * 128:(kc + 1) * 128], xg[:, kc, b0:b0 + nb])], [wb, xg_bs[kc]], [a_b],
                          start=(kc == 0), stop=(kc == KC - 1))
                    pe_mm(b_t[:, :nb], [(wt[:, 1024 + kc * 128:1024 + (kc + 1) * 128], xg[:, kc, b0:b0 + nb])], [wb, xg_bs[kc]], [b_b],
                          start=(kc == 0), stop=(kc == KC - 1))
            return res

        def ffn(TG, blocks, grow_pre, grow_post, do_prenorm=True, bg=None):
            if do_prenorm:
                prenorm(TG, blocks, grow_pre, xg, xg_bs)
            for m in range(FC):
                wt, wb, _ = next_unit()
                pre = first_unit(wt, wb, blocks) if m == 0 else None
                for bi, (b0, nb) in enumerate(blocks):
                    if pre is not None:
                        (gt, gb), (upt, upb) = pre[bi]
                    else:
                        gt, gb = bank()
                        upt, upb = bank()
                        pe_mm(gt[:, :nb], [(wt[:, kc * 128:(kc + 1) * 128], xg[:, kc, b0:b0 + nb]) for kc in range(KC)], [wb] + xg_bs, [gb])
                        pe_mm(upt[:, :nb], [(wt[:, 1024 + kc * 128:1024 + (kc + 1) * 128], xg[:, kc, b0:b0 + nb]) for kc in range(KC)], [wb] + xg_bs, [upb])
                    t, tb_ = tmpf()
                    a_act(t[:, :nb], gt[:, :nb], AF.Silu, [gb], [tb_])
                    v_tt(act[:, m, b0:b0 + nb], upt[:, :nb], t[:, :nb], ALU.mult, [upb, tb_], [act_b, gh_b, cvn_b])
                if bg and m >= 1 and m % 2 == 1:
                    bg.pop(0)()
            while bg:
                bg.pop(0)()
            sbk = take_banks(len(blocks))
            pending = []
            for o in range(KC):
                w0 = next_unit()
                w1 = next_unit()
                for bi, (b0, nb) in enumerate(blocks):
                    bt, bb = bank()
                    pairs = []
                    for kc in range(FC):
                        wt = (w0 if kc < 11 else w1)[0]
                        kk = kc % 11
                        pairs.append((wt[:, kk * 128:(kk + 1) * 128], act[:, kc, b0:b0 + nb]))
                    pe_mm(bt[:, :nb], pairs, [w0[1], w1[1], act_b, gh_b, cvn_b], [bb])
                    for fn_ in pending:
                        fn_()
                    pending = []
                    a_act(ffo[:, o, b0:b0 + nb], bt[:, :nb], AF.Copy, [bb], [ffo_bs[o]])
                    q, qb = tmpb()
                    a_act(q[:, :nb], bt[:, :nb], AF.Square, [bb], [qb])
                    pending.append(lambda bi=bi, nb=nb, q=q, qb=qb, o=o: stats_mm(sbk[bi], nb, q[:, :nb], qb, o == 0, o == KC - 1))
            for fn_ in pending:
                fn_()
            postnorm(TG, blocks, sbk, grow_post, 0.5)
            give_banks(sbk)

        def mixer(TG, TGp, has_s, blocks, last):
            prenorm(TG, blocks, R_GMPRE, xg, xg_bs)

            def split(b0, nb):
                npr = min(nb, TGp - b0)
                return npr, nb - npr

            def build_dg(c):
                dg, dgb = Dg[c % 2]
                v_tt(dg[:, :, :], identb[:, :].unsqueeze(1).broadcast_to([128, CK, 128]),
                     P[:, c, R_WDW:R_WDW + CK].unsqueeze(2).broadcast_to([128, CK, 128]), ALU.mult, [const_b, P_b], [dgb])

            for c in range(KC):
                if c == 3:
                    build_dg(0)
                if c == 5:
                    build_dg(1)
                wt, wb, _ = next_unit()
                u, ub = ut[c]
                v_copy(u[:, 0:30], uh[:, c, :], [uh_b], [ub])
                if has_s:
                    v_copy(u[:, U_S:U_S + 30], P[:, c, R_SCONV:R_SCONV + 30], [P_b], [ub])
                pre = first_unit(wt, wb, blocks) if c == 0 else None
                for bi, (b0, nb) in enumerate(blocks):
                    npr, nsm = split(b0, nb)
                    if pre is not None:
                        (vt, vb), (gt, gb) = pre[bi]
                    else:
                        vt, vb = bank()
                        gt, gb = bank()
                        pe_mm(vt[:, :nb], [(wt[:, kc * 128:(kc + 1) * 128], xg[:, kc, b0:b0 + nb]) for kc in range(KC)], [wb] + xg_bs, [vb])
                        pe_mm(gt[:, :nb], [(wt[:, 1024 + kc * 128:1024 + (kc + 1) * 128], xg[:, kc, b0:b0 + nb]) for kc in range(KC)], [wb] + xg_bs, [gb])
                    t, tb_ = tmpf()
                    a_act(t[:, :nb], gt[:, :nb], AF.Sigmoid, [gb], [tb_])
                    v_tt(u[:, 30 + b0:30 + b0 + npr], vt[:, :npr], t[:, :npr], ALU.mult, [vb, tb_], [ub])
                    if nsm:
                        v_tt(u[:, U_S + 30:U_S + 30 + SS], vt[:, npr:nb], t[:, npr:nb], ALU.mult, [vb, tb_], [ub])
                    if last and bi == len(blocks) - 1:
                        v_tt(SO[:, c, O_CP:O_CP + 30], vt[:, npr - 30:npr], t[:, npr - 30:npr], ALU.mult, [vb, tb_], [SO_b])
                        v_tt(SO[:, c, O_CS:O_CS + 30], vt[:, npr + 2:npr + 32], t[:, npr + 2:npr + 32], ALU.mult, [vb, tb_], [SO_b])
                v_copy(uh[:, c, :], u[:, TGp:TGp + 30], [ub], [uh_b])

            spieces = [(q0, min(256, TG - q0)) for q0 in range(0, TG, 256)]
            sPQ = take_banks(len(spieces))
            def ln_stats(c):
                cbq, cbq_b = cbqs[c % 2]
                for pi, (q0, w) in enumerate(spieces):
                    st, stb = sPQ[pi]
                    pe_mm(st[:, 0:2 * w].rearrange("p (a b) -> p a b", a=2), [(ones[:, :], cbq[:, :, q0:q0 + w])], [const_b, cbq_b], [stb],
                          start=(c == 0), stop=(c == KC - 1))

            st8 = [dict() for _ in range(KC)]

            def H1(c):
                d = st8[c]
                wt, wb, _ = next_unit()
                ri, rib = rr("rxi", rxi)
                d["ri"] = (ri, rib)
                v_copy(ri[:, 0:3], rh[:, c, :], [rh_b], [rib])
                if has_s:
                    v_copy(ri[:, RX_S:RX_S + 3], P[:, c, R_SRC:R_SRC + 3], [P_b], [rib])
                rg, rgb = tmpb()
                d["rg"] = (rg, rgb)
                for bi, (b0, nb) in enumerate(blocks):
                    npr, nsm = split(b0, nb)
                    xt_, xb_ = bank()
                    gt, gb = bank()
                    pe_mm(xt_[:, :nb], [(wt[:, kc * 128:(kc + 1) * 128], xg[:, kc, b0:b0 + nb]) for kc in range(KC)], [wb] + xg_bs, [xb_])
                    pe_mm(gt[:, :nb], [(wt[:, 1024 + kc * 128:1024 + (kc + 1) * 128], xg[:, kc, b0:b0 + nb]) for kc in range(KC)], [wb] + xg_bs, [gb])
                    v_copy(ri[:, 3 + b0:3 + b0 + npr], xt_[:, :npr], [xb_], [rib])
                    if nsm:
                        v_copy(ri[:, RX_S + 3:RX_S + 3 + SS], xt_[:, npr:nb], [xb_], [rib])
                    a_act(rg[:, b0:b0 + nb], gt[:, :nb], AF.Gelu_apprx_tanh, [gb], [rgb])

            def H2(c):
                d = st8[c]
                ri, rib = d["ri"]
                v_copy(rh[:, c, :], ri[:, TGp:TGp + 3], [rib], [rh_b])
                if last:
                    v_copy(SO[:, c, O_RP:O_RP + 3], ri[:, TGp:TGp + 3], [rib], [SO_b])
                    v_copy(SO[:, c, O_RS:O_RS + 3], ri[:, RX_S + SS:RX_S + SS + 3], [rib], [SO_b])
                rx, rxb = tmpf()
                d["rx"] = (rx, rxb)
                segs = [(0, 0, TGp)] + ([(RX_S, TGp, SS)] if has_s else [])
                for (src0, dst0, n) in segs:
                    v_ts(rx[:, dst0:dst0 + n], ri[:, src0:src0 + n], prm(c, R_WRC), prm(c, R_BRC), ALU.mult, ALU.add, [rib, P_b], [rxb])
                    for k in range(1, RK):
                        v_stt(rx[:, dst0:dst0 + n], ri[:, src0 + k:src0 + k + n], prm(c, R_WRC + k), rx[:, dst0:dst0 + n], ALU.mult, ALU.add, [rib, P_b, rxb], [rxb])
                rxq, rxqb = tmpb()
                d["rxq"] = (rxq, rxqb)
                a_act(rxq[:, :TG], rx[:, :TG], AF.Copy, [rxb], [rxqb])

            def H3(c):
                u, ub = ut[c]
                dg, dgb = Dg[c % 2]
                for bi, (b0, nb) in enumerate(blocks):
                    npr, nsm = split(b0, nb)
                    bt, bb = bank()
                    pe_mm(bt[:, :npr], [(dg[:, k, :], u[:, b0 + k:b0 + k + npr]) for k in range(CK)], [dgb, ub], [bb])
                    if nsm:
                        pe_mm(bt[:, npr:nb], [(dg[:, k, :], u[:, U_S + k:U_S + k + SS]) for k in range(CK)], [dgb, ub], [bb])
                    a_act(ffo[:, c, b0:b0 + nb], bt[:, :nb], AF.Identity, [bb, P_b], [ffo_bs[c]], bias=prm(c, R_BDW))
                if c + 2 < KC:
                    build_dg(c + 2)

            def H4(c):
                d = st8[c]
                rxq, rxqb = d["rxq"]
                rt, rtb = tmpf()
                it, itb = tmpf()
                d["rt"], d["it"] = (rt, rtb), (it, itb)
                for bi, (b0, nb) in enumerate(blocks):
                    rp, rpb = bank()
                    ip, ipb = bank()
                    pe_mm(rp[:, :nb], [(gw[:, c * 128:(c + 1) * 128], rxq[:, b0:b0 + nb])], [gwb, rxqb], [rpb])
                    pe_mm(ip[:, :nb], [(gw[:, 1024 + c * 128:1024 + (c + 1) * 128], rxq[:, b0:b0 + nb])], [gwb, rxqb], [ipb])
                    a_act(rt[:, b0:b0 + nb], rp[:, :nb], AF.Tanh, [rpb, c1_b], [rtb], scale=0.5, bias=bh[:, c:c + 1])
                    a_act(it[:, b0:b0 + nb], ip[:, :nb], AF.Tanh, [ipb, c1_b], [itb], scale=0.5, bias=bh[:, KC + c:KC + c + 1])
                if c >= 1:
                    ln_stats(c - 1)
                cbq, cbq_b = cbqs[c % 2]
                a_act(cbq[:, 0, :TG], ffo[:, c, :TG], AF.Copy, [ffo_bs[c]], [cbq_b])
                a_act(cbq[:, 1, :TG], ffo[:, c, :TG], AF.Square, [ffo_bs[c]], [cbq_b])

            def T1(c):
                d = st8[c]
                rt, rtb = d["rt"]
                at, atb = tmpf()
                d["at"] = (at, atb)
                a_act(at[:, :TG], rt[:, :TG], AF.Exp, [rtb, c1_b], [atb], scale=c1h[:, c:c + 1], bias=c1h[:, c:c + 1])
                a_act(rt[:, :TG], at[:, :TG], AF.Square, [atb], [rtb])
                a_act(rt[:, :TG], rt[:, :TG], AF.Sqrt, [rtb, c1_b], [rtb], scale=-0.25, bias=q_t[:, 0:1])

            def T2(c):
                d = st8[c]
                rt, rtb = d["rt"]
                it, itb = d["it"]
                at, atb = d["at"]
                rx, rxb = d["rx"]
                rg, rgb = d["rg"]
                ri, rib = d["ri"]
                v_stt(it[:, :TG], it[:, :TG], 1.0, rx[:, :TG], ALU.add, ALU.mult, [itb, rxb], [itb])
                v_tt(it[:, :TG], it[:, :TG], rt[:, :TG], ALU.mult, [itb, rtb], [itb])
                hs, hsb = tmpf()
                op(DVE, lambda: nc.vector.tensor_tensor_scan(out=hs[:, :TGp], data0=at[:, :TGp], data1=it[:, :TGp],
                                                             initial=hprev[:, c:c + 1], op0=ALU.mult, op1=ALU.add),
                   [atb, itb, hprev_b], [hsb])
                if has_s:
                    op(DVE, lambda: nc.vector.tensor_tensor_scan(out=hs[:, TGp:TG], data0=at[:, TGp:TG], data1=it[:, TGp:TG],
                                                                 initial=prm(c, R_SH), op0=ALU.mult, op1=ALU.add),
                       [atb, itb, P_b], [hsb])
                if dbg and c == int(os.environ.get('DBGC', '0')) and not has_s and TGp == 704 and cnt.get("dumped") is None:
                    cnt["dumped"] = 1
                    dump(0, rx[:, :], rxb); dump(1, rt[:, :], rtb); dump(2, it[:, :], itb); dump(3, at[:, :], atb); dump(4, hs[:, :], hsb)
                    dump(5, c1[:, :], c1_b, KC); dump(7, ffo[:, c, :], ffo_bs[c]); dump(8, x[:, c, :], x_bs[c])
                v_copy(hprev[:, c:c + 1], hs[:, TGp - 1:TGp], [hsb], [hprev_b])
                if last:
                    v_copy(SO[:, c, O_HP:O_HP + 1], hs[:, TGp - 1:TGp], [hsb], [SO_b])
                    v_copy(SO[:, c, O_HS:O_HS + 1], hs[:, TG - 1:TG], [hsb], [SO_b])
                v_tt(gh[:, c, :TG], hs[:, :TG], rg[:, :TG], ALU.mult, [hsb, rgb], [gh_b])
                st8[c].clear()

            H1(0); H2(0); H3(0); H4(0)
            for c in range(KC - 1):
                H1(c + 1)
                T1(c)
                H2(c + 1)
                T2(c)
                H3(c + 1)
                H4(c + 1)
            ln_stats(KC - 1)

            for pi, (q0, w) in enumerate(spieces):
                st, stb = sPQ[pi]
                rs = rstd[:, q0:q0 + w]
                a_act(mean[:, q0:q0 + w], st[:, 0:w], AF.Copy, [stb], [mean_b], scale=1.0 / D)
                v_tt(rs, mean[:, q0:q0 + w], mean[:, q0:q0 + w], ALU.mult, [mean_b], [rstd_b])
                v_stt(rs, st[:, w:2 * w], 1.0 / D, rs, ALU.mult, ALU.subtract, [stb, rstd_b], [rstd_b])
                a_act(rs, rs, AF.Ln, [rstd_b, c1_b], [rstd_b], bias=eps_t[:, 0:1])
                a_act(rs, rs, AF.Exp, [rstd_b], [rstd_b], scale=-0.5)
            give_banks(sPQ)
            if dbg and cnt.get("dumped3") is None:
                cnt["dumped3"] = 1
                dump(12, mean[:, :], mean_b); dump(13, rstd[:, :], rstd_b)

            def ln_sub(c):
                v_tt(ffo[:, c, :TG], ffo[:, c, :TG], mean[:, :TG], ALU.subtract, [ffo_bs[c], mean_b], [ffo_bs[c]])

            def ln_rest(c):
                v_tt(ffo[:, c, :TG], ffo[:, c, :TG], rstd[:, :TG], ALU.mult, [ffo_bs[c], rstd_b], [ffo_bs[c]])
                a_act(cvn[:, c, :TG], ffo[:, c, :TG], AF.Silu, [ffo_bs[c], P_b], [cvn_b], scale=prm(c, R_LNG), bias=prm(c, R_LNB))

            def ln_range(c0_, c1_):
                ln_sub(c0_)
                for c in range(c0_, c1_):
                    if c + 1 < c1_:
                        ln_sub(c + 1)
                    ln_rest(c)

            T1(KC - 1)
            ln_range(0, 4)
            T2(KC - 1)
            ln_range(4, KC)

            for o in range(KC):
                w1 = next_unit()
                w2 = next_unit()
                srcs = [(w1, 0, xg, xg_bs), (w1, 1024, xg, xg_bs), (w2, 0, cvn, [cvn_b]), (w2, 1024, gh, [gh_b])]
                pss = [[bank() for _ in range(4)] for _ in blocks]
                order = ([(bi, j) for bi in range(len(blocks)) for j in (0, 1, 3)] + [(bi, 2) for bi in range(len(blocks))]) if o == 0 \
                    else [(bi, j) for bi in range(len(blocks)) for j in range(4)]
                for bi, j in order:
                    b0, nb = blocks[bi]
                    pt, pb = pss[bi][j]
                    wu, off, rhs_t, rhs_b = srcs[j]
                    pe_mm(pt[:, :nb], [(wu[0][:, off + kc * 128:off + (kc + 1) * 128], rhs_t[:, kc, b0:b0 + nb]) for kc in range(KC)],
                          [wu[1]] + rhs_b, [pb])
                for bi, (b0, nb) in enumerate(blocks):
                    ps = pss[bi]
                    ta, tab = tmpf()
                    tr_, trb = tmpf()
                    a_act(ta[:, :nb], ps[0][0][:, :nb], AF.Sigmoid, [ps[0][1]], [tab])
                    a_act(tr_[:, :nb], ps[1][0][:, :nb], AF.Sigmoid, [ps[1][1]], [trb])
                    v_tt(ta[:, :nb], ps[2][0][:, :nb], ta[:, :nb], ALU.mult, [ps[2][1], tab], [tab])
                    v_tt(tr_[:, :nb], ps[3][0][:, :nb], tr_[:, :nb], ALU.mult, [ps[3][1], trb], [trb])
                    v_tt(mg[:, o, b0:b0 + nb], ta[:, :nb], tr_[:, :nb], ALU.add, [tab, trb], [mg_b])

            sbk = take_banks(len(blocks))
            pending = []
            for oo in range(4):
                wu = next_unit()
                for o in (2 * oo, 2 * oo + 1):
                    off = (o % 2) * 1024
                    for bi, (b0, nb) in enumerate(blocks):
                        bt, bb = bank()
                        pe_mm(bt[:, :nb], [(wu[0][:, off + kc * 128:off + (kc + 1) * 128], mg[:, kc, b0:b0 + nb]) for kc in range(KC)],
                              [wu[1], mg_b], [bb])
                        for fn_ in pending:
                            fn_()
                        pending = []
                        a_act(ffo[:, o, b0:b0 + nb], bt[:, :nb], AF.Copy, [bb], [ffo_bs[o]])
                        q, qb = tmpb()
                        a_act(q[:, :nb], bt[:, :nb], AF.Square, [bb], [qb])
                        pending.append(lambda bi=bi, nb=nb, q=q, qb=qb, o=o: stats_mm(sbk[bi], nb, q[:, :nb], qb, o == 0, o == KC - 1))
            for fn_ in pending:
                fn_()
            postnorm(TG, blocks, sbk, R_GMPOST, 1.0)
            give_banks(sbk)

        def group_cfg(gi):
            p0, TGp, has_s = GROUPS[gi]
            TG = TGp + (SS if has_s else 0)
            h = TGp // 2
            return p0, TGp, has_s, TG, [(0, h), (h, TG - h)]

        def load_tile(src, r0, c0, nt):
            st_t, st_b, st_ds = rr("stg", stage)
            xt_b, rr_b = Buf("xt"), Buf("rr")
            dma(SP, st_ds, st_t[:nt, :], src[r0:r0 + nt, :], writes=[st_b])
            for half in range(2):
                bt, bb = bank()
                for q in range(4):
                    kc = half * 4 + q
                    pe_tr(bt[:, q * 128:q * 128 + nt], st_t[:nt, kc * 128:(kc + 1) * 128], ident[:nt, :nt], [st_b, const_b], [bb])
                a_act(x[:, half * 4:half * 4 + 4, c0:c0 + nt], bt[:, :].rearrange("p (q n) -> p q n", q=4)[:, :, :nt], AF.Copy, [bb],
                      x_bs[half * 4:half * 4 + 4] + [xt_b])
            cq_t, cq_b = cbqs[cnt["ldt"] % 2]
            cnt["ldt"] += 1
            q3 = cq_t[:, :, :].rearrange("p a b -> p (a b)")[:, 0:KC * 128].rearrange("p (k n) -> p k n", k=KC)[:, :, :nt]
            a_act(q3, x[:, :, c0:c0 + nt], AF.Square, [xt_b], [cq_b])
            for fn_ in load_pend:
                fn_()
            load_pend.clear()

            def rest(c0=c0, nt=nt, q3=q3, cq_b=cq_b, xt_b=xt_b, rr_b=rr_b):
                sbank = bank()
                for kc in range(KC):
                    stats_mm(sbank, nt, q3[:, kc, :], cq_b, kc == 0, kc == KC - 1)
                rstd_from(sbank, c0, nt, 1.0, wb=[rr_b, rstd_b])
                for c in range(KC):
                    v_stt(xg[:, c, c0:c0 + nt], x[:, c, c0:c0 + nt], prm(c, R_G1PRE), rstd[:, c0:c0 + nt], ALU.mult, ALU.mult,
                          [xt_b, P_b, rr_b], [xg_bs[c]])
            load_pend.append(rest)

        def flush_load():
            for fn_ in load_pend:
                fn_()
            load_pend.clear()
            for c in range(KC):
                x_bs[c].r[DVE] = DVE.count
                x_bs[c].r[ACT] = ACT.count
            rstd_b.r[DVE] = DVE.count

        def store_tile(dst, r0, c0, nt):
            st_t, st_b, st_ds = rr("stg", stage)
            for half in range(2):
                bt, bb = bank()
                for q in range(4):
                    kc = half * 4 + q
                    pe_tr(bt[:nt, q * 128:(q + 1) * 128], ffo[:, kc, c0:c0 + nt], ident[:, :], [ffo_bs[kc], const_b], [bb])
                a_act(st_t[:nt, half * 512:(half + 1) * 512], bt[:nt, :], AF.Copy, [bb], [st_b])
            dma(ACT, st_ds, dst[r0:r0 + nt, :], st_t[:nt, :], reads=[st_b])

        def tiles_of(gi, dp, ds_):
            p0, TGp, has_s, TG, blocks = group_cfg(gi)
            tl = [(dp, p0 + t0, t0, min(128, TGp - t0)) for t0 in range(0, TGp, 128)]
            if has_s:
                tl.append((ds_, 0, TGp, SS))
            return tl

        deferred = []
        for tl in tiles_of(0, xp_d, xs_d):
            load_tile(*tl)
        flush_load()
        for gi in range(len(GROUPS)):
            p0, TGp, has_s, TG, blocks = group_cfg(gi)
            last = gi == len(GROUPS) - 1
            ffn(TG, blocks, R_G1PRE, R_G1POST, do_prenorm=False, bg=deferred)
            mixer(TG, TGp, has_s, blocks, last)
            ffn(TG, blocks, R_G2PRE, R_G2POST)
            prenorm(TG, blocks, R_GFIN, ffo, ffo_bs)
            outs = tiles_of(gi, yp_d, ys_d)
            if last:
                for tl in outs:
                    store_tile(*tl)
            else:
                deferred.extend([(lambda tl=tl: store_tile(*tl)) for tl in outs])
                for tl in tiles_of(gi + 1, xp_d, xs_d):
                    load_tile(*tl)
                flush_load()

        st_t, st_b, st_ds = rr("stg", stage)
        for half in range(2):
            bt, bb = bank()
            for q in range(4):
                kc = half * 4 + q
                pe_tr(bt[:NSO, q * 128:(q + 1) * 128], SO[:, kc, :], ident[:, :], [SO_b, const_b], [bb])
            a_act(st_t[:NSO, half * 512:(half + 1) * 512], bt[:NSO, :], AF.Copy, [bb], [st_b])
        dma(SP, st_ds, so_d, st_t[:NSO, :], reads=[st_b])
        for (_, _, ds) in stage:
            nc.sync.wait_ge(ds.sem, ds.count)
        for ds in dbg_dss:
            if ds.count:
                nc.sync.wait_ge(ds.sem, ds.count)
    return nc


_CACHE = {}


def kernel(x_prompt, x_sample, state_conv, state_rconv, state_h,
           g_ffn1_pre, g_ffn1_post, w_ffn1_in, w_ffn1_out,
           g_mix_pre, g_mix_post, w_in,
           w_dw, b_dw, ln_g, ln_b, w_conv_out,
           w_rconv, b_rconv, w_rg_a, b_rg_a, w_rg_x, b_rg_x, lam, w_rnn_out,
           w_out,
           g_ffn2_pre, g_ffn2_post, w_ffn2_in, w_ffn2_out, g_final):
    f = lambda a: np.ascontiguousarray(np.asarray(a, dtype=np.float32))
    if "nc" not in _CACHE:
        _CACHE["nc"] = build_program()
    nc = _CACHE["nc"]
    vec_rows = [g_ffn1_pre, g_ffn1_post, g_mix_pre, g_mix_post, b_dw, ln_g, ln_b, b_rconv, b_rg_a, b_rg_x, lam,
                g_ffn2_pre, g_ffn2_post, g_final]
    common = np.concatenate([f(v)[0][None, :] for v in vec_rows] + [f(w_dw)[0], f(w_rconv)[0]], axis=0)
    shared = {
        "ident": np.eye(128, dtype=np.float32),
        "w1i": f(w_ffn1_in)[0], "w1o": f(w_ffn1_out)[0], "wi": f(w_in)[0],
        "wco": f(w_conv_out)[0], "wro": f(w_rnn_out)[0], "wo": f(w_out)[0],
        "wga": f(w_rg_a)[0], "wgx": f(w_rg_x)[0],
        "w2i": f(w_ffn2_in)[0], "w2o": f(w_ffn2_out)[0],
    }
    xp, xs = f(x_prompt), f(x_sample)
    sc, sr, sh = f(state_conv)[0], f(state_rconv)[0], f(state_h)[0]
    in_maps = []
    for i in range(NCORES):
        rows = np.ascontiguousarray(np.concatenate([common, sc[i], sr[i], sh[i][None, :]], axis=0))
        m = dict(shared)
        m.update({"xp": xp[i], "xs": xs[i], "rows": rows})
        in_maps.append(m)
    res = run_bass_kernel_spmd(nc, in_maps, core_ids=list(range(NCORES)))
    r = res.results
    yp = np.stack([r[i]["yp"] for i in range(NCORES)])
    ys = np.stack([r[i]["ys"] for i in range(NCORES)])
    so = np.stack([r[i]["so"] for i in range(NCORES)])
    return (yp.astype(np.float32), ys.astype(np.float32),
            so[None, :, O_CP:O_CP + 30, :], so[None, :, O_RP:O_RP + 3, :], so[None, :, O_HP, :],
            so[None, :, O_CS:O_CS + 30, :], so[None, :, O_RS:O_RS + 3, :], so[None, :, O_HS, :])
```

```python
import os
from contextlib import ExitStack

import numpy as np
import concourse.bass as bass
import concourse.mybir as mybir
from concourse.bass_utils import run_bass_kernel_spmd

F32 = mybir.dt.float32
BF16 = mybir.dt.bfloat16
AF = mybir.ActivationFunctionType
ALU = mybir.AluOpType

D = 1024
KC = 8
DFF = 2816
FC = 22
S = 2048
SS = 32
CK = 31
RK = 4
EPS = 1e-6
NCORES = 8

R_G1PRE, R_G1POST, R_GMPRE, R_GMPOST, R_BDW, R_LNG, R_LNB, R_BRC, R_BA, R_BX, R_LAM, R_G2PRE, R_G2POST, R_GFIN = range(14)
R_WDW = 14
R_WRC = R_WDW + CK
R_SCONV = R_WRC + RK
R_SRC = R_SCONV + 30
R_SH = R_SRC + 3
NR = R_SH + 1
O_CP, O_RP, O_HP, O_CS, O_RS, O_HS = 0, 30, 33, 34, 64, 67
NSO = 68

GROUPS = [(0, 704, False), (704, 704, False), (1408, 640, True)]
TGM = 704
U_S = 30 + TGM
UW = U_S + 30 + SS
RX_S = 3 + TGM
RXW = RX_S + 3 + SS
NSLOT = 6


class Eng:
    def __init__(self, name, h, sem):
        self.name, self.h, self.sem = name, h, sem
        self.count = 0
        self.seen = {}
        self.strict = name in ("act", "dve", "pool")


class DSem:
    def __init__(self, name, sem):
        self.name, self.sem = name, sem
        self.count = 0


class Buf:
    def __init__(self, name):
        self.name = name
        self.w = {}
        self.r = {}


def _deps(eng, reads, writes):
    deps = {}
    for b in reads:
        for o, c in b.w.items():
            if c > deps.get(o, 0):
                deps[o] = c
    for b in writes:
        for o, c in b.w.items():
            if (o is not eng or eng.strict) and c > deps.get(o, 0):
                deps[o] = c
        for o, c in b.r.items():
            if (o is not eng or eng.strict) and c > deps.get(o, 0):
                deps[o] = c
    return deps


def _wait(eng, deps):
    for o, c in deps.items():
        if c <= eng.seen.get(o, 0):
            continue
        eng.h.wait_ge(o.sem, c)
        eng.seen[o] = c


def op(eng, fn, reads=(), writes=()):
    _wait(eng, _deps(eng, reads, writes))
    inst = fn()
    eng.count += 1
    inst.then_inc(eng.sem, 1)
    for b in reads:
        b.r[eng] = eng.count
    for b in writes:
        b.w[eng] = eng.count


def dma(q, dsem, out_ap, in_ap, reads=(), writes=()):
    deps = {}
    for b in reads:
        for o, c in b.w.items():
            if c > deps.get(o, 0):
                deps[o] = c
    for b in writes:
        for o, c in b.w.items():
            if o is not q and o is not dsem and c > deps.get(o, 0):
                deps[o] = c
        for o, c in b.r.items():
            if o is not q and c > deps.get(o, 0):
                deps[o] = c
    _wait(q, deps)
    q.h.dma_start(out=out_ap, in_=in_ap).then_inc(dsem.sem, 16)
    dsem.count += 16
    for b in reads:
        b.r[dsem] = dsem.count
    for b in writes:
        b.w[dsem] = dsem.count


def build_program(dbg=False):
    nc = bass.Bass("TRN2", target_bir_lowering=False)

    def din(name, shape):
        return nc.dram_tensor(name, shape, F32, kind="ExternalInput").ap()

    def dout(name, shape):
        return nc.dram_tensor(name, shape, F32, kind="ExternalOutput").ap()

    xp_d = din("xp", [S, D])
    xs_d = din("xs", [SS, D])
    rows_d = din("rows", [NR, D])
    ident_d = din("ident", [128, 128])
    w1i_d = din("w1i", [D, 2 * DFF])
    w1o_d = din("w1o", [DFF, D])
    wi_d = din("wi", [D, 6 * D])
    wco_d = din("wco", [D, D])
    wro_d = din("wro", [D, D])
    wo_d = din("wo", [D, D])
    wga_d = din("wga", [8, 128, 128])
    wgx_d = din("wgx", [8, 128, 128])
    w2i_d = din("w2i", [D, 2 * DFF])
    w2o_d = din("w2o", [DFF, D])
    yp_d = dout("yp", [S, D])
    ys_d = dout("ys", [SS, D])
    so_d = dout("so", [NSO, D])
    dbg_d = dout("dbg", [16, 128, TGM]) if dbg else None

    es = ExitStack()
    with es:
        def sb(name, shape, dt):
            return es.enter_context(nc.sbuf_tensor(name, shape, dt))

        def sem(name):
            return es.enter_context(nc.semaphore(name))

        PE = Eng("pe", nc.tensor, sem("s_pe"))
        ACT = Eng("act", nc.scalar, sem("s_act"))
        DVE = Eng("dve", nc.vector, sem("s_dve"))
        POOL = Eng("pool", nc.gpsimd, sem("s_pool"))
        SP = Eng("sp", nc.sync, sem("s_sp"))

        x = sb("x", [128, KC, TGM], F32)
        xg = sb("xg", [128, KC, TGM], BF16)
        mg = sb("mg", [128, KC, TGM], BF16)
        ffo = sb("ffo", [128, KC, TGM], F32)
        act = sb("act", [128, FC, TGM], BF16)
        gh = act[:, 0:8, :]
        cvn = act[:, 8:16, :]
        mg_b, act_b, gh_b, cvn_b = Buf("mg"), Buf("act"), Buf("gh"), Buf("cvn")
        xg_bs = [Buf(f"xg{c}") for c in range(KC)]
        x_bs = [Buf(f"x{c}") for c in range(KC)]
        ffo_bs = [Buf(f"ffo{c}") for c in range(KC)]
        rstd = sb("rstd", [128, TGM], F32)
        mean = sb("mean", [128, TGM], F32)
        rstd_b, mean_b = Buf("rstd"), Buf("mean")
        P = sb("P", [128, KC, NR], F32)
        P_b = Buf("P")
        c1 = sb("c1", [128, KC], F32)
        c1_b = Buf("c1")
        c1h = sb("c1h", [128, KC], F32)
        eps_t = sb("eps_t", [128, 1], F32)
        q_t = sb("q_t", [128, 1], F32)
        bh = sb("bh", [128, 2 * KC], F32)
        SO = sb("SO", [128, KC, NSO], F32)
        SO_b = Buf("SO")
        uh = sb("uh", [128, KC, 30], BF16)
        rh = sb("rh", [128, KC, 3], F32)
        hprev = sb("hprev", [128, KC], F32)
        uh_b, rh_b, hprev_b = Buf("uh"), Buf("rh"), Buf("hprev")
        ident = sb("identf", [128, 128], F32)
        identb = sb("identb", [128, 128], BF16)
        ones = sb("ones", [128, 128], BF16)
        const_b = Buf("const")
        NTF, NTB = 7, 4
        tf = [(sb(f"tf{i}", [128, TGM], F32), Buf(f"tf{i}")) for i in range(NTF)]
        tb = [(sb(f"tb{i}", [128, TGM], BF16), Buf(f"tb{i}")) for i in range(NTB)]
        ut = [(sb(f"ut{i}", [128, UW], BF16), Buf(f"ut{i}")) for i in range(KC)]
        rxi = [(sb(f"rxi{i}", [128, RXW], F32), Buf(f"rxi{i}")) for i in range(1)]
        cbqs = [(sb(f"cbq{i}", [128, 2, TGM], BF16), Buf(f"cbq{i}")) for i in range(2)]
        Dg = [(sb(f"Dg{i}", [128, CK, 128], BF16), Buf(f"Dg{i}")) for i in range(2)]
        stage = [(sb(f"stg{i}", [128, D], F32), Buf(f"stg{i}"), DSem(f"stg{i}", sem(f"d_stg{i}"))) for i in range(2)]
        ring = [(sb(f"wr{i}", [128, 2048], BF16), Buf(f"wr{i}"), DSem(f"wr{i}", sem(f"d_wr{i}"))) for i in range(NSLOT)]
        banks = [(es.enter_context(nc.psum_tensor(f"bk{i}", [128, 512], F32)), Buf(f"bk{i}")) for i in range(8)]
        misc_ds = DSem("misc", sem("d_misc"))
        dbg_dss = [DSem(f"dbg{i}", sem(f"d_dbg{i}")) for i in range(16)] if dbg else []

        def dump(i, ap, buf, n=TGM):
            if dbg:
                dma(SP, dbg_dss[i], dbg_d[i, :, :n], ap, reads=[buf])

        cnt = {"tf": 0, "tb": 0, "ut": 0, "rxi": 0, "dg": 0, "stg": 0, "bank": 0, "ldt": 0}
        load_pend = []
        pool = list(range(8))

        def rr(key, lst):
            i = cnt[key] % len(lst)
            cnt[key] += 1
            return lst[i]

        def tmpf():
            return rr("tf", tf)

        def tmpb():
            return rr("tb", tb)

        def bank():
            i = pool[cnt["bank"] % len(pool)]
            cnt["bank"] += 1
            return banks[i]

        def take_banks(n):
            out = [banks[pool.pop()] for _ in range(n)]
            return out

        def give_banks(bs):
            for b in bs:
                pool.append([i for i in range(8) if banks[i] is b][0])

        def pe_mm(out_ap, pairs, reads, writes, start=True, stop=True):
            def fn():
                n = len(pairs)
                ins = None
                for i, (l, r) in enumerate(pairs):
                    ins = nc.tensor.matmul(out_ap, lhsT=l, rhs=r, start=(start and i == 0), stop=(stop and i == n - 1))
                return ins
            op(PE, fn, reads, writes)

        def pe_tr(out_ap, in_ap, idn, reads, writes):
            op(PE, lambda: nc.tensor.transpose(out=out_ap, in_=in_ap, identity=idn), reads, writes)

        def a_act(out_ap, in_ap, func, reads, writes, scale=None, bias=None):
            kw = {}
            if scale is not None:
                kw["scale"] = scale
            if bias is not None:
                kw["bias"] = bias
            op(ACT, lambda: nc.scalar.activation(out=out_ap, in_=in_ap, func=func, **kw), reads, writes)

        def v_tt(out_ap, in0, in1, alu, reads, writes):
            op(DVE, lambda: nc.vector.tensor_tensor(out=out_ap, in0=in0, in1=in1, op=alu), reads, writes)

        def v_stt(out_ap, in0, scalar, in1, op0, op1, reads, writes):
            op(DVE, lambda: nc.vector.scalar_tensor_tensor(out=out_ap, in0=in0, scalar=scalar, in1=in1, op0=op0, op1=op1), reads, writes)

        def v_ts(out_ap, in0, s1, s2, op0, op1, reads, writes):
            op(DVE, lambda: nc.vector.tensor_scalar(out=out_ap, in0=in0, scalar1=s1, scalar2=s2, op0=op0, op1=op1), reads, writes)

        def v_copy(out_ap, in_ap, reads, writes):
            op(DVE, lambda: nc.vector.tensor_copy(out=out_ap, in_=in_ap), reads, writes)

        def v_recip(out_ap, in_ap, reads, writes):
            op(DVE, lambda: nc.vector.reciprocal(out=out_ap, in_=in_ap), reads, writes)

        def prm(c, r):
            return P[:, c, r:r + 1]

        def wv(w, kc):
            return w.rearrange("(kc p) n -> p kc n", p=128)

        units = []

        def ffn_units(wi_, wo_):
            wiv = wv(wi_, KC)
            wov = wv(wo_, FC)
            for m in range(FC):
                units.append([(0, KC, wiv[:, :, m * 128:(m + 1) * 128]),
                              (1024, KC, wiv[:, :, DFF + m * 128:DFF + (m + 1) * 128])])
            for o in range(KC):
                for h in range(2):
                    units.append([(0, 11, wov[:, 11 * h:11 * h + 11, o * 128:(o + 1) * 128])])

        def mixer_units():
            wiv = wv(wi_d, KC)
            for c in range(KC):
                units.append([(0, KC, wiv[:, :, c * 128:(c + 1) * 128]),
                              (1024, KC, wiv[:, :, D + c * 128:D + (c + 1) * 128])])
            for c in range(KC):
                units.append([(0, KC, wiv[:, :, 2 * D + c * 128:2 * D + (c + 1) * 128]),
                              (1024, KC, wiv[:, :, 3 * D + c * 128:3 * D + (c + 1) * 128])])
            wcov, wrov, wov = wv(wco_d, KC), wv(wro_d, KC), wv(wo_d, KC)
            for o in range(KC):
                units.append([(0, KC, wiv[:, :, 4 * D + o * 128:4 * D + (o + 1) * 128]),
                              (1024, KC, wiv[:, :, 5 * D + o * 128:5 * D + (o + 1) * 128])])
                units.append([(0, KC, wcov[:, :, o * 128:(o + 1) * 128]),
                              (1024, KC, wrov[:, :, o * 128:(o + 1) * 128])])
            for oo in range(4):
                units.append([(0, KC, wov[:, :, (2 * oo) * 128:(2 * oo + 1) * 128]),
                              (1024, KC, wov[:, :, (2 * oo + 1) * 128:(2 * oo + 2) * 128])])

        for _ in GROUPS:
            ffn_units(w1i_d, w1o_d)
            mixer_units()
            ffn_units(w2i_d, w2o_d)

        wstate = {"issued": 0, "used": 0}

        def issue_unit():
            u = wstate["issued"]
            if u >= len(units):
                return
            t, b, ds = ring[u % NSLOT]
            for (off, a, src) in units[u]:
                dst = t[:, off:off + a * 128].rearrange("p (a b) -> p a b", a=a)
                dma(POOL, ds, dst, src, writes=[b])
            wstate["issued"] += 1

        def next_unit():
            u = wstate["used"]
            while wstate["issued"] < min(len(units), u + NSLOT - 1):
                issue_unit()
            wstate["used"] += 1
            return ring[u % NSLOT]

        gw = sb("gw", [128, 2048], BF16)
        gwb = Buf("gw")
        gw_ds = DSem("gw", sem("d_gw"))
        dma(POOL, gw_ds, gw[:, 0:1024].rearrange("p (a b) -> p a b", a=8), wga_d.rearrange("n h k -> h n k"), writes=[gwb])
        dma(POOL, gw_ds, gw[:, 1024:2048].rearrange("p (a b) -> p a b", a=8), wgx_d.rearrange("n h k -> h n k"), writes=[gwb])
        dma(SP, misc_ds, ident[:], ident_d, writes=[const_b])
        st_t, st_b, st_ds = stage[0]
        dma(SP, st_ds, st_t[:NR, :], rows_d, writes=[st_b])
        op(DVE, lambda: nc.vector.memset(ones[:], 1.0), writes=[const_b])
        v_copy(identb[:], ident[:], [const_b], [const_b])
        op(DVE, lambda: nc.vector.memset(uh[:], 0.0), writes=[uh_b])
        op(DVE, lambda: nc.vector.memset(rh[:], 0.0), writes=[rh_b])
        op(DVE, lambda: nc.vector.memset(hprev[:], 0.0), writes=[hprev_b])
        for half in range(2):
            bt, bb = bank()
            for q in range(4):
                kc = half * 4 + q
                pe_tr(bt[:, q * 128:q * 128 + NR], st_t[:NR, kc * 128:(kc + 1) * 128], ident[:NR, :NR], [st_b, const_b], [bb])
            a_act(P[:, half * 4:half * 4 + 4, :], bt[:, :].rearrange("p (q n) -> p q n", q=4)[:, :, :NR], AF.Copy, [bb], [P_b])
        t1, t1b = tmpf()
        a_act(t1[:, 0:KC], P[:, :, R_LAM], AF.Exp, [P_b], [t1b], scale=-1.0)
        a_act(t1[:, 8:8 + KC], t1[:, 0:KC], AF.Ln, [t1b], [t1b], bias=1.0)
        v_ts(c1[:, :], t1[:, 8:8 + KC], -8.0, None, ALU.mult, ALU.bypass, [t1b], [c1_b])
        v_ts(c1h[:, :], t1[:, 8:8 + KC], -4.0, None, ALU.mult, ALU.bypass, [t1b], [c1_b])
        op(DVE, lambda: nc.vector.memset(eps_t[:], EPS), writes=[c1_b])
        op(DVE, lambda: nc.vector.memset(q_t[:], 0.25), writes=[c1_b])
        v_ts(bh[:, 0:KC], P[:, :, R_BA], 0.5, None, ALU.mult, ALU.bypass, [P_b], [c1_b])
        v_ts(bh[:, KC:2 * KC], P[:, :, R_BX], 0.5, None, ALU.mult, ALU.bypass, [P_b], [c1_b])

        def stats_mm(sbank, nb, src_ap, src_b, first, last):
            st, sbuf_ = sbank
            pe_mm(st[:, :nb], [(ones[:, :], src_ap)], [const_b, src_b], [sbuf_], start=first, stop=last)

        def rstd_from(sbank, b0, nb, f, wb=None):
            st, sbuf_ = sbank
            t, tb_ = tmpf()
            a_act(t[:, :nb], st[:, :nb], AF.Ln, [sbuf_, c1_b], [tb_], scale=1.0 / D, bias=eps_t[:, 0:1])
            a_act(rstd[:, b0:b0 + nb], t[:, :nb], AF.Exp, [tb_], [rstd_b] if wb is None else wb, scale=-0.5, bias=float(np.log(f)))

        def rstd_multi(items, f):
            tmps = []
            for (sbank, b0, nb) in items:
                st, sbuf_ = sbank
                t, tb_ = tmpf()
                tmps.append((t, tb_))
                a_act(t[:, :nb], st[:, :nb], AF.Ln, [sbuf_, c1_b], [tb_], scale=1.0 / D, bias=eps_t[:, 0:1])
            for (sbank, b0, nb), (t, tb_) in zip(items, tmps):
                a_act(rstd[:, b0:b0 + nb], t[:, :nb], AF.Exp, [tb_], [rstd_b], scale=-0.5, bias=float(np.log(f)))

        def prenorm(TG, blocks, grow, out_t, out_b):
            sbk = take_banks(len(blocks))
            for c in range(KC):
                q, qb = tmpb()
                a_act(q[:, :TG], x[:, c, :TG], AF.Square, [x_bs[c]], [qb])
                for bi, (b0, nb) in enumerate(blocks):
                    stats_mm(sbk[bi], nb, q[:, b0:b0 + nb], qb, c == 0, c == KC - 1)
            rstd_multi([(sbk[bi], b0, nb) for bi, (b0, nb) in enumerate(blocks)], 1.0)
            give_banks(sbk)
            for c in range(KC):
                v_stt(out_t[:, c, :TG], x[:, c, :TG], prm(c, grow), rstd[:, :TG], ALU.mult, ALU.mult, [x_bs[c], P_b, rstd_b], [out_b[c] if isinstance(out_b, list) else out_b])

        def postnorm(TG, blocks, sbk, grow, f):
            rstd_multi([(sbk[bi], b0, nb) for bi, (b0, nb) in enumerate(blocks)], f)
            def p1(c):
                v_tt(ffo[:, c, :TG], ffo[:, c, :TG], rstd[:, :TG], ALU.mult, [ffo_bs[c], rstd_b], [ffo_bs[c]])

            p1(0)
            for c in range(KC):
                if c + 1 < KC:
                    p1(c + 1)
                v_stt(x[:, c, :TG], ffo[:, c, :TG], prm(c, grow), x[:, c, :TG], ALU.mult, ALU.add, [ffo_bs[c], P_b, x_bs[c]], [x_bs[c]])

        def first_unit(wt, wb, blocks):
            res = [(bank(), bank()) for _ in blocks]
            for kc in range(KC):
                for (b0, nb), ((a_t, a_b), (b_t, b_b)) in zip(blocks, res):
                    pe_mm(a_t[:, :nb], [(wt[:, kc * 128:(kc + 1) * 128], xg[:, kc, b0:b0 + nb])], [wb, xg_bs[kc]], [a_b],
                          start=(kc == 0), stop=(kc == KC - 1))
                    pe_mm(b_t[:, :nb], [(wt[:, 1024 + kc * 128:1024 + (kc + 1) * 128], xg[:, kc, b0:b0 + nb])], [wb, xg_bs[kc]], [b_b],
                          start=(kc == 0), stop=(kc == KC - 1))
            return res

        def ffn(TG, blocks, grow_pre, grow_post, do_prenorm=True, bg=None):
            if do_prenorm:
                prenorm(TG, blocks, grow_pre, xg, xg_bs)
            for m in range(FC):
                wt, wb, _ = next_unit()
                pre = first_unit(wt, wb, blocks) if m == 0 else None
                for bi, (b0, nb) in enumerate(blocks):
                    if pre is not None:
                        (gt, gb), (upt, upb) = pre[bi]
                    else:
                        gt, gb = bank()
                        upt, upb = bank()
                        pe_mm(gt[:, :nb], [(wt[:, kc * 128:(kc + 1) * 128], xg[:, kc, b0:b0 + nb]) for kc in range(KC)], [wb] + xg_bs, [gb])
                        pe_mm(upt[:, :nb], [(wt[:, 1024 + kc * 128:1024 + (kc + 1) * 128], xg[:, kc, b0:b0 + nb]) for kc in range(KC)], [wb] + xg_bs, [upb])
                    t, tb_ = tmpf()
                    a_act(t[:, :nb], gt[:, :nb], AF.Silu, [gb], [tb_])
                    v_tt(act[:, m, b0:b0 + nb], upt[:, :nb], t[:, :nb], ALU.mult, [upb, tb_], [act_b, gh_b, cvn_b])
                if bg and m >= 1 and m % 2 == 1:
                    bg.pop(0)()
            while bg:
                bg.pop(0)()
            sbk = take_banks(len(blocks))
            pending = []
            for o in range(KC):
                w0 = next_unit()
                w1 = next_unit()
                for bi, (b0, nb) in enumerate(blocks):
                    bt, bb = bank()
                    pairs = []
                    for kc in range(FC):
                        wt = (w0 if kc < 11 else w1)[0]
                        kk = kc % 11
                        pairs.append((wt[:, kk * 128:(kk + 1) * 128], act[:, kc, b0:b0 + nb]))
                    pe_mm(bt[:, :nb], pairs, [w0[1], w1[1], act_b, gh_b, cvn_b], [bb])
                    for fn_ in pending:
                        fn_()
                    pending = []
                    a_act(ffo[:, o, b0:b0 + nb], bt[:, :nb], AF.Copy, [bb], [ffo_bs[o]])
                    q, qb = tmpb()
                    a_act(q[:, :nb], bt[:, :nb], AF.Square, [bb], [qb])
                    pending.append(lambda bi=bi, nb=nb, q=q, qb=qb, o=o: stats_mm(sbk[bi], nb, q[:, :nb], qb, o == 0, o == KC - 1))
            for fn_ in pending:
                fn_()
            postnorm(TG, blocks, sbk, grow_post, 0.5)
            give_banks(sbk)

        def mixer(TG, TGp, has_s, blocks, last):
            prenorm(TG, blocks, R_GMPRE, xg, xg_bs)

            def split(b0, nb):
                npr = min(nb, TGp - b0)
                return npr, nb - npr

            def build_dg(c):
                dg, dgb = Dg[c % 2]
                v_tt(dg[:, :, :], identb[:, :].unsqueeze(1).broadcast_to([128, CK, 128]),
                     P[:, c, R_WDW:R_WDW + CK].unsqueeze(2).broadcast_to([128, CK, 128]), ALU.mult, [const_b, P_b], [dgb])

            for c in range(KC):
                if c == 3:
                    build_dg(0)
                if c == 5:
                    build_dg(1)
                wt, wb, _ = next_unit()
                u, ub = ut[c]
                v_copy(u[:, 0:30], uh[:, c, :], [uh_b], [ub])
                if has_s:
                    v_copy(u[:, U_S:U_S + 30], P[:, c, R_SCONV:R_SCONV + 30], [P_b], [ub])
                pre = first_unit(wt, wb, blocks) if c == 0 else None
                for bi, (b0, nb) in enumerate(blocks):
                    npr, nsm = split(b0, nb)
                    if pre is not None:
                        (vt, vb), (gt, gb) = pre[bi]
                    else:
                        vt, vb = bank()
                        gt, gb = bank()
                        pe_mm(vt[:, :nb], [(wt[:, kc * 128:(kc + 1) * 128], xg[:, kc, b0:b0 + nb]) for kc in range(KC)], [wb] + xg_bs, [vb])
                        pe_mm(gt[:, :nb], [(wt[:, 1024 + kc * 128:1024 + (kc + 1) * 128], xg[:, kc, b0:b0 + nb]) for kc in range(KC)], [wb] + xg_bs, [gb])
                    t, tb_ = tmpf()
                    a_act(t[:, :nb], gt[:, :nb], AF.Sigmoid, [gb], [tb_])
                    v_tt(u[:, 30 + b0:30 + b0 + npr], vt[:, :npr], t[:, :npr], ALU.mult, [vb, tb_], [ub])
                    if nsm:
                        v_tt(u[:, U_S + 30:U_S + 30 + SS], vt[:, npr:nb], t[:, npr:nb], ALU.mult, [vb, tb_], [ub])
                    if last and bi == len(blocks) - 1:
                        v_tt(SO[:, c, O_CP:O_CP + 30], vt[:, npr - 30:npr], t[:, npr - 30:npr], ALU.mult, [vb, tb_], [SO_b])
                        v_tt(SO[:, c, O_CS:O_CS + 30], vt[:, npr + 2:npr + 32], t[:, npr + 2:npr + 32], ALU.mult, [vb, tb_], [SO_b])
                v_copy(uh[:, c, :], u[:, TGp:TGp + 30], [ub], [uh_b])

            spieces = [(q0, min(256, TG - q0)) for q0 in range(0, TG, 256)]
            sPQ = take_banks(len(spieces))
            def ln_stats(c):
                cbq, cbq_b = cbqs[c % 2]
                for pi, (q0, w) in enumerate(spieces):
                    st, stb = sPQ[pi]
                    pe_mm(st[:, 0:2 * w].rearrange("p (a b) -> p a b", a=2), [(ones[:, :], cbq[:, :, q0:q0 + w])], [const_b, cbq_b], [stb],
                          start=(c == 0), stop=(c == KC - 1))

            st8 = [dict() for _ in range(KC)]

            def H1(c):
                d = st8[c]
                wt, wb, _ = next_unit()
                ri, rib = rr("rxi", rxi)
                d["ri"] = (ri, rib)
                v_copy(ri[:, 0:3], rh[:, c, :], [rh_b], [rib])
                if has_s:
                    v_copy(ri[:, RX_S:RX_S + 3], P[:, c, R_SRC:R_SRC + 3], [P_b], [rib])
                rg, rgb = tmpb()
                d["rg"] = (rg, rgb)
                for bi, (b0, nb) in enumerate(blocks):
                    npr, nsm = split(b0, nb)
                    xt_, xb_ = bank()
                    gt, gb = bank()
                    pe_mm(xt_[:, :nb], [(wt[:, kc * 128:(kc + 1) * 128], xg[:, kc, b0:b0 + nb]) for kc in range(KC)], [wb] + xg_bs, [xb_])
                    pe_mm(gt[:, :nb], [(wt[:, 1024 + kc * 128:1024 + (kc + 1) * 128], xg[:, kc, b0:b0 + nb]) for kc in range(KC)], [wb] + xg_bs, [gb])
                    v_copy(ri[:, 3 + b0:3 + b0 + npr], xt_[:, :npr], [xb_], [rib])
                    if nsm:
                        v_copy(ri[:, RX_S + 3:RX_S + 3 + SS], xt_[:, npr:nb], [xb_], [rib])
                    a_act(rg[:, b0:b0 + nb], gt[:, :nb], AF.Gelu_apprx_tanh, [gb], [rgb])

            def H2(c):
                d = st8[c]
                ri, rib = d["ri"]
                v_copy(rh[:, c, :], ri[:, TGp:TGp + 3], [rib], [rh_b])
                if last:
                    v_copy(SO[:, c, O_RP:O_RP + 3], ri[:, TGp:TGp + 3], [rib], [SO_b])
                    v_copy(SO[:, c, O_RS:O_RS + 3], ri[:, RX_S + SS:RX_S + SS + 3], [rib], [SO_b])
                rx, rxb = tmpf()
                d["rx"] = (rx, rxb)
                segs = [(0, 0, TGp)] + ([(RX_S, TGp, SS)] if has_s else [])
                for (src0, dst0, n) in segs:
                    v_ts(rx[:, dst0:dst0 + n], ri[:, src0:src0 + n], prm(c, R_WRC), prm(c, R_BRC), ALU.mult, ALU.add, [rib, P_b], [rxb])
                    for k in range(1, RK):
                        v_stt(rx[:, dst0:dst0 + n], ri[:, src0 + k:src0 + k + n], prm(c, R_WRC + k), rx[:, dst0:dst0 + n], ALU.mult, ALU.add, [rib, P_b, rxb], [rxb])
                rxq, rxqb = tmpb()
                d["rxq"] = (rxq, rxqb)
                a_act(rxq[:, :TG], rx[:, :TG], AF.Copy, [rxb], [rxqb])

            def H3(c):
                u, ub = ut[c]
                dg, dgb = Dg[c % 2]
                for bi, (b0, nb) in enumerate(blocks):
                    npr, nsm = split(b0, nb)
                    bt, bb = bank()
                    pe_mm(bt[:, :npr], [(dg[:, k, :], u[:, b0 + k:b0 + k + npr]) for k in range(CK)], [dgb, ub], [bb])
                    if nsm:
                        pe_mm(bt[:, npr:nb], [(dg[:, k, :], u[:, U_S + k:U_S + k + SS]) for k in range(CK)], [dgb, ub], [bb])
                    a_act(ffo[:, c, b0:b0 + nb], bt[:, :nb], AF.Identity, [bb, P_b], [ffo_bs[c]], bias=prm(c, R_BDW))
                if c + 2 < KC:
                    build_dg(c + 2)

            def H4(c):
                d = st8[c]
                rxq, rxqb = d["rxq"]
                rt, rtb = tmpf()
                it, itb = tmpf()
                d["rt"], d["it"] = (rt, rtb), (it, itb)
                for bi, (b0, nb) in enumerate(blocks):
                    rp, rpb = bank()
                    ip, ipb = bank()
                    pe_mm(rp[:, :nb], [(gw[:, c * 128:(c + 1) * 128], rxq[:, b0:b0 + nb])], [gwb, rxqb], [rpb])
                    pe_mm(ip[:, :nb], [(gw[:, 1024 + c * 128:1024 + (c + 1) * 128], rxq[:, b0:b0 + nb])], [gwb, rxqb], [ipb])
                    a_act(rt[:, b0:b0 + nb], rp[:, :nb], AF.Tanh, [rpb, c1_b], [rtb], scale=0.5, bias=bh[:, c:c + 1])
                    a_act(it[:, b0:b0 + nb], ip[:, :nb], AF.Tanh, [ipb, c1_b], [itb], scale=0.5, bias=bh[:, KC + c:KC + c + 1])
                if c >= 1:
                    ln_stats(c - 1)
                cbq, cbq_b = cbqs[c % 2]
                a_act(cbq[:, 0, :TG], ffo[:, c, :TG], AF.Copy, [ffo_bs[c]], [cbq_b])
                a_act(cbq[:, 1, :TG], ffo[:, c, :TG], AF.Square, [ffo_bs[c]], [cbq_b])

            def T1(c):
                d = st8[c]
                rt, rtb = d["rt"]
                at, atb = tmpf()
                d["at"] = (at, atb)
                a_act(at[:, :TG], rt[:, :TG], AF.Exp, [rtb, c1_b], [atb], scale=c1h[:, c:c + 1], bias=c1h[:, c:c + 1])
                a_act(rt[:, :TG], at[:, :TG], AF.Square, [atb], [rtb])
                a_act(rt[:, :TG], rt[:, :TG], AF.Sqrt, [rtb, c1_b], [rtb], scale=-0.25, bias=q_t[:, 0:1])

            def T2(c):
                d = st8[c]
                rt, rtb = d["rt"]
                it, itb = d["it"]
                at, atb = d["at"]
                rx, rxb = d["rx"]
                rg, rgb = d["rg"]
                ri, rib = d["ri"]
                v_stt(it[:, :TG], it[:, :TG], 1.0, rx[:, :TG], ALU.add, ALU.mult, [itb, rxb], [itb])
                v_tt(it[:, :TG], it[:, :TG], rt[:, :TG], ALU.mult, [itb, rtb], [itb])
                hs, hsb = tmpf()
                op(DVE, lambda: nc.vector.tensor_tensor_scan(out=hs[:, :TGp], data0=at[:, :TGp], data1=it[:, :TGp],
                                                             initial=hprev[:, c:c + 1], op0=ALU.mult, op1=ALU.add),
                   [atb, itb, hprev_b], [hsb])
                if has_s:
                    op(DVE, lambda: nc.vector.tensor_tensor_scan(out=hs[:, TGp:TG], data0=at[:, TGp:TG], data1=it[:, TGp:TG],
                                                                 initial=prm(c, R_SH), op0=ALU.mult, op1=ALU.add),
                       [atb, itb, P_b], [hsb])
                if dbg and c == int(os.environ.get('DBGC', '0')) and not has_s and TGp == 704 and cnt.get("dumped") is None:
                    cnt["dumped"] = 1
                    dump(0, rx[:, :], rxb); dump(1, rt[:, :], rtb); dump(2, it[:, :], itb); dump(3, at[:, :], atb); dump(4, hs[:, :], hsb)
                    dump(5, c1[:, :], c1_b, KC); dump(7, ffo[:, c, :], ffo_bs[c]); dump(8, x[:, c, :], x_bs[c])
                v_copy(hprev[:, c:c + 1], hs[:, TGp - 1:TGp], [hsb], [hprev_b])
                if last:
                    v_copy(SO[:, c, O_HP:O_HP + 1], hs[:, TGp - 1:TGp], [hsb], [SO_b])
                    v_copy(SO[:, c, O_HS:O_HS + 1], hs[:, TG - 1:TG], [hsb], [SO_b])
                v_tt(gh[:, c, :TG], hs[:, :TG], rg[:, :TG], ALU.mult, [hsb, rgb], [gh_b])
                st8[c].clear()

            H1(0); H2(0); H3(0); H4(0)
            for c in range(KC - 1):
                H1(c + 1)
                T1(c)
                H2(c + 1)
                T2(c)
                H3(c + 1)
                H4(c + 1)
            ln_stats(KC - 1)

            pcs = [(sPQ[pi][0], sPQ[pi][1], q0, w, rstd[:, q0:q0 + w]) for pi, (q0, w) in enumerate(spieces)]
            for st, stb, q0, w, rs in pcs:
                a_act(mean[:, q0:q0 + w], st[:, 0:w], AF.Copy, [stb], [mean_b], scale=1.0 / D)
            for st, stb, q0, w, rs in pcs:
                v_tt(rs, mean[:, q0:q0 + w], mean[:, q0:q0 + w], ALU.mult, [mean_b], [rstd_b])
            for st, stb, q0, w, rs in pcs:
                v_stt(rs, st[:, w:2 * w], 1.0 / D, rs, ALU.mult, ALU.subtract, [stb, rstd_b], [rstd_b])
            for st, stb, q0, w, rs in pcs:
                a_act(rs, rs, AF.Ln, [rstd_b, c1_b], [rstd_b], bias=eps_t[:, 0:1])
            for st, stb, q0, w, rs in pcs:
                a_act(rs, rs, AF.Exp, [rstd_b], [rstd_b], scale=-0.5)
            give_banks(sPQ)
            if dbg and cnt.get("dumped3") is None:
                cnt["dumped3"] = 1
                dump(12, mean[:, :], mean_b); dump(13, rstd[:, :], rstd_b)

            def ln_sub(c):
                v_tt(ffo[:, c, :TG], ffo[:, c, :TG], mean[:, :TG], ALU.subtract, [ffo_bs[c], mean_b], [ffo_bs[c]])

            def ln_rest(c):
                v_tt(ffo[:, c, :TG], ffo[:, c, :TG], rstd[:, :TG], ALU.mult, [ffo_bs[c], rstd_b], [ffo_bs[c]])
                a_act(cvn[:, c, :TG], ffo[:, c, :TG], AF.Silu, [ffo_bs[c], P_b], [cvn_b], scale=prm(c, R_LNG), bias=prm(c, R_LNB))

            def ln_range(c0_, c1_):
                ln_sub(c0_)
                for c in range(c0_, c1_):
                    if c + 1 < c1_:
                        ln_sub(c + 1)
                    ln_rest(c)

            T1(KC - 1)
            ln_range(0, 4)
            T2(KC - 1)
            ln_range(4, KC)

            for o in range(KC):
                w1 = next_unit()
                w2 = next_unit()
                srcs = [(w1, 0, xg, xg_bs), (w1, 1024, xg, xg_bs), (w2, 0, cvn, [cvn_b]), (w2, 1024, gh, [gh_b])]
                pss = [[bank() for _ in range(4)] for _ in blocks]
                order = ([(bi, j) for bi in range(len(blocks)) for j in (0, 1, 3)] + [(bi, 2) for bi in range(len(blocks))]) if o == 0 \
                    else [(bi, j) for bi in range(len(blocks)) for j in range(4)]
                for bi, j in order:
                    b0, nb = blocks[bi]
                    pt, pb = pss[bi][j]
                    wu, off, rhs_t, rhs_b = srcs[j]
                    pe_mm(pt[:, :nb], [(wu[0][:, off + kc * 128:off + (kc + 1) * 128], rhs_t[:, kc, b0:b0 + nb]) for kc in range(KC)],
                          [wu[1]] + rhs_b, [pb])
                for bi, (b0, nb) in enumerate(blocks):
                    ps = pss[bi]
                    ta, tab = tmpf()
                    tr_, trb = tmpf()
                    a_act(ta[:, :nb], ps[0][0][:, :nb], AF.Sigmoid, [ps[0][1]], [tab])
                    a_act(tr_[:, :nb], ps[1][0][:, :nb], AF.Sigmoid, [ps[1][1]], [trb])
                    v_tt(ta[:, :nb], ps[2][0][:, :nb], ta[:, :nb], ALU.mult, [ps[2][1], tab], [tab])
                    v_tt(tr_[:, :nb], ps[3][0][:, :nb], tr_[:, :nb], ALU.mult, [ps[3][1], trb], [trb])
                    v_tt(mg[:, o, b0:b0 + nb], ta[:, :nb], tr_[:, :nb], ALU.add, [tab, trb], [mg_b])

            sbk = take_banks(len(blocks))
            pending = []
            for oo in range(4):
                wu = next_unit()
                for o in (2 * oo, 2 * oo + 1):
                    off = (o % 2) * 1024
                    for bi, (b0, nb) in enumerate(blocks):
                        bt, bb = bank()
                        pe_mm(bt[:, :nb], [(wu[0][:, off + kc * 128:off + (kc + 1) * 128], mg[:, kc, b0:b0 + nb]) for kc in range(KC)],
                              [wu[1], mg_b], [bb])
                        for fn_ in pending:
                            fn_()
                        pending = []
                        a_act(ffo[:, o, b0:b0 + nb], bt[:, :nb], AF.Copy, [bb], [ffo_bs[o]])
                        q, qb = tmpb()
                        a_act(q[:, :nb], bt[:, :nb], AF.Square, [bb], [qb])
                        pending.append(lambda bi=bi, nb=nb, q=q, qb=qb, o=o: stats_mm(sbk[bi], nb, q[:, :nb], qb, o == 0, o == KC - 1))
            for fn_ in pending:
                fn_()
            postnorm(TG, blocks, sbk, R_GMPOST, 1.0)
            give_banks(sbk)

        def group_cfg(gi):
            p0, TGp, has_s = GROUPS[gi]
            TG = TGp + (SS if has_s else 0)
            h = TGp // 2
            return p0, TGp, has_s, TG, [(0, h), (h, TG - h)]

        def load_tile(src, r0, c0, nt):
            st_t, st_b, st_ds = rr("stg", stage)
            xt_b, rr_b = Buf("xt"), Buf("rr")
            dma(SP, st_ds, st_t[:nt, :], src[r0:r0 + nt, :], writes=[st_b])
            for half in range(2):
                bt, bb = bank()
                for q in range(4):
                    kc = half * 4 + q
                    pe_tr(bt[:, q * 128:q * 128 + nt], st_t[:nt, kc * 128:(kc + 1) * 128], ident[:nt, :nt], [st_b, const_b], [bb])
                a_act(x[:, half * 4:half * 4 + 4, c0:c0 + nt], bt[:, :].rearrange("p (q n) -> p q n", q=4)[:, :, :nt], AF.Copy, [bb],
                      x_bs[half * 4:half * 4 + 4] + [xt_b])
            cq_t, cq_b = cbqs[cnt["ldt"] % 2]
            cnt["ldt"] += 1
            q3 = cq_t[:, :, :].rearrange("p a b -> p (a b)")[:, 0:KC * 128].rearrange("p (k n) -> p k n", k=KC)[:, :, :nt]
            a_act(q3, x[:, :, c0:c0 + nt], AF.Square, [xt_b], [cq_b])
            for fn_ in load_pend:
                fn_()
            load_pend.clear()

            def rest(c0=c0, nt=nt, q3=q3, cq_b=cq_b, xt_b=xt_b, rr_b=rr_b):
                sbank = bank()
                for kc in range(KC):
                    stats_mm(sbank, nt, q3[:, kc, :], cq_b, kc == 0, kc == KC - 1)
                rstd_from(sbank, c0, nt, 1.0, wb=[rr_b, rstd_b])
                for c in range(KC):
                    v_stt(xg[:, c, c0:c0 + nt], x[:, c, c0:c0 + nt], prm(c, R_G1PRE), rstd[:, c0:c0 + nt], ALU.mult, ALU.mult,
                          [xt_b, P_b, rr_b], [xg_bs[c]])
            load_pend.append(rest)

        def flush_load():
            for fn_ in load_pend:
                fn_()
            load_pend.clear()
            for c in range(KC):
                x_bs[c].r[DVE] = DVE.count
                x_bs[c].r[ACT] = ACT.count
            rstd_b.r[DVE] = DVE.count

        def store_tile(dst, r0, c0, nt):
            st_t, st_b, st_ds = rr("stg", stage)
            for half in range(2):
                bt, bb = bank()
                for q in range(4):
                    kc = half * 4 + q
                    pe_tr(bt[:nt, q * 128:(q + 1) * 128], ffo[:, kc, c0:c0 + nt], ident[:, :], [ffo_bs[kc], const_b], [bb])
                a_act(st_t[:nt, half * 512:(half + 1) * 512], bt[:nt, :], AF.Copy, [bb], [st_b])
            dma(ACT, st_ds, dst[r0:r0 + nt, :], st_t[:nt, :], reads=[st_b])

        def tiles_of(gi, dp, ds_):
            p0, TGp, has_s, TG, blocks = group_cfg(gi)
            tl = [(dp, p0 + t0, t0, min(128, TGp - t0)) for t0 in range(0, TGp, 128)]
            if has_s:
                tl.append((ds_, 0, TGp, SS))
            return tl

        deferred = []
        for tl in tiles_of(0, xp_d, xs_d):
            load_tile(*tl)
        flush_load()
        for gi in range(len(GROUPS)):
            p0, TGp, has_s, TG, blocks = group_cfg(gi)
            last = gi == len(GROUPS) - 1
            ffn(TG, blocks, R_G1PRE, R_G1POST, do_prenorm=False, bg=deferred)
            mixer(TG, TGp, has_s, blocks, last)
            ffn(TG, blocks, R_G2PRE, R_G2POST)
            prenorm(TG, blocks, R_GFIN, ffo, ffo_bs)
            outs = tiles_of(gi, yp_d, ys_d)
            if last:
                for tl in outs:
                    store_tile(*tl)
            else:
                deferred.extend([(lambda tl=tl: store_tile(*tl)) for tl in outs])
                for tl in tiles_of(gi + 1, xp_d, xs_d):
                    load_tile(*tl)
                flush_load()

        st_t, st_b, st_ds = rr("stg", stage)
        for half in range(2):
            bt, bb = bank()
            for q in range(4):
                kc = half * 4 + q
                pe_tr(bt[:NSO, q * 128:(q + 1) * 128], SO[:, kc, :], ident[:, :], [SO_b, const_b], [bb])
            a_act(st_t[:NSO, half * 512:(half + 1) * 512], bt[:NSO, :], AF.Copy, [bb], [st_b])
        dma(SP, st_ds, so_d, st_t[:NSO, :], reads=[st_b])
        for (_, _, ds) in stage:
            nc.sync.wait_ge(ds.sem, ds.count)
        for ds in dbg_dss:
            if ds.count:
                nc.sync.wait_ge(ds.sem, ds.count)
    return nc


_CACHE = {}


def kernel(x_prompt, x_sample, state_conv, state_rconv, state_h,
           g_ffn1_pre, g_ffn1_post, w_ffn1_in, w_ffn1_out,
           g_mix_pre, g_mix_post, w_in,
           w_dw, b_dw, ln_g, ln_b, w_conv_out,
           w_rconv, b_rconv, w_rg_a, b_rg_a, w_rg_x, b_rg_x, lam, w_rnn_out,
           w_out,
           g_ffn2_pre, g_ffn2_post, w_ffn2_in, w_ffn2_out, g_final):
    f = lambda a: np.ascontiguousarray(np.asarray(a, dtype=np.float32))
    if "nc" not in _CACHE:
        _CACHE["nc"] = build_program()
    nc = _CACHE["nc"]
    vec_rows = [g_ffn1_pre, g_ffn1_post, g_mix_pre, g_mix_post, b_dw, ln_g, ln_b, b_rconv, b_rg_a, b_rg_x, lam,
                g_ffn2_pre, g_ffn2_post, g_final]
    common = np.concatenate([f(v)[0][None, :] for v in vec_rows] + [f(w_dw)[0], f(w_rconv)[0]], axis=0)
    shared = {
        "ident": np.eye(128, dtype=np.float32),
        "w1i": f(w_ffn1_in)[0], "w1o": f(w_ffn1_out)[0], "wi": f(w_in)[0],
        "wco": f(w_conv_out)[0], "wro": f(w_rnn_out)[0], "wo": f(w_out)[0],
        "wga": f(w_rg_a)[0], "wgx": f(w_rg_x)[0],
        "w2i": f(w_ffn2_in)[0], "w2o": f(w_ffn2_out)[0],
    }
    xp, xs = f(x_prompt), f(x_sample)
    sc, sr, sh = f(state_conv)[0], f(state_rconv)[0], f(state_h)[0]
    in_maps = []
    for i in range(NCORES):
        rows = np.ascontiguousarray(np.concatenate([common, sc[i], sr[i], sh[i][None, :]], axis=0))
        m = dict(shared)
        m.update({"xp": xp[i], "xs": xs[i], "rows": rows})
        in_maps.append(m)
    res = run_bass_kernel_spmd(nc, in_maps, core_ids=list(range(NCORES)))
    r = res.results
    yp = np.stack([r[i]["yp"] for i in range(NCORES)])
    ys = np.stack([r[i]["ys"] for i in range(NCORES)])
    so = np.stack([r[i]["so"] for i in range(NCORES)])
    return (yp.astype(np.float32), ys.astype(np.float32),
            so[None, :, O_CP:O_CP + 30, :], so[None, :, O_RP:O_RP + 3, :], so[None, :, O_HP, :],
            so[None, :, O_CS:O_CS + 30, :], so[None, :, O_RS:O_RS + 3, :], so[None, :, O_HS, :])
```

```python
import os
from contextlib import ExitStack

import numpy as np
import concourse.bass as bass
import concourse.mybir as mybir
from concourse.bass_utils import run_bass_kernel_spmd

F32 = mybir.dt.float32
BF16 = mybir.dt.bfloat16
AF = mybir.ActivationFunctionType
ALU = mybir.AluOpType

D = 1024
KC = 8
DFF = 2816
FC = 22
S = 2048
SS = 32
CK = 31
RK = 4
EPS = 1e-6
NCORES = 8

R_G1PRE, R_G1POST, R_GMPRE, R_GMPOST, R_BDW, R_LNG, R_LNB, R_BRC, R_BA, R_BX, R_LAM, R_G2PRE, R_G2POST, R_GFIN = range(14)
R_WDW = 14
R_WRC = R_WDW + CK
R_SCONV = R_WRC + RK
R_SRC = R_SCONV + 30
R_SH = R_SRC + 3
NR = R_SH + 1
O_CP, O_RP, O_HP, O_CS, O_RS, O_HS = 0, 30, 33, 34, 64, 67
NSO = 68

GROUPS = [(0, 704, False), (704, 704, False), (1408, 640, True)]
TGM = 704
U_S = 30 + TGM
UW = U_S + 30 + SS
RX_S = 3 + TGM
RXW = RX_S + 3 + SS
NSLOT = 6


class Eng:
    def __init__(self, name, h, sem):
        self.name, self.h, self.sem = name, h, sem
        self.count = 0
        self.seen = {}
        self.strict = name in ("act", "dve", "pool")


class DSem:
    def __init__(self, name, sem):
        self.name, self.sem = name, sem
        self.count = 0


class Buf:
    def __init__(self, name):
        self.name = name
        self.w = {}
        self.r = {}


def _deps(eng, reads, writes):
    deps = {}
    for b in reads:
        for o, c in b.w.items():
            if c > deps.get(o, 0):
                deps[o] = c
    for b in writes:
        for o, c in b.w.items():
            if (o is not eng or eng.strict) and c > deps.get(o, 0):
                deps[o] = c
        for o, c in b.r.items():
            if (o is not eng or eng.strict) and c > deps.get(o, 0):
                deps[o] = c
    return deps


def _wait(eng, deps):
    for o, c in deps.items():
        if c <= eng.seen.get(o, 0):
            continue
        eng.h.wait_ge(o.sem, c)
        eng.seen[o] = c


def op(eng, fn, reads=(), writes=()):
    _wait(eng, _deps(eng, reads, writes))
    inst = fn()
    eng.count += 1
    inst.then_inc(eng.sem, 1)
    for b in reads:
        b.r[eng] = eng.count
    for b in writes:
        b.w[eng] = eng.count


def dma(q, dsem, out_ap, in_ap, reads=(), writes=()):
    deps = {}
    for b in reads:
        for o, c in b.w.items():
            if c > deps.get(o, 0):
                deps[o] = c
    for b in writes:
        for o, c in b.w.items():
            if o is not q and o is not dsem and c > deps.get(o, 0):
                deps[o] = c
        for o, c in b.r.items():
            if o is not q and c > deps.get(o, 0):
                deps[o] = c
    _wait(q, deps)
    q.h.dma_start(out=out_ap, in_=in_ap).then_inc(dsem.sem, 16)
    dsem.count += 16
    for b in reads:
        b.r[dsem] = dsem.count
    for b in writes:
        b.w[dsem] = dsem.count


def build_program(dbg=False):
    nc = bass.Bass("TRN2", target_bir_lowering=False)

    def din(name, shape):
        return nc.dram_tensor(name, shape, F32, kind="ExternalInput").ap()

    def dout(name, shape):
        return nc.dram_tensor(name, shape, F32, kind="ExternalOutput").ap()

    xp_d = din("xp", [S, D])
    xs_d = din("xs", [SS, D])
    rows_d = din("rows", [NR, D])
    ident_d = din("ident", [128, 128])
    w1i_d = din("w1i", [D, 2 * DFF])
    w1o_d = din("w1o", [DFF, D])
    wi_d = din("wi", [D, 6 * D])
    wco_d = din("wco", [D, D])
    wro_d = din("wro", [D, D])
    wo_d = din("wo", [D, D])
    wga_d = din("wga", [8, 128, 128])
    wgx_d = din("wgx", [8, 128, 128])
    w2i_d = din("w2i", [D, 2 * DFF])
    w2o_d = din("w2o", [DFF, D])
    yp_d = dout("yp", [S, D])
    ys_d = dout("ys", [SS, D])
    so_d = dout("so", [NSO, D])
    dbg_d = dout("dbg", [16, 128, TGM]) if dbg else None

    es = ExitStack()
    with es:
        def sb(name, shape, dt):
            return es.enter_context(nc.sbuf_tensor(name, shape, dt))

        def sem(name):
            return es.enter_context(nc.semaphore(name))

        PE = Eng("pe", nc.tensor, sem("s_pe"))
        ACT = Eng("act", nc.scalar, sem("s_act"))
        DVE = Eng("dve", nc.vector, sem("s_dve"))
        POOL = Eng("pool", nc.gpsimd, sem("s_pool"))
        SP = Eng("sp", nc.sync, sem("s_sp"))

        x = sb("x", [128, KC, TGM], F32)
        xg = sb("xg", [128, KC, TGM], BF16)
        mg = sb("mg", [128, KC, TGM], BF16)
        ffo = sb("ffo", [128, KC, TGM], F32)
        act = sb("act", [128, FC, TGM], BF16)
        gh = act[:, 0:8, :]
        cvn = act[:, 8:16, :]
        mg_b, act_b, gh_b, cvn_b = Buf("mg"), Buf("act"), Buf("gh"), Buf("cvn")
        xg_bs = [Buf(f"xg{c}") for c in range(KC)]
        x_bs = [Buf(f"x{c}") for c in range(KC)]
        ffo_bs = [Buf(f"ffo{c}") for c in range(KC)]
        rstd = sb("rstd", [128, TGM], F32)
        mean = sb("mean", [128, TGM], F32)
        rstd_b, mean_b = Buf("rstd"), Buf("mean")
        P = sb("P", [128, KC, NR], F32)
        P_b = Buf("P")
        c1 = sb("c1", [128, KC], F32)
        c1_b = Buf("c1")
        c1h = sb("c1h", [128, KC], F32)
        eps_t = sb("eps_t", [128, 1], F32)
        q_t = sb("q_t", [128, 1], F32)
        bh = sb("bh", [128, 2 * KC], F32)
        SO = sb("SO", [128, KC, NSO], F32)
        SO_b = Buf("SO")
        uh = sb("uh", [128, KC, 30], BF16)
        rh = sb("rh", [128, KC, 3], F32)
        hprev = sb("hprev", [128, KC], F32)
        uh_b, rh_b, hprev_b = Buf("uh"), Buf("rh"), Buf("hprev")
        ident = sb("identf", [128, 128], F32)
        identb = sb("identb", [128, 128], BF16)
        ones = sb("ones", [128, 128], BF16)
        const_b = Buf("const")
        NTF, NTB = 7, 4
        tf = [(sb(f"tf{i}", [128, TGM], F32), Buf(f"tf{i}")) for i in range(NTF)]
        tb = [(sb(f"tb{i}", [128, TGM], BF16), Buf(f"tb{i}")) for i in range(NTB)]
        ut = [(sb(f"ut{i}", [128, UW], BF16), Buf(f"ut{i}")) for i in range(KC)]
        rxi = [(sb(f"rxi{i}", [128, RXW], F32), Buf(f"rxi{i}")) for i in range(1)]
        cbqs = [(sb(f"cbq{i}", [128, 2, TGM], BF16), Buf(f"cbq{i}")) for i in range(2)]
        Dg = [(sb(f"Dg{i}", [128, CK, 128], BF16), Buf(f"Dg{i}")) for i in range(2)]
        stage = [(sb(f"stg{i}", [128, D], F32), Buf(f"stg{i}"), DSem(f"stg{i}", sem(f"d_stg{i}"))) for i in range(2)]
        ring = [(sb(f"wr{i}", [128, 2048], BF16), Buf(f"wr{i}"), DSem(f"wr{i}", sem(f"d_wr{i}"))) for i in range(NSLOT)]
        banks = [(es.enter_context(nc.psum_tensor(f"bk{i}", [128, 512], F32)), Buf(f"bk{i}")) for i in range(8)]
        misc_ds = DSem("misc", sem("d_misc"))
        dbg_dss = [DSem(f"dbg{i}", sem(f"d_dbg{i}")) for i in range(16)] if dbg else []

        def dump(i, ap, buf, n=TGM):
            if dbg:
                dma(SP, dbg_dss[i], dbg_d[i, :, :n], ap, reads=[buf])

        cnt = {"tf": 0, "tb": 0, "ut": 0, "rxi": 0, "dg": 0, "stg": 0, "bank": 0, "ldt": 0}
        load_pend = []
        pool = list(range(8))

        def rr(key, lst):
            i = cnt[key] % len(lst)
            cnt[key] += 1
            return lst[i]

        def tmpf():
            return rr("tf", tf)

        def tmpb():
            return rr("tb", tb)

        def bank():
            i = pool[cnt["bank"] % len(pool)]
            cnt["bank"] += 1
            return banks[i]

        def take_banks(n):
            out = [banks[pool.pop()] for _ in range(n)]
            return out

        def give_banks(bs):
            for b in bs:
                pool.append([i for i in range(8) if banks[i] is b][0])

        def pe_mm(out_ap, pairs, reads, writes, start=True, stop=True):
            def fn():
                n = len(pairs)
                ins = None
                for i, (l, r) in enumerate(pairs):
                    ins = nc.tensor.matmul(out_ap, lhsT=l, rhs=r, start=(start and i == 0), stop=(stop and i == n - 1))
                return ins
            op(PE, fn, reads, writes)

        def pe_tr(out_ap, in_ap, idn, reads, writes):
            op(PE, lambda: nc.tensor.transpose(out=out_ap, in_=in_ap, identity=idn), reads, writes)

        def a_act(out_ap, in_ap, func, reads, writes, scale=None, bias=None):
            kw = {}
            if scale is not None:
                kw["scale"] = scale
            if bias is not None:
                kw["bias"] = bias
            op(ACT, lambda: nc.scalar.activation(out=out_ap, in_=in_ap, func=func, **kw), reads, writes)

        def v_tt(out_ap, in0, in1, alu, reads, writes):
            op(DVE, lambda: nc.vector.tensor_tensor(out=out_ap, in0=in0, in1=in1, op=alu), reads, writes)

        def v_stt(out_ap, in0, scalar, in1, op0, op1, reads, writes):
            op(DVE, lambda: nc.vector.scalar_tensor_tensor(out=out_ap, in0=in0, scalar=scalar, in1=in1, op0=op0, op1=op1), reads, writes)

        def v_ts(out_ap, in0, s1, s2, op0, op1, reads, writes):
            op(DVE, lambda: nc.vector.tensor_scalar(out=out_ap, in0=in0, scalar1=s1, scalar2=s2, op0=op0, op1=op1), reads, writes)

        def v_copy(out_ap, in_ap, reads, writes):
            op(DVE, lambda: nc.vector.tensor_copy(out=out_ap, in_=in_ap), reads, writes)

        def v_recip(out_ap, in_ap, reads, writes):
            op(DVE, lambda: nc.vector.reciprocal(out=out_ap, in_=in_ap), reads, writes)

        def prm(c, r):
            return P[:, c, r:r + 1]

        def wv(w, kc):
            return w.rearrange("(kc p) n -> p kc n", p=128)

        units = []

        def ffn_units(wi_, wo_):
            wiv = wv(wi_, KC)
            wov = wv(wo_, FC)
            for m in range(FC):
                units.append([(0, KC, wiv[:, :, m * 128:(m + 1) * 128]),
                              (1024, KC, wiv[:, :, DFF + m * 128:DFF + (m + 1) * 128])])
            for o in range(KC):
                for h in range(2):
                    units.append([(0, 11, wov[:, 11 * h:11 * h + 11, o * 128:(o + 1) * 128])])

        def mixer_units():
            wiv = wv(wi_d, KC)
            for c in range(KC):
                units.append([(0, KC, wiv[:, :, c * 128:(c + 1) * 128]),
                              (1024, KC, wiv[:, :, D + c * 128:D + (c + 1) * 128])])
            for c in range(KC):
                units.append([(0, KC, wiv[:, :, 2 * D + c * 128:2 * D + (c + 1) * 128]),
                              (1024, KC, wiv[:, :, 3 * D + c * 128:3 * D + (c + 1) * 128])])
            wcov, wrov, wov = wv(wco_d, KC), wv(wro_d, KC), wv(wo_d, KC)
            for o in range(KC):
                units.append([(0, KC, wiv[:, :, 4 * D + o * 128:4 * D + (o + 1) * 128]),
                              (1024, KC, wiv[:, :, 5 * D + o * 128:5 * D + (o + 1) * 128])])
                units.append([(0, KC, wcov[:, :, o * 128:(o + 1) * 128]),
                              (1024, KC, wrov[:, :, o * 128:(o + 1) * 128])])
            for oo in range(4):
                units.append([(0, KC, wov[:, :, (2 * oo) * 128:(2 * oo + 1) * 128]),
                              (1024, KC, wov[:, :, (2 * oo + 1) * 128:(2 * oo + 2) * 128])])

        for _ in GROUPS:
            ffn_units(w1i_d, w1o_d)
            mixer_units()
            ffn_units(w2i_d, w2o_d)

        wstate = {"issued": 0, "used": 0}

        def issue_unit():
            u = wstate["issued"]
            if u >= len(units):
                return
            t, b, ds = ring[u % NSLOT]
            for (off, a, src) in units[u]:
                dst = t[:, off:off + a * 128].rearrange("p (a b) -> p a b", a=a)
                dma(POOL, ds, dst, src, writes=[b])
            wstate["issued"] += 1

        def next_unit():
            u = wstate["used"]
            while wstate["issued"] < min(len(units), u + NSLOT - 1):
                issue_unit()
            wstate["used"] += 1
            return ring[u % NSLOT]

        gw = sb("gw", [128, 2048], BF16)
        gwb = Buf("gw")
        gw_ds = DSem("gw", sem("d_gw"))
        dma(POOL, gw_ds, gw[:, 0:1024].rearrange("p (a b) -> p a b", a=8), wga_d.rearrange("n h k -> h n k"), writes=[gwb])
        dma(POOL, gw_ds, gw[:, 1024:2048].rearrange("p (a b) -> p a b", a=8), wgx_d.rearrange("n h k -> h n k"), writes=[gwb])
        dma(SP, misc_ds, ident[:], ident_d, writes=[const_b])
        st_t, st_b, st_ds = stage[0]
        dma(SP, st_ds, st_t[:NR, :], rows_d, writes=[st_b])
        cnt["stg"] = 1
        op(DVE, lambda: nc.vector.memset(ones[:], 1.0), writes=[const_b])
        v_copy(identb[:], ident[:], [const_b], [const_b])
        op(DVE, lambda: nc.vector.memset(uh[:], 0.0), writes=[uh_b])
        op(DVE, lambda: nc.vector.memset(rh[:], 0.0), writes=[rh_b])
        op(DVE, lambda: nc.vector.memset(hprev[:], 0.0), writes=[hprev_b])
        for half in range(2):
            bt, bb = bank()
            for q in range(4):
                kc = half * 4 + q
                pe_tr(bt[:, q * 128:q * 128 + NR], st_t[:NR, kc * 128:(kc + 1) * 128], ident[:NR, :NR], [st_b, const_b], [bb])
            a_act(P[:, half * 4:half * 4 + 4, :], bt[:, :].rearrange("p (q n) -> p q n", q=4)[:, :, :NR], AF.Copy, [bb], [P_b])
        t1, t1b = tmpf()
        a_act(t1[:, 0:KC], P[:, :, R_LAM], AF.Exp, [P_b], [t1b], scale=-1.0)
        a_act(t1[:, 8:8 + KC], t1[:, 0:KC], AF.Ln, [t1b], [t1b], bias=1.0)
        v_ts(c1[:, :], t1[:, 8:8 + KC], -8.0, None, ALU.mult, ALU.bypass, [t1b], [c1_b])
        v_ts(c1h[:, :], t1[:, 8:8 + KC], -4.0, None, ALU.mult, ALU.bypass, [t1b], [c1_b])
        op(DVE, lambda: nc.vector.memset(eps_t[:], EPS), writes=[c1_b])
        op(DVE, lambda: nc.vector.memset(q_t[:], 0.25), writes=[c1_b])
        v_ts(bh[:, 0:KC], P[:, :, R_BA], 0.5, None, ALU.mult, ALU.bypass, [P_b], [c1_b])
        v_ts(bh[:, KC:2 * KC], P[:, :, R_BX], 0.5, None, ALU.mult, ALU.bypass, [P_b], [c1_b])

        def stats_mm(sbank, nb, src_ap, src_b, first, last):
            st, sbuf_ = sbank
            pe_mm(st[:, :nb], [(ones[:, :], src_ap)], [const_b, src_b], [sbuf_], start=first, stop=last)

        def rstd_from(sbank, b0, nb, f, wb=None):
            st, sbuf_ = sbank
            t, tb_ = tmpf()
            a_act(t[:, :nb], st[:, :nb], AF.Ln, [sbuf_, c1_b], [tb_], scale=1.0 / D, bias=eps_t[:, 0:1])
            a_act(rstd[:, b0:b0 + nb], t[:, :nb], AF.Exp, [tb_], [rstd_b] if wb is None else wb, scale=-0.5, bias=float(np.log(f)))

        def rstd_multi(items, f):
            tmps = []
            for (sbank, b0, nb) in items:
                st, sbuf_ = sbank
                t, tb_ = tmpf()
                tmps.append((t, tb_))
                a_act(t[:, :nb], st[:, :nb], AF.Ln, [sbuf_, c1_b], [tb_], scale=1.0 / D, bias=eps_t[:, 0:1])
            for (sbank, b0, nb), (t, tb_) in zip(items, tmps):
                a_act(rstd[:, b0:b0 + nb], t[:, :nb], AF.Exp, [tb_], [rstd_b], scale=-0.5, bias=float(np.log(f)))

        def prenorm(TG, blocks, grow, out_t, out_b):
            sbk = take_banks(len(blocks))
            for c in range(KC):
                q, qb = tmpb()
                a_act(q[:, :TG], x[:, c, :TG], AF.Square, [x_bs[c]], [qb])
                for bi, (b0, nb) in enumerate(blocks):
                    stats_mm(sbk[bi], nb, q[:, b0:b0 + nb], qb, c == 0, c == KC - 1)
            rstd_multi([(sbk[bi], b0, nb) for bi, (b0, nb) in enumerate(blocks)], 1.0)
            give_banks(sbk)
            for c in range(KC):
                v_stt(out_t[:, c, :TG], x[:, c, :TG], prm(c, grow), rstd[:, :TG], ALU.mult, ALU.mult, [x_bs[c], P_b, rstd_b], [out_b[c] if isinstance(out_b, list) else out_b])

        def postnorm(TG, blocks, sbk, grow, f):
            rstd_multi([(sbk[bi], b0, nb) for bi, (b0, nb) in enumerate(blocks)], f)
            def p1(c):
                v_tt(ffo[:, c, :TG], ffo[:, c, :TG], rstd[:, :TG], ALU.mult, [ffo_bs[c], rstd_b], [ffo_bs[c]])

            p1(0)
            for c in range(KC):
                if c + 1 < KC:
                    p1(c + 1)
                v_stt(x[:, c, :TG], ffo[:, c, :TG], prm(c, grow), x[:, c, :TG], ALU.mult, ALU.add, [ffo_bs[c], P_b, x_bs[c]], [x_bs[c]])

        def first_unit(wt, wb, blocks):
            res = [(bank(), bank()) for _ in blocks]
            for kc in range(KC):
                for (b0, nb), ((a_t, a_b), (b_t, b_b)) in zip(blocks, res):
                    pe_mm(a_t[:, :nb], [(wt[:, kc * 128:(kc + 1) * 128], xg[:, kc, b0:b0 + nb])], [wb, xg_bs[kc]], [a_b],
                          start=(kc == 0), stop=(kc == KC - 1))
                    pe_mm(b_t[:, :nb], [(wt[:, 1024 + kc * 128:1024 + (kc + 1) * 128], xg[:, kc, b0:b0 + nb])], [wb, xg_bs[kc]], [b_b],
                          start=(kc == 0), stop=(kc == KC - 1))
            return res

        def ffn(TG, blocks, grow_pre, grow_post, do_prenorm=True, bg=None):
            if do_prenorm:
                prenorm(TG, blocks, grow_pre, xg, xg_bs)
            for m in range(FC):
                wt, wb, _ = next_unit()
                pre = first_unit(wt, wb, blocks) if m == 0 else None
                for bi, (b0, nb) in enumerate(blocks):
                    if pre is not None:
                        (gt, gb), (upt, upb) = pre[bi]
                    else:
                        gt, gb = bank()
                        upt, upb = bank()
                        pe_mm(gt[:, :nb], [(wt[:, kc * 128:(kc + 1) * 128], xg[:, kc, b0:b0 + nb]) for kc in range(KC)], [wb] + xg_bs, [gb])
                        pe_mm(upt[:, :nb], [(wt[:, 1024 + kc * 128:1024 + (kc + 1) * 128], xg[:, kc, b0:b0 + nb]) for kc in range(KC)], [wb] + xg_bs, [upb])
                    t, tb_ = tmpf()
                    a_act(t[:, :nb], gt[:, :nb], AF.Silu, [gb], [tb_])
                    v_tt(act[:, m, b0:b0 + nb], upt[:, :nb], t[:, :nb], ALU.mult, [upb, tb_], [act_b, gh_b, cvn_b])
                if bg and m >= 1 and m % 2 == 1:
                    bg.pop(0)()
            while bg:
                bg.pop(0)()
            sbk = take_banks(len(blocks))
            pending = []
            for o in range(KC):
                w0 = next_unit()
                w1 = next_unit()
                for bi, (b0, nb) in enumerate(blocks):
                    bt, bb = bank()
                    pairs = []
                    for kc in range(FC):
                        wt = (w0 if kc < 11 else w1)[0]
                        kk = kc % 11
                        pairs.append((wt[:, kk * 128:(kk + 1) * 128], act[:, kc, b0:b0 + nb]))
                    pe_mm(bt[:, :nb], pairs, [w0[1], w1[1], act_b, gh_b, cvn_b], [bb])
                    for fn_ in pending:
                        fn_()
                    pending = []
                    a_act(ffo[:, o, b0:b0 + nb], bt[:, :nb], AF.Copy, [bb], [ffo_bs[o]])
                    q, qb = tmpb()
                    a_act(q[:, :nb], bt[:, :nb], AF.Square, [bb], [qb])
                    pending.append(lambda bi=bi, nb=nb, q=q, qb=qb, o=o: stats_mm(sbk[bi], nb, q[:, :nb], qb, o == 0, o == KC - 1))
            for fn_ in pending:
                fn_()
            postnorm(TG, blocks, sbk, grow_post, 0.5)
            give_banks(sbk)

        def mixer(TG, TGp, has_s, blocks, last):
            prenorm(TG, blocks, R_GMPRE, xg, xg_bs)

            def split(b0, nb):
                npr = min(nb, TGp - b0)
                return npr, nb - npr

            def build_dg(c):
                dg, dgb = Dg[c % 2]
                v_tt(dg[:, :, :], identb[:, :].unsqueeze(1).broadcast_to([128, CK, 128]),
                     P[:, c, R_WDW:R_WDW + CK].unsqueeze(2).broadcast_to([128, CK, 128]), ALU.mult, [const_b, P_b], [dgb])

            for c in range(KC):
                if c == 3:
                    build_dg(0)
                if c == 5:
                    build_dg(1)
                wt, wb, _ = next_unit()
                u, ub = ut[c]
                v_copy(u[:, 0:30], uh[:, c, :], [uh_b], [ub])
                if has_s:
                    v_copy(u[:, U_S:U_S + 30], P[:, c, R_SCONV:R_SCONV + 30], [P_b], [ub])
                pre = first_unit(wt, wb, blocks) if c == 0 else None
                for bi, (b0, nb) in enumerate(blocks):
                    npr, nsm = split(b0, nb)
                    if pre is not None:
                        (vt, vb), (gt, gb) = pre[bi]
                    else:
                        vt, vb = bank()
                        gt, gb = bank()
                        pe_mm(vt[:, :nb], [(wt[:, kc * 128:(kc + 1) * 128], xg[:, kc, b0:b0 + nb]) for kc in range(KC)], [wb] + xg_bs, [vb])
                        pe_mm(gt[:, :nb], [(wt[:, 1024 + kc * 128:1024 + (kc + 1) * 128], xg[:, kc, b0:b0 + nb]) for kc in range(KC)], [wb] + xg_bs, [gb])
                    t, tb_ = tmpf()
                    a_act(t[:, :nb], gt[:, :nb], AF.Sigmoid, [gb], [tb_])
                    v_tt(u[:, 30 + b0:30 + b0 + npr], vt[:, :npr], t[:, :npr], ALU.mult, [vb, tb_], [ub])
                    if nsm:
                        v_tt(u[:, U_S + 30:U_S + 30 + SS], vt[:, npr:nb], t[:, npr:nb], ALU.mult, [vb, tb_], [ub])
                    if last and bi == len(blocks) - 1:
                        v_tt(SO[:, c, O_CP:O_CP + 30], vt[:, npr - 30:npr], t[:, npr - 30:npr], ALU.mult, [vb, tb_], [SO_b])
                        v_tt(SO[:, c, O_CS:O_CS + 30], vt[:, npr + 2:npr + 32], t[:, npr + 2:npr + 32], ALU.mult, [vb, tb_], [SO_b])
                v_copy(uh[:, c, :], u[:, TGp:TGp + 30], [ub], [uh_b])

            spieces = [(q0, min(256, TG - q0)) for q0 in range(0, TG, 256)]
            sPQ = take_banks(len(spieces))
            def ln_stats(c):
                cbq, cbq_b = cbqs[c % 2]
                for pi, (q0, w) in enumerate(spieces):
                    st, stb = sPQ[pi]
                    pe_mm(st[:, 0:2 * w].rearrange("p (a b) -> p a b", a=2), [(ones[:, :], cbq[:, :, q0:q0 + w])], [const_b, cbq_b], [stb],
                          start=(c == 0), stop=(c == KC - 1))

            st8 = [dict() for _ in range(KC)]

            def H1(c):
                d = st8[c]
                wt, wb, _ = next_unit()
                ri, rib = rr("rxi", rxi)
                d["ri"] = (ri, rib)
                v_copy(ri[:, 0:3], rh[:, c, :], [rh_b], [rib])
                if has_s:
                    v_copy(ri[:, RX_S:RX_S + 3], P[:, c, R_SRC:R_SRC + 3], [P_b], [rib])
                rg, rgb = tmpb()
                d["rg"] = (rg, rgb)
                for bi, (b0, nb) in enumerate(blocks):
                    npr, nsm = split(b0, nb)
                    xt_, xb_ = bank()
                    gt, gb = bank()
                    pe_mm(xt_[:, :nb], [(wt[:, kc * 128:(kc + 1) * 128], xg[:, kc, b0:b0 + nb]) for kc in range(KC)], [wb] + xg_bs, [xb_])
                    pe_mm(gt[:, :nb], [(wt[:, 1024 + kc * 128:1024 + (kc + 1) * 128], xg[:, kc, b0:b0 + nb]) for kc in range(KC)], [wb] + xg_bs, [gb])
                    v_copy(ri[:, 3 + b0:3 + b0 + npr], xt_[:, :npr], [xb_], [rib])
                    if nsm:
                        v_copy(ri[:, RX_S + 3:RX_S + 3 + SS], xt_[:, npr:nb], [xb_], [rib])
                    a_act(rg[:, b0:b0 + nb], gt[:, :nb], AF.Gelu_apprx_tanh, [gb], [rgb])

            def H2(c):
                d = st8[c]
                ri, rib = d["ri"]
                v_copy(rh[:, c, :], ri[:, TGp:TGp + 3], [rib], [rh_b])
                if last:
                    v_copy(SO[:, c, O_RP:O_RP + 3], ri[:, TGp:TGp + 3], [rib], [SO_b])
                    v_copy(SO[:, c, O_RS:O_RS + 3], ri[:, RX_S + SS:RX_S + SS + 3], [rib], [SO_b])
                rx, rxb = tmpf()
                d["rx"] = (rx, rxb)
                segs = [(0, 0, TGp)] + ([(RX_S, TGp, SS)] if has_s else [])
                for (src0, dst0, n) in segs:
                    v_ts(rx[:, dst0:dst0 + n], ri[:, src0:src0 + n], prm(c, R_WRC), prm(c, R_BRC), ALU.mult, ALU.add, [rib, P_b], [rxb])
                    for k in range(1, RK):
                        v_stt(rx[:, dst0:dst0 + n], ri[:, src0 + k:src0 + k + n], prm(c, R_WRC + k), rx[:, dst0:dst0 + n], ALU.mult, ALU.add, [rib, P_b, rxb], [rxb])
                rxq, rxqb = tmpb()
                d["rxq"] = (rxq, rxqb)
                a_act(rxq[:, :TG], rx[:, :TG], AF.Copy, [rxb], [rxqb])

            def H3(c):
                u, ub = ut[c]
                dg, dgb = Dg[c % 2]
                for bi, (b0, nb) in enumerate(blocks):
                    npr, nsm = split(b0, nb)
                    bt, bb = bank()
                    pe_mm(bt[:, :npr], [(dg[:, k, :], u[:, b0 + k:b0 + k + npr]) for k in range(CK)], [dgb, ub], [bb])
                    if nsm:
                        pe_mm(bt[:, npr:nb], [(dg[:, k, :], u[:, U_S + k:U_S + k + SS]) for k in range(CK)], [dgb, ub], [bb])
                    a_act(ffo[:, c, b0:b0 + nb], bt[:, :nb], AF.Identity, [bb, P_b], [ffo_bs[c]], bias=prm(c, R_BDW))
                if c + 2 < KC:
                    build_dg(c + 2)

            def H4(c):
                d = st8[c]
                rxq, rxqb = d["rxq"]
                rt, rtb = tmpf()
                it, itb = tmpf()
                d["rt"], d["it"] = (rt, rtb), (it, itb)
                for bi, (b0, nb) in enumerate(blocks):
                    rp, rpb = bank()
                    ip, ipb = bank()
                    pe_mm(rp[:, :nb], [(gw[:, c * 128:(c + 1) * 128], rxq[:, b0:b0 + nb])], [gwb, rxqb], [rpb])
                    pe_mm(ip[:, :nb], [(gw[:, 1024 + c * 128:1024 + (c + 1) * 128], rxq[:, b0:b0 + nb])], [gwb, rxqb], [ipb])
                    a_act(rt[:, b0:b0 + nb], rp[:, :nb], AF.Tanh, [rpb, c1_b], [rtb], scale=0.5, bias=bh[:, c:c + 1])
                    a_act(it[:, b0:b0 + nb], ip[:, :nb], AF.Tanh, [ipb, c1_b], [itb], scale=0.5, bias=bh[:, KC + c:KC + c + 1])
                if c >= 1:
                    ln_stats(c - 1)
                cbq, cbq_b = cbqs[c % 2]
                a_act(cbq[:, 0, :TG], ffo[:, c, :TG], AF.Copy, [ffo_bs[c]], [cbq_b])
                a_act(cbq[:, 1, :TG], ffo[:, c, :TG], AF.Square, [ffo_bs[c]], [cbq_b])

            def T1(c):
                d = st8[c]
                rt, rtb = d["rt"]
                at, atb = tmpf()
                d["at"] = (at, atb)
                a_act(at[:, :TG], rt[:, :TG], AF.Exp, [rtb, c1_b], [atb], scale=c1h[:, c:c + 1], bias=c1h[:, c:c + 1])
                a_act(rt[:, :TG], at[:, :TG], AF.Square, [atb], [rtb])
                a_act(rt[:, :TG], rt[:, :TG], AF.Sqrt, [rtb, c1_b], [rtb], scale=-0.25, bias=q_t[:, 0:1])

            def T2(c):
                d = st8[c]
                rt, rtb = d["rt"]
                it, itb = d["it"]
                at, atb = d["at"]
                rx, rxb = d["rx"]
                rg, rgb = d["rg"]
                ri, rib = d["ri"]
                v_stt(it[:, :TG], it[:, :TG], 1.0, rx[:, :TG], ALU.add, ALU.mult, [itb, rxb], [itb])
                v_tt(it[:, :TG], it[:, :TG], rt[:, :TG], ALU.mult, [itb, rtb], [itb])
                hs, hsb = tmpf()
                op(DVE, lambda: nc.vector.tensor_tensor_scan(out=hs[:, :TGp], data0=at[:, :TGp], data1=it[:, :TGp],
                                                             initial=hprev[:, c:c + 1], op0=ALU.mult, op1=ALU.add),
                   [atb, itb, hprev_b], [hsb])
                if has_s:
                    op(DVE, lambda: nc.vector.tensor_tensor_scan(out=hs[:, TGp:TG], data0=at[:, TGp:TG], data1=it[:, TGp:TG],
                                                                 initial=prm(c, R_SH), op0=ALU.mult, op1=ALU.add),
                       [atb, itb, P_b], [hsb])
                if dbg and c == int(os.environ.get('DBGC', '0')) and not has_s and TGp == 704 and cnt.get("dumped") is None:
                    cnt["dumped"] = 1
                    dump(0, rx[:, :], rxb); dump(1, rt[:, :], rtb); dump(2, it[:, :], itb); dump(3, at[:, :], atb); dump(4, hs[:, :], hsb)
                    dump(5, c1[:, :], c1_b, KC); dump(7, ffo[:, c, :], ffo_bs[c]); dump(8, x[:, c, :], x_bs[c])
                v_copy(hprev[:, c:c + 1], hs[:, TGp - 1:TGp], [hsb], [hprev_b])
                if last:
                    v_copy(SO[:, c, O_HP:O_HP + 1], hs[:, TGp - 1:TGp], [hsb], [SO_b])
                    v_copy(SO[:, c, O_HS:O_HS + 1], hs[:, TG - 1:TG], [hsb], [SO_b])
                v_tt(gh[:, c, :TG], hs[:, :TG], rg[:, :TG], ALU.mult, [hsb, rgb], [gh_b])
                st8[c].clear()

            H1(0); H2(0); H3(0); H4(0)
            for c in range(KC - 1):
                H1(c + 1)
                T1(c)
                H2(c + 1)
                T2(c)
                H3(c + 1)
                H4(c + 1)
            ln_stats(KC - 1)

            pcs = [(sPQ[pi][0], sPQ[pi][1], q0, w, rstd[:, q0:q0 + w]) for pi, (q0, w) in enumerate(spieces)]
            for st, stb, q0, w, rs in pcs:
                a_act(mean[:, q0:q0 + w], st[:, 0:w], AF.Copy, [stb], [mean_b], scale=1.0 / D)
            for st, stb, q0, w, rs in pcs:
                v_tt(rs, mean[:, q0:q0 + w], mean[:, q0:q0 + w], ALU.mult, [mean_b], [rstd_b])
            for st, stb, q0, w, rs in pcs:
                v_stt(rs, st[:, w:2 * w], 1.0 / D, rs, ALU.mult, ALU.subtract, [stb, rstd_b], [rstd_b])
            for st, stb, q0, w, rs in pcs:
                a_act(rs, rs, AF.Ln, [rstd_b, c1_b], [rstd_b], bias=eps_t[:, 0:1])
            for st, stb, q0, w, rs in pcs:
                a_act(rs, rs, AF.Exp, [rstd_b], [rstd_b], scale=-0.5)
            give_banks(sPQ)
            if dbg and cnt.get("dumped3") is None:
                cnt["dumped3"] = 1
                dump(12, mean[:, :], mean_b); dump(13, rstd[:, :], rstd_b)

            def ln_sub(c):
                v_tt(ffo[:, c, :TG], ffo[:, c, :TG], mean[:, :TG], ALU.subtract, [ffo_bs[c], mean_b], [ffo_bs[c]])

            def ln_rest(c):
                v_tt(ffo[:, c, :TG], ffo[:, c, :TG], rstd[:, :TG], ALU.mult, [ffo_bs[c], rstd_b], [ffo_bs[c]])
                a_act(cvn[:, c, :TG], ffo[:, c, :TG], AF.Silu, [ffo_bs[c], P_b], [cvn_b], scale=prm(c, R_LNG), bias=prm(c, R_LNB))

            def ln_range(c0_, c1_):
                ln_sub(c0_)
                for c in range(c0_, c1_):
                    if c + 1 < c1_:
                        ln_sub(c + 1)
                    ln_rest(c)

            T1(KC - 1)
            ln_range(0, 4)
            T2(KC - 1)
            ln_range(4, KC)

            for o in range(KC):
                w1 = next_unit()
                w2 = next_unit()
                srcs = [(w1, 0, xg, xg_bs), (w1, 1024, xg, xg_bs), (w2, 0, cvn, [cvn_b]), (w2, 1024, gh, [gh_b])]
                pss = [[bank() for _ in range(4)] for _ in blocks]
                order = ([(bi, j) for bi in range(len(blocks)) for j in (0, 1, 3)] + [(bi, 2) for bi in range(len(blocks))]) if o == 0 \
                    else [(bi, j) for bi in range(len(blocks)) for j in range(4)]
                for bi, j in order:
                    b0, nb = blocks[bi]
                    pt, pb = pss[bi][j]
                    wu, off, rhs_t, rhs_b = srcs[j]
                    pe_mm(pt[:, :nb], [(wu[0][:, off + kc * 128:off + (kc + 1) * 128], rhs_t[:, kc, b0:b0 + nb]) for kc in range(KC)],
                          [wu[1]] + rhs_b, [pb])
                for bi, (b0, nb) in enumerate(blocks):
                    ps = pss[bi]
                    ta, tab = tmpf()
                    tr_, trb = tmpf()
                    a_act(ta[:, :nb], ps[0][0][:, :nb], AF.Sigmoid, [ps[0][1]], [tab])
                    a_act(tr_[:, :nb], ps[1][0][:, :nb], AF.Sigmoid, [ps[1][1]], [trb])
                    v_tt(ta[:, :nb], ps[2][0][:, :nb], ta[:, :nb], ALU.mult, [ps[2][1], tab], [tab])
                    v_tt(tr_[:, :nb], ps[3][0][:, :nb], tr_[:, :nb], ALU.mult, [ps[3][1], trb], [trb])
                    v_tt(mg[:, o, b0:b0 + nb], ta[:, :nb], tr_[:, :nb], ALU.add, [tab, trb], [mg_b])

            sbk = take_banks(len(blocks))
            pending = []
            for oo in range(4):
                wu = next_unit()
                for o in (2 * oo, 2 * oo + 1):
                    off = (o % 2) * 1024
                    for bi, (b0, nb) in enumerate(blocks):
                        bt, bb = bank()
                        pe_mm(bt[:, :nb], [(wu[0][:, off + kc * 128:off + (kc + 1) * 128], mg[:, kc, b0:b0 + nb]) for kc in range(KC)],
                              [wu[1], mg_b], [bb])
                        for fn_ in pending:
                            fn_()
                        pending = []
                        a_act(ffo[:, o, b0:b0 + nb], bt[:, :nb], AF.Copy, [bb], [ffo_bs[o]])
                        q, qb = tmpb()
                        a_act(q[:, :nb], bt[:, :nb], AF.Square, [bb], [qb])
                        pending.append(lambda bi=bi, nb=nb, q=q, qb=qb, o=o: stats_mm(sbk[bi], nb, q[:, :nb], qb, o == 0, o == KC - 1))
            for fn_ in pending:
                fn_()
            postnorm(TG, blocks, sbk, R_GMPOST, 1.0)
            give_banks(sbk)

        def group_cfg(gi):
            p0, TGp, has_s = GROUPS[gi]
            TG = TGp + (SS if has_s else 0)
            h = TGp // 2
            return p0, TGp, has_s, TG, [(0, h), (h, TG - h)]

        def load_tile(src, r0, c0, nt):
            st_t, st_b, st_ds = rr("stg", stage)
            xt_b, rr_b = Buf("xt"), Buf("rr")
            dma(SP, st_ds, st_t[:nt, :], src[r0:r0 + nt, :], writes=[st_b])
            for half in range(2):
                bt, bb = bank()
                for q in range(4):
                    kc = half * 4 + q
                    pe_tr(bt[:, q * 128:q * 128 + nt], st_t[:nt, kc * 128:(kc + 1) * 128], ident[:nt, :nt], [st_b, const_b], [bb])
                a_act(x[:, half * 4:half * 4 + 4, c0:c0 + nt], bt[:, :].rearrange("p (q n) -> p q n", q=4)[:, :, :nt], AF.Copy, [bb],
                      x_bs[half * 4:half * 4 + 4] + [xt_b])
            cq_t, cq_b = cbqs[cnt["ldt"] % 2]
            cnt["ldt"] += 1
            q3 = cq_t[:, :, :].rearrange("p a b -> p (a b)")[:, 0:KC * 128].rearrange("p (k n) -> p k n", k=KC)[:, :, :nt]
            a_act(q3, x[:, :, c0:c0 + nt], AF.Square, [xt_b], [cq_b])
            for fn_ in load_pend:
                fn_()
            load_pend.clear()

            def rest(c0=c0, nt=nt, q3=q3, cq_b=cq_b, xt_b=xt_b, rr_b=rr_b):
                sbank = bank()
                for kc in range(KC):
                    stats_mm(sbank, nt, q3[:, kc, :], cq_b, kc == 0, kc == KC - 1)
                rstd_from(sbank, c0, nt, 1.0, wb=[rr_b, rstd_b])
                for c in range(KC):
                    v_stt(xg[:, c, c0:c0 + nt], x[:, c, c0:c0 + nt], prm(c, R_G1PRE), rstd[:, c0:c0 + nt], ALU.mult, ALU.mult,
                          [xt_b, P_b, rr_b], [xg_bs[c]])
            load_pend.append(rest)

        def flush_load():
            for fn_ in load_pend:
                fn_()
            load_pend.clear()
            for c in range(KC):
                x_bs[c].r[DVE] = DVE.count
                x_bs[c].r[ACT] = ACT.count
            rstd_b.r[DVE] = DVE.count

        def store_tile(dst, r0, c0, nt):
            st_t, st_b, st_ds = rr("stg", stage)
            for half in range(2):
                bt, bb = bank()
                for q in range(4):
                    kc = half * 4 + q
                    pe_tr(bt[:nt, q * 128:(q + 1) * 128], ffo[:, kc, c0:c0 + nt], ident[:, :], [ffo_bs[kc], const_b], [bb])
                a_act(st_t[:nt, half * 512:(half + 1) * 512], bt[:nt, :], AF.Copy, [bb], [st_b])
            dma(ACT, st_ds, dst[r0:r0 + nt, :], st_t[:nt, :], reads=[st_b])

        def tiles_of(gi, dp, ds_):
            p0, TGp, has_s, TG, blocks = group_cfg(gi)
            tl = [(dp, p0 + t0, t0, min(128, TGp - t0)) for t0 in range(0, TGp, 128)]
            if has_s:
                tl.append((ds_, 0, TGp, SS))
            return tl

        deferred = []
        for tl in tiles_of(0, xp_d, xs_d):
            load_tile(*tl)
        flush_load()
        for gi in range(len(GROUPS)):
            p0, TGp, has_s, TG, blocks = group_cfg(gi)
            last = gi == len(GROUPS) - 1
            ffn(TG, blocks, R_G1PRE, R_G1POST, do_prenorm=False, bg=deferred)
            mixer(TG, TGp, has_s, blocks, last)
            if last:
                st_t, st_b, st_ds = rr("stg", stage)
                for half in range(2):
                    bt, bb = bank()
                    for q in range(4):
                        kc = half * 4 + q
                        pe_tr(bt[:NSO, q * 128:(q + 1) * 128], SO[:, kc, :], ident[:, :], [SO_b, const_b], [bb])
                    a_act(st_t[:NSO, half * 512:(half + 1) * 512], bt[:NSO, :], AF.Copy, [bb], [st_b])
                dma(SP, st_ds, so_d, st_t[:NSO, :], reads=[st_b])
            ffn(TG, blocks, R_G2PRE, R_G2POST)
            prenorm(TG, blocks, R_GFIN, ffo, ffo_bs)
            outs = tiles_of(gi, yp_d, ys_d)
            if last:
                for tl in outs:
                    store_tile(*tl)
            else:
                deferred.extend([(lambda tl=tl: store_tile(*tl)) for tl in outs])
                for tl in tiles_of(gi + 1, xp_d, xs_d):
                    load_tile(*tl)
                flush_load()

        for (_, _, ds) in stage:
            nc.sync.wait_ge(ds.sem, ds.count)
        for ds in dbg_dss:
            if ds.count:
                nc.sync.wait_ge(ds.sem, ds.count)
    return nc


_CACHE = {}


def kernel(x_prompt, x_sample, state_conv, state_rconv, state_h,
           g_ffn1_pre, g_ffn1_post, w_ffn1_in, w_ffn1_out,
           g_mix_pre, g_mix_post, w_in,
           w_dw, b_dw, ln_g, ln_b, w_conv_out,
           w_rconv, b_rconv, w_rg_a, b_rg_a, w_rg_x, b_rg_x, lam, w_rnn_out,
           w_out,
           g_ffn2_pre, g_ffn2_post, w_ffn2_in, w_ffn2_out, g_final):
    f = lambda a: np.ascontiguousarray(np.asarray(a, dtype=np.float32))
    if "nc" not in _CACHE:
        _CACHE["nc"] = build_program()
    nc = _CACHE["nc"]
    vec_rows = [g_ffn1_pre, g_ffn1_post, g_mix_pre, g_mix_post, b_dw, ln_g, ln_b, b_rconv, b_rg_a, b_rg_x, lam,
                g_ffn2_pre, g_ffn2_post, g_final]
    common = np.concatenate([f(v)[0][None, :] for v in vec_rows] + [f(w_dw)[0], f(w_rconv)[0]], axis=0)
    shared = {
        "ident": np.eye(128, dtype=np.float32),
        "w1i": f(w_ffn1_in)[0], "w1o": f(w_ffn1_out)[0], "wi": f(w_in)[0],
        "wco": f(w_conv_out)[0], "wro": f(w_rnn_out)[0], "wo": f(w_out)[0],
        "wga": f(w_rg_a)[0], "wgx": f(w_rg_x)[0],
        "w2i": f(w_ffn2_in)[0], "w2o": f(w_ffn2_out)[0],
    }
    xp, xs = f(x_prompt), f(x_sample)
    sc, sr, sh = f(state_conv)[0], f(state_rconv)[0], f(state_h)[0]
    in_maps = []
    for i in range(NCORES):
        rows = np.ascontiguousarray(np.concatenate([common, sc[i], sr[i], sh[i][None, :]], axis=0))
        m = dict(shared)
        m.update({"xp": xp[i], "xs": xs[i], "rows": rows})
        in_maps.append(m)
    res = run_bass_kernel_spmd(nc, in_maps, core_ids=list(range(NCORES)))
    r = res.results
    yp = np.stack([r[i]["yp"] for i in range(NCORES)])
    ys = np.stack([r[i]["ys"] for i in range(NCORES)])
    so = np.stack([r[i]["so"] for i in range(NCORES)])
    return (yp.astype(np.float32), ys.astype(np.float32),
            so[None, :, O_CP:O_CP + 30, :], so[None, :, O_RP:O_RP + 3, :], so[None, :, O_HP, :],
            so[None, :, O_CS:O_CS + 30, :], so[None, :, O_RS:O_RS + 3, :], so[None, :, O_HS, :])
```
